# Optimizing a Trainium2 kernel written in Bass

```python
import math
import jax, jax.numpy as jnp
from jax import lax
import numpy as np

D_MODEL = 1024
BATCH = 2
SEQ = 8192
DEPTH = 1
DEC_BATCH = 128
DEC_SEQ = 1
PAST_LEN = 16384
PAGE_SIZE = 128

WINDOW = 128
ATT_BLOCK = 128
HEAD_DIM = 64
N_Q_HEADS = 8
N_KV_HEADS = 2
GQA_GROUP = N_Q_HEADS // N_KV_HEADS
ATT_WIDTH = N_Q_HEADS * HEAD_DIM
KV_WIDTH = N_KV_HEADS * HEAD_DIM
N_BUCKETS = 32
MAX_DISTANCE = WINDOW
M_HEADS = 4
M_DV = (D_MODEL // 2) // M_HEADS
M_DK = M_DV // 2
M_WIDTH = M_HEADS * M_DV
M_QK_WIDTH = M_HEADS * M_DK
MLSTM_CHUNK = 64
D_FF = -(-8 * D_MODEL // (3 * 256)) * 256
N_IN = ATT_WIDTH + 2 * KV_WIDTH + 2 * M_QK_WIDTH + 2 * M_WIDTH + 2 * M_HEADS + 2 * D_MODEL
EPS = 1e-6
NEG = -1e30

kernel_name = 'hybrid_swa_sink_mlstm_decoder_step'


def in_split_points():
    sizes = [ATT_WIDTH, KV_WIDTH, KV_WIDTH, M_QK_WIDTH, M_QK_WIDTH, M_WIDTH, M_WIDTH,
             M_HEADS, M_HEADS, D_MODEL, D_MODEL]
    return [int(s) for s in np.cumsum(sizes)[:-1]]


def rmsnorm(x, g):
    xf = x.astype(jnp.float32)
    y = xf * lax.rsqrt(jnp.mean(xf * xf, axis=-1, keepdims=True) + EPS)
    return (y * g.astype(jnp.float32)).astype(x.dtype)


def t5_bucket(dist):
    n = jnp.maximum(dist, 0)
    max_exact = N_BUCKETS // 2
    nf = jnp.maximum(n, 1).astype(jnp.float32)
    large = max_exact + (jnp.log(nf / max_exact) / math.log(MAX_DISTANCE / max_exact)
                         * (N_BUCKETS - max_exact)).astype(jnp.int32)
    large = jnp.minimum(large, N_BUCKETS - 1)
    return jnp.where(n < max_exact, n, large)


def sink_window_attention(q, k, v, dist, valid, rel_bias, sinks):
    qf = q.astype(jnp.float32) * (HEAD_DIM ** -0.5)
    s = jnp.einsum('...qhgd,...khd->...hgqk', qf, k.astype(jnp.float32))
    nq, nk = dist.shape
    bias = rel_bias.astype(jnp.float32)[t5_bucket(dist)]
    bias = jnp.transpose(bias, (2, 0, 1)).reshape(N_KV_HEADS, GQA_GROUP, nq, nk)
    s = jnp.where(valid, s + bias, NEG)
    sink = sinks.astype(jnp.float32).reshape(N_KV_HEADS, GQA_GROUP, 1, 1)
    m = jnp.maximum(jnp.max(s, axis=-1, keepdims=True), sink)
    p = jnp.exp(s - m)
    denom = jnp.sum(p, axis=-1, keepdims=True) + jnp.exp(sink - m)
    return jnp.einsum('...hgqk,...khd->...qhgd', p / denom, v.astype(jnp.float32))


def attn_prompt(q, k, v, rel_bias, sinks):
    bsz, s_len = q.shape[:2]
    nb = s_len // ATT_BLOCK
    qb = q.reshape(bsz, nb, ATT_BLOCK, N_KV_HEADS, GQA_GROUP, HEAD_DIM)

    def band(t):
        tb = t.reshape(bsz, nb, ATT_BLOCK, N_KV_HEADS, HEAD_DIM)
        prev = jnp.pad(tb, ((0, 0), (1, 0), (0, 0), (0, 0), (0, 0)))[:, :-1]
        return jnp.concatenate([prev, tb], axis=2)

    kb, vb = band(k), band(v)
    qi = jnp.arange(ATT_BLOCK)[:, None]
    kj = jnp.arange(2 * ATT_BLOCK)[None, :]
    dist = ATT_BLOCK + qi - kj
    blk = jnp.arange(nb)[:, None, None]
    valid = (dist >= 0) & (dist < WINDOW) & (blk * ATT_BLOCK - ATT_BLOCK + kj >= 0)
    o = sink_window_attention(qb, kb, vb, dist, valid[:, None, None], rel_bias, sinks)
    return o.reshape(bsz, s_len, ATT_WIDTH)


def attn_sample(q, k, v, buf_k, buf_v, rel_bias, sinks):
    bsz, s_len = q.shape[:2]
    wb = buf_k.shape[1]
    keys = jnp.concatenate([buf_k.astype(k.dtype), k], axis=1)
    vals = jnp.concatenate([buf_v.astype(v.dtype), v], axis=1)
    qi = jnp.arange(s_len)[:, None]
    kj = jnp.arange(wb + s_len)[None, :]
    dist = wb + qi - kj
    valid = (dist >= 0) & (dist < WINDOW)
    o = sink_window_attention(q, keys, vals, dist, valid, rel_bias, sinks)
    return o.reshape(bsz, s_len, ATT_WIDTH), keys[:, -wb:], vals[:, -wb:]


def mlstm_chunkwise(q, k, v, ig, lf, c0, n0, m0):
    bsz, s_len, nh, _ = q.shape
    L = math.gcd(s_len, MLSTM_CHUNK)
    nc = s_len // L

    def to_chunks(t):
        t = t.reshape((bsz, nc, L, nh) + t.shape[3:])
        return jnp.moveaxis(jnp.moveaxis(t, 1, 0), 3, 2)

    causal = jnp.tril(jnp.ones((L, L), dtype=bool))

    def step(carry, inp):
        c, n, m = carry
        qc, kc, vc, igc, lfc = inp
        b = jnp.cumsum(lfc, axis=-1)
        a = igc - b
        m_t = b + jnp.maximum(m[..., None], lax.cummax(a, axis=a.ndim - 1))
        log_d = a[..., None, :] + b[..., :, None] - m_t[..., :, None]
        d = jnp.exp(jnp.where(causal, log_d, NEG))
        inter = jnp.exp(m[..., None] + b - m_t)
        w = d * jnp.einsum('bhtd,bhsd->bhts', qc, kc)
        num = inter[..., None] * jnp.einsum('bhvd,bhtd->bhtv', c, qc) + jnp.einsum('bhts,bhsv->bhtv', w, vc)
        den = inter * jnp.einsum('bhd,bhtd->bht', n, qc) + jnp.sum(w, axis=-1)
        h = num / jnp.maximum(jnp.abs(den), jnp.exp(-m_t))[..., None]
        m_end = m_t[..., -1]
        w_s = jnp.exp(a + b[..., -1:] - m_end[..., None])
        decay = jnp.exp(m + b[..., -1] - m_end)
        c_new = decay[..., None, None] * c + jnp.einsum('bhs,bhsv,bhsd->bhvd', w_s, vc, kc)
        n_new = decay[..., None] * n + jnp.einsum('bhs,bhsd->bhd', w_s, kc)
        return (c_new, n_new, m_end), h

    xs = (to_chunks(q), to_chunks(k), to_chunks(v), to_chunks(ig), to_chunks(lf))
    (c1, n1, m1), hs = lax.scan(step, (c0, n0, m0), xs)
    hs = jnp.moveaxis(jnp.moveaxis(hs, 3, 2), 0, 1).reshape(bsz, s_len, nh, v.shape[-1])
    return hs, c1, n1, m1


def decoder_layer(x, buf_k, buf_v, c0, n0, m0, rel_bias, w_in, b_if, sinks, g_attn_norm, g_head,
                  w_att_out, w_mlstm_out, w_out, g_ffn_norm, w_gate, w_up, w_down):
    f32 = jnp.float32
    bsz, s_len, _ = x.shape
    h = rmsnorm(x, g_attn_norm)
    z = h @ w_in
    q_a, k_a, v_a, q_m, k_m, v_m, o_m, i_pre, f_pre, gate_a, gate_m = jnp.split(z, in_split_points(), axis=-1)
    q_a = q_a.reshape(bsz, s_len, N_KV_HEADS, GQA_GROUP, HEAD_DIM)
    k_a = k_a.reshape(bsz, s_len, N_KV_HEADS, HEAD_DIM)
    v_a = v_a.reshape(bsz, s_len, N_KV_HEADS, HEAD_DIM)
    if buf_k is None:
        y_att = attn_prompt(q_a, k_a, v_a, rel_bias, sinks)
        new_k, new_v = k_a[:, -WINDOW:], v_a[:, -WINDOW:]
    else:
        y_att, new_k, new_v = attn_sample(q_a, k_a, v_a, buf_k, buf_v, rel_bias, sinks)
    qm = q_m.astype(f32).reshape(bsz, s_len, M_HEADS, M_DK)
    km = k_m.astype(f32).reshape(bsz, s_len, M_HEADS, M_DK) * (M_DK ** -0.5)
    vm = v_m.astype(f32).reshape(bsz, s_len, M_HEADS, M_DV)
    ig = i_pre.astype(f32) + b_if[0].astype(f32)
    lf = jax.nn.log_sigmoid(f_pre.astype(f32) + b_if[1].astype(f32))
    if c0 is None:
        c0 = jnp.zeros((bsz, M_HEADS, M_DV, M_DK), f32)
        n0 = jnp.zeros((bsz, M_HEADS, M_DK), f32)
        m0 = jnp.zeros((bsz, M_HEADS), f32)
    hm, c1, n1, m1 = mlstm_chunkwise(qm, km, vm, ig, lf, c0.astype(f32), n0.astype(f32), m0.astype(f32))
    hm = hm * lax.rsqrt(jnp.mean(hm * hm, axis=-1, keepdims=True) + EPS)
    hm = hm.reshape(bsz, s_len, M_WIDTH) * g_head.astype(f32)
    y_mlstm = hm.astype(x.dtype) * jax.nn.sigmoid(o_m)
    y_a = y_att.astype(x.dtype) @ w_att_out
    y_m = y_mlstm @ w_mlstm_out
    mixed = jax.nn.sigmoid(gate_a) * y_a + jax.nn.sigmoid(gate_m) * y_m
    x = x + mixed @ w_out
    h2 = rmsnorm(x, g_ffn_norm)
    x = x + (jax.nn.silu(h2 @ w_gate) * (h2 @ w_up)) @ w_down
    return x, (new_k, new_v, c1, n1, m1)


def setup_inputs(seed: int = 0) -> dict:
    key = jax.random.key(seed)
    ks = jax.random.split(key, 24)
    f32 = jnp.float32

    def nrm(k, shape, scale):
        return scale * jax.random.normal(k, shape, f32)

    win_buf = min(WINDOW, PAST_LEN)
    b_if = jnp.stack([nrm(ks[9], (DEPTH, M_HEADS), 0.1),
                      jnp.linspace(3.0, 6.0, M_HEADS)[None, :] + nrm(ks[10], (DEPTH, M_HEADS), 0.1)], axis=1)
    return {
        'x_prompt': nrm(ks[0], (BATCH, SEQ, D_MODEL), 1.0),
        'x_sample': nrm(ks[1], (DEC_BATCH, DEC_SEQ, D_MODEL), 1.0),
        'cache_k_win': nrm(ks[2], (DEPTH, DEC_BATCH, win_buf, N_KV_HEADS, HEAD_DIM), 1.0),
        'cache_v_win': nrm(ks[3], (DEPTH, DEC_BATCH, win_buf, N_KV_HEADS, HEAD_DIM), 1.0),
        'state_mlstm_C': nrm(ks[4], (DEPTH, DEC_BATCH, M_HEADS, M_DV, M_DK), 0.1),
        'state_mlstm_n': nrm(ks[5], (DEPTH, DEC_BATCH, M_HEADS, M_DK), 0.1),
        'state_mlstm_m': nrm(ks[6], (DEPTH, DEC_BATCH, M_HEADS), 1.0),
        'rel_bias': nrm(ks[7], (N_BUCKETS, N_Q_HEADS), 0.1),
        'w_in': nrm(ks[8], (DEPTH, D_MODEL, N_IN), D_MODEL ** -0.5),
        'b_if': b_if,
        'sinks': nrm(ks[11], (DEPTH, N_Q_HEADS), 0.5),
        'g_attn_norm': 1.0 + nrm(ks[12], (DEPTH, D_MODEL), 0.05),
        'g_head': 1.0 + nrm(ks[13], (DEPTH, M_WIDTH), 0.05),
        'w_att_out': nrm(ks[14], (DEPTH, ATT_WIDTH, D_MODEL), ATT_WIDTH ** -0.5),
        'w_mlstm_out': nrm(ks[15], (DEPTH, M_WIDTH, D_MODEL), M_WIDTH ** -0.5),
        'w_out': nrm(ks[16], (DEPTH, D_MODEL, D_MODEL), D_MODEL ** -0.5),
        'g_ffn_norm': 1.0 + nrm(ks[17], (DEPTH, D_MODEL), 0.05),
        'w_gate': nrm(ks[18], (DEPTH, D_MODEL, D_FF), D_MODEL ** -0.5),
        'w_up': nrm(ks[19], (DEPTH, D_MODEL, D_FF), D_MODEL ** -0.5),
        'w_down': nrm(ks[20], (DEPTH, D_FF, D_MODEL), D_FF ** -0.5),
        'g_final': 1.0 + nrm(ks[21], (D_MODEL,), 0.05),
    }


def reference(x_prompt, x_sample, cache_k_win, cache_v_win, state_mlstm_C, state_mlstm_n, state_mlstm_m,
              rel_bias, w_in, b_if, sinks, g_attn_norm, g_head, w_att_out, w_mlstm_out, w_out,
              g_ffn_norm, w_gate, w_up, w_down, g_final):
    xp, xs = x_prompt, x_sample
    p_new, s_new = [], []
    for l in range(DEPTH):
        lw = (rel_bias, w_in[l], b_if[l], sinks[l], g_attn_norm[l], g_head[l], w_att_out[l],
              w_mlstm_out[l], w_out[l], g_ffn_norm[l], w_gate[l], w_up[l], w_down[l])
        xp, st_p = decoder_layer(xp, None, None, None, None, None, *lw)
        xs, st_s = decoder_layer(xs, cache_k_win[l], cache_v_win[l], state_mlstm_C[l],
                                 state_mlstm_n[l], state_mlstm_m[l], *lw)
        p_new.append(st_p)
        s_new.append(st_s)
    y_prompt = rmsnorm(xp, g_final)
    y_sample = rmsnorm(xs, g_final)
    p_k, p_v, p_c, p_n, p_m = [jnp.stack(a) for a in zip(*p_new)]
    s_k, s_v, s_c, s_n, s_m = [jnp.stack(a) for a in zip(*s_new)]
    return (y_prompt, y_sample, p_k, p_v, p_c, p_n, p_m, s_k, s_v, s_c, s_n, s_m)
```

```python
import contextlib
import math
import numpy as np
import ml_dtypes
import concourse.bass as bass
import concourse.mybir as mybir
from concourse.ap import AP
from concourse.bass_utils import run_bass_kernel_spmd

F32 = mybir.dt.float32
BF16 = mybir.dt.bfloat16
AF = mybir.ActivationFunctionType
ALU = mybir.AluOpType

D = 1024
SEQ = 8192
NCORE = 8
SEG = 2048
NT = 16
SBT = 8
NSB = NT // SBT
TSB = SBT * 128
NBLK = TSB // 512
N_IN = 4360
DFF = 2816
NFF = DFF // 128
EPS = 1e-6
NEGB = -30000.0
O_QA, O_KA, O_VA, O_QM, O_KM, O_VM, O_OM, O_I, O_F, O_GA, O_GM = 0, 512, 640, 768, 1024, 1280, 1792, 2304, 2308, 2312, 3336


class Tile:
    __slots__ = ("name", "w", "r", "excl")

    def __init__(self, name, excl=False):
        self.name = name
        self.w = None
        self.r = {}
        self.excl = excl


class Sched:
    def __init__(self, nc, sems, lanes_per_q=8):
        self.nc = nc
        self.engs = {"pe": nc.tensor, "act": nc.scalar, "dve": nc.vector, "pool": nc.gpsimd, "sp": nc.sync}
        self.q = {k: [] for k in self.engs}
        self.cnt = {k: 0 for k in self.engs}
        it = iter(sems)
        self.sem = {k: next(it) for k in self.engs}
        self.lanes = {}
        for k in ("sp", "pool"):
            self.lanes[k] = [[next(it), 0] for _ in range(lanes_per_q)]
        self.lane_i = {k: 0 for k in self.lanes}
        self.seen = {k: {} for k in self.engs}

    def _need(self, eng, waits, ev, same_ok=False):
        if ev is None:
            return
        sem, val, src = ev
        if src == eng and (same_ok or eng == "pe"):
            return
        key = id(sem)
        if self.seen[eng].get(key, 0) >= val:
            return
        cur = waits.get(key)
        if cur is None or cur[1] < val:
            waits[key] = (sem, val)

    def _deps(self, eng, reads, writes):
        waits = {}
        for t in reads:
            self._need(eng, waits, t.w)
        for t in writes:
            self._need(eng, waits, t.w, same_ok=True)
            for ev in t.r.values():
                self._need(eng, waits, ev, same_ok=True)
        for key, (sem, val) in waits.items():
            self.seen[eng][key] = val
        return list(waits.values())

    def op(self, eng, fn, reads=(), writes=()):
        if any(t.excl for t in reads):
            writes = list(writes) + [t for t in reads if t.excl]
            reads = [t for t in reads if not t.excl]
        waits = self._deps(eng, reads, writes)
        self.cnt[eng] += 1
        sem = self.sem[eng]
        ev = (sem, self.cnt[eng], eng)
        self.q[eng].append((waits, fn, (sem, 1)))
        for t in reads:
            t.r[id(sem)] = ev
        for t in writes:
            t.w = ev
            t.r = {}
        return ev

    def dma(self, q, out_ap, in_ap, reads=(), writes=(), **kw):
        waits = self._deps(q, reads, writes)
        lanes = self.lanes[q]
        li = self.lane_i[q]
        self.lane_i[q] = (li + 1) % len(lanes)
        lane = lanes[li]
        sem = lane[0]
        if lane[1] > 0 and self.seen[q].get(id(sem), 0) < lane[1]:
            waits.append((sem, lane[1]))
            self.seen[q][id(sem)] = lane[1]
        lane[1] += 16
        ev = (sem, lane[1], "dma")

        def fn(e, out_ap=out_ap, in_ap=in_ap, kw=kw):
            return e.dma_start(out=out_ap, in_=in_ap, **kw)

        self.q[q].append((waits, fn, (sem, 16)))
        for t in reads:
            t.r[id(sem)] = ev
        for t in writes:
            t.w = ev
            t.r = {}
        return ev

    def barrier(self):
        evs = []
        for k in self.engs:
            if self.cnt[k] > 0:
                evs.append((self.sem[k], self.cnt[k], k))
        for k, lanes in self.lanes.items():
            for sem, val in lanes:
                if val > 0:
                    evs.append((sem, val, "dma"))
        for k in self.engs:
            waits = []
            for sem, val, src in evs:
                if src == k:
                    continue
                if self.seen[k].get(id(sem), 0) >= val:
                    continue
                self.seen[k][id(sem)] = val
                waits.append((sem, val))
            if waits:
                self.q[k].append((waits, None, None))

    def finish(self):
        waits = []
        for k, lanes in self.lanes.items():
            for sem, val in lanes:
                if val > 0:
                    waits.append((sem, val))
        self.q["sp"].append((waits, None, None))

    def emit(self):
        nc = self.nc
        with nc.Block() as block:
            def mk(name):
                def body(e):
                    for waits, fn, inc in self.q[name]:
                        ws = list(waits)
                        if fn is None:
                            for sem, val in ws:
                                e.wait_ge(sem, val)
                            continue
                        for sem, val in ws[:-1]:
                            e.wait_ge(sem, val)
                        ins = fn(e)
                        if ws:
                            ins._wait_ge(ws[-1][0], ws[-1][1])
                        ins.then_inc(inc[0], inc[1])
                return body
            block.tensor(mk("pe"))
            block.scalar(mk("act"))
            block.vector(mk("dve"))
            block.gpsimd(mk("pool"))
            block.sync(mk("sp"))


def t5_bucket_np(n):
    n = np.maximum(n, 0)
    max_exact = 16
    nf = np.maximum(n, 1).astype(np.float32)
    large = max_exact + (np.log(nf / max_exact) / math.log(128 / max_exact) * (32 - max_exact)).astype(np.int32)
    large = np.minimum(large, 31)
    return np.where(n < max_exact, n, large)


class _Stop(Exception):
    pass


def build_program(debug=(), stop=None, nonce=0.0):
    nc = bass.Bass("TRN2", target_bir_lowering=False)
    dbg_outs = {}

    def stage(name):
        if stop == name:
            raise _Stop()

    def din(name, shape, dt=F32):
        return nc.dram_tensor(name, list(shape), dt, kind="ExternalInput")

    def dout(name, shape, dt=F32):
        return nc.dram_tensor(name, list(shape), dt, kind="ExternalOutput")

    xs_d = din("xs", [128 + SEG, D]).ap()
    flag_d = din("flag", [128, 1]).ap()
    w_in_d = din("w_in", [D, N_IN]).ap()
    b_if_d = din("b_if", [2, 4]).ap()
    sinks_d = din("sinks", [1, 8]).ap()
    relb_d = din("rel_bias", [32, 8]).ap()
    g_attn_d = din("g_attn", [1, D]).ap()
    g_head_d = din("g_head", [1, 512]).ap()
    g_ffn_d = din("g_ffn", [1, D]).ap()
    g_fin_d = din("g_final", [1, D]).ap()
    w_ao_d = din("w_att_out", [512, D]).ap()
    w_mo_d = din("w_mlstm_out", [512, D]).ap()
    w_out_d = din("w_out", [D, D]).ap()
    w_gate_d = din("w_gate", [D, DFF]).ap()
    w_up_d = din("w_up", [D, DFF]).ap()
    w_down_d = din("w_down", [DFF, D]).ap()
    ident_bf_d = din("ident_bf", [128, 128], BF16).ap()
    ident_f_d = din("ident_f", [128, 128]).ap()
    ohT_d = din("ohT_rev", [32, 128]).ap()
    caus_d = din("causT", [128, 128]).ap()
    sel_d = din("sel", [4, 128]).ap()
    pm_d = din("pm", [4, 2]).ap()
    xsm_d = din("xsm", [16, D]).ap()
    ck_d = din("ck", [16, 128, 128])
    cv_d = din("cv", [16, 128, 128])
    sC_d = din("sC", [16, 4, 128, 64])
    sn_d = din("sn", [16, 4, 64])
    sm_d = din("sm", [16, 4])
    xprev_d = din("xprev", [3 * SEG, D]).ap()
    pact_d = din("pact", [128, 4]).ap()

    y_d = dout("y", [SEG, D]).ap()
    pk_d = dout("pk", [128, 128]).ap()
    pv_d = dout("pv", [128, 128]).ap()
    pC_d = dout("pC", [4, 128, 64]).ap()
    pn_d = dout("pn", [4, 64]).ap()
    pm_out_d = dout("pm_out", [4, 1]).ap()
    ys_d = dout("ys", [16, D]).ap()
    sko_d = dout("sko", [16, 128, 128])
    svo_d = dout("svo", [16, 128, 128])
    sCo_d = dout("sCo", [16, 4, 128, 64])
    sno_d = dout("sno", [16, 4, 2, 64])
    smo_d = dout("smo", [16, 4, 2])
    wr_scr = nc.dram_tensor("wr_scr", [8, 512], F32)

    es = contextlib.ExitStack()
    with es:
        def sb(name, shape, dt=F32):
            return es.enter_context(nc.sbuf_tensor("sb_" + name, list(shape), dt))

        sems = [es.enter_context(nc.semaphore(f"s{i}")) for i in range(5 + 16)]
        S = Sched(nc, sems)
        ps = es.enter_context(nc.psum_tensor("ps", [128, 8, 512], F32))
        PB = [Tile(f"bank{i}", excl=True) for i in range(8)]
        bank_rr = [0]

        def nextbank():
            b = bank_rr[0]
            bank_rr[0] = (b + 1) % 8
            return b

        def ps_bf(b):
            return ps[:, b, :].bitcast(BF16)

        def ACT(out, in_, func, reads, writes, **kw):
            S.op("act", lambda e: e.activation(out, in_, func, **kw), reads, writes)

        def TT(out, a, b, op, reads, writes, eng="dve"):
            S.op(eng, lambda e: e.tensor_tensor(out, a, b, op), reads, writes)

        def TS(out, a, s1, s2, op0, op1, reads, writes, eng="dve"):
            S.op(eng, lambda e: e.tensor_scalar(out, a, s1, s2, op0, op1), reads, writes)

        def STT(out, a, scalar, b, op0, op1, reads, writes, eng="dve"):
            S.op(eng, lambda e: e.scalar_tensor_tensor(out, a, scalar, b, op0, op1), reads, writes)

        def CP(out, in_, reads, writes, eng="dve"):
            S.op(eng, lambda e: e.tensor_copy(out, in_), reads, writes)

        def MM(out, lhsT, rhs, start, stop, reads, writes):
            S.op("pe", lambda e: e.matmul(out, lhsT, rhs, start=start, stop=stop), reads, writes)

        def TR(out, in_, ident, reads, writes):
            S.op("pe", lambda e: e.transpose(out, in_, ident), reads, writes)

        def DMA(q, out, in_, reads, writes, **kw):
            S.dma(q, out, in_, reads, writes, **kw)

        class Ring:
            def __init__(self, name, shape, dt, n):
                self.bufs = [(sb(f"{name}{i}", shape, dt), Tile(f"{name}{i}")) for i in range(n)]
                self.i = 0

            def next(self):
                r = self.bufs[self.i]
                self.i = (self.i + 1) % len(self.bufs)
                return r

        in_pre = [False]
        pre_tiles = {}
        pre_ones_done = set()
        pre_ring = [None]

        def dbg(name, ap, tile, shape, dt=F32):
            if name not in debug or (in_pre[0] and name in dbg_outs):
                return
            o = dout("dbg_" + name, shape, dt).ap()
            dbg_outs[name] = o
            DMA("sp", o, ap, [tile], [Tile("dbgo_" + name)])

        ident_bf = sb("ident_bf", [128, 128], BF16); Tidb = Tile("idb")
        ident_f = sb("ident_f", [128, 128]); Tidf = Tile("idf")
        caus = sb("caus", [128, 128]); Tcaus = Tile("caus")
        ones_bf = sb("ones_bf", [128, 128], BF16); Tones = Tile("ones")
        one_c = sb("one_c", [128, 1]); Tonesr = Tile("onec")
        BTall = sb("BTall", [128, 2, 2, 2, 2, 128]); TBT = Tile("BTall")
        esink = sb("esink", [1, 8, 128], BF16); Tesink = Tile("esink")
        flag = sb("flag", [128, 1]); Tflag = Tile("flag")
        zero_c = sb("zero_c", [128, 1]); Tzero = Tile("zero")
        eps_c = sb("eps_c", [128, 1]); Teps = Tile("eps")
        g_attn = sb("g_attnT", [128, 8]); Tgattn = Tile("gattn")
        g_ffn = sb("g_ffnT", [128, 8]); Tgffn = Tile("gffn")
        g_fin = sb("g_fin", [128, D]); Tgfin = Tile("gfin")
        g_head = sb("g_head", [128, 512]); Tghead = Tile("ghead")
        b_i = sb("b_i", [4, 1]); b_f = sb("b_f", [4, 1]); Tbif = Tile("bif")
        selm = sb("selm", [4, 128]); pmm = sb("pmm", [4, 2]); Tsel = Tile("sel")
        kaT_n = sb("kaT_n", [128, 128 + SEG], BF16)
        kaT_s = sb("kaT_s", [128, 128 + SEG], BF16)
        TkaT = [Tile(f"kaT{t}") for t in range(NT + 1)]
        va = sb("va", [128, NT + 1, 2, 2, 64], BF16)
        Tva = [Tile(f"va{t}") for t in range(NT + 1)]
        ST = sb("ST", [128, 2, 129]); TST = Tile("ST")
        rstate = sb("rstate", [4, 4]); Trst = Tile("rstate")
        hT = sb("hT", [128, 8, TSB], BF16)
        ThT = [Tile(f"hT{t}") for t in range(SBT)]
        hTh = sb("hTh", [128, 8, 128], BF16); ThTh = Tile("hTh")
        yattT = sb("yattT", [128, 4, TSB], BF16)
        TyaT = [Tile(f"yaT{t}") for t in range(SBT)]
        ymT = sb("ymT", [128, 4, TSB], BF16)
        TymT = [Tile(f"ymT{t}") for t in range(SBT)]
        hTs = sb("hTs", [128, 8, 16], BF16); ThTs = Tile("hTs")
        zs = sb("zs", [16, 2312]); Tzs = Tile("zs")
        yattTs = sb("yattTs", [128, 4, 16], BF16); TyaTs = Tile("yaTs")
        ymTs = sb("ymTs", [128, 4, 16], BF16); TymTs = Tile("ymTs")
        wbuf = sb("wbuf", [128, 3 * 4096], BF16)

        class VRing:
            def __init__(self, name, n, size):
                self.bufs = [(wbuf[:, i * size:(i + 1) * size], Tile(f"{name}{i}")) for i in range(n)]
                self.i = 0
                self.n = n

            def next(self):
                r = self.bufs[self.i]
                self.i = (self.i + 1) % len(self.bufs)
                return r
        xring = Ring("xt", [128, D], F32, 3)
        hbring = Ring("hbf", [128, D], BF16, 2)
        d4ring = Ring("d4", [128, 12], F32, 4)
        scal = sb("scal", [128, 64]); scal_i = [0]
        Tscal = [Tile(f"scal{i}") for i in range(64)]

        def newscal():
            i = scal_i[0]
            scal_i[0] = (i + 1) % 64
            return scal[:, i:i + 1], Tscal[i]

        ARENA_BYTES = 83 * 1024
        arena = sb("arena", [128, ARENA_BYTES // 4], F32)

        def carve(off_bytes, shape, dt):
            esz = 2 if dt == BF16 else 4
            n = int(np.prod(shape[1:]))
            assert off_bytes % 4 == 0 and off_bytes + n * esz <= ARENA_BYTES, (off_bytes, shape)
            v = arena[0:shape[0], off_bytes // 4: off_bytes // 4 + (n * esz) // 4]
            if dt == BF16:
                v = v.bitcast(BF16)
            if len(shape) == 2:
                return v
            names = "abcdef"[: len(shape) - 1]
            pat = "p (" + " ".join(names) + ") -> p " + " ".join(names)
            kw = {names[i]: shape[i + 1] for i in range(len(shape) - 1)}
            return v.rearrange(pat, **kw)

        DMA("sp", ident_bf[:], ident_bf_d, [], [Tidb])
        DMA("sp", ident_f[:], ident_f_d, [], [Tidf])
        DMA("sp", caus[:], caus_d, [], [Tcaus])
        DMA("sp", flag[:], flag_d, [], [Tflag])
        DMA("sp", g_attn[:], AP(g_attn_d.tensor, 0, [[1, 128], [128, 8]]), [], [Tgattn], allow_slow_non_contiguous=True)
        DMA("sp", g_ffn[:], AP(g_ffn_d.tensor, 0, [[1, 128], [128, 8]]), [], [Tgffn], allow_slow_non_contiguous=True)
        DMA("sp", g_fin[:], AP(g_fin_d.tensor, 0, [[0, 128], [1, D]]), [], [Tgfin])
        DMA("sp", g_head[:], AP(g_head_d.tensor, 0, [[0, 128], [1, 512]]), [], [Tghead])
        DMA("sp", b_i[:], AP(b_if_d.tensor, 0, [[1, 4], [1, 1]]), [], [Tbif])
        DMA("sp", b_f[:], AP(b_if_d.tensor, 4, [[1, 4], [1, 1]]), [], [Tbif])
        DMA("sp", selm[:], sel_d, [], [Tsel])
        DMA("sp", pmm[:], pm_d, [], [Tsel])
        S.op("dve", lambda e: e.memset(ones_bf[:], 1.0), [], [Tones])
        S.op("dve", lambda e: e.memset(one_c[:], 1.0), [], [Tonesr])
        S.op("dve", lambda e: e.memset(zero_c[:], 0.0), [], [Tzero])
        S.op("dve", lambda e: e.memset(eps_c[:], EPS), [], [Teps])
        S.op("dve", lambda e: e.memset(ST[:], 0.0), [], [TST])
        S.op("dve", lambda e: e.memset(rstate[:], 0.0), [], [Trst])

        relb = carve(0, [32, 8], F32); Trelb = Tile("relb")
        ohT = carve(64, [32, 128], F32); TohT = Tile("ohT")
        DMA("sp", relb[:], relb_d, [], [Trelb])
        DMA("sp", ohT[:], ohT_d, [], [TohT])
        tbr = carve(1024, [8, 512], F32); Ttbr = Tile("tbr")
        S.op("dve", lambda e: e.memset(tbr[:], NEGB), [], [Ttbr])
        b0 = nextbank()
        MM(ps[0:8, b0, 0:128], relb[:], ohT[:], True, True, [Trelb, TohT], [PB[b0]])
        CP(tbr[:, 256:384], ps[0:8, b0, 0:128], [PB[b0]], [Ttbr])
        Tscr = Tile("wr_scr")
        DMA("sp", wr_scr.ap(), tbr[:], [Ttbr], [Tscr])
        hk = carve(4096, [128, 8, 128], F32); Thk = Tile("hk")
        for c, base in ((1, 256), (0, 128)):
            DMA("sp", hk[:], AP(wr_scr, base, [[1, 128], [512, 8], [1, 128]]), [Tscr], [Thk])
            hv = hk[:]
            pst = list(hv.ap[0])
            for g in range(2):
                rev = AP(hv.tensor, hv.offset + (4 * g) * 128 + 127, [pst, [128, 2], [256, 2], [-1, 128]])
                CP(BTall[:, g, :, c, :, :], rev, [Thk], [TBT])
        sk = carve(8192 + 64, [1, 8], F32); Tsk = Tile("sk")
        DMA("sp", sk[:], sinks_d, [], [Tsk])
        ACT(sk[:], sk[:], AF.Exp, [Tsk], [Tsk])
        CP(esink[:], sk[:].unsqueeze(2).to_broadcast([1, 8, 128]), [Tsk], [Tesink])
        dbg("BTall", BTall[:], TBT, [128, 2, 2, 2, 2, 128])
        S.barrier()

        w_in_v = w_in_d.rearrange("(k p) n -> p k n", p=128)
        w_gate_v = w_gate_d.rearrange("(k p) n -> p k n", p=128)
        w_up_v = w_up_d.rearrange("(k p) n -> p k n", p=128)
        w_out_v = w_out_d.rearrange("(k p) n -> p k n", p=128)
        w_ao_v = w_ao_d.rearrange("(k p) n -> p k n", p=128)
        w_mo_v = w_mo_d.rearrange("(k p) n -> p k n", p=128)
        w_down_v = w_down_d.rearrange("(k p) n -> p k n", p=128)

        class WStream:
            def __init__(self, ring):
                self.ring = ring
                self.specs = []
                self.loaded = 0
                self.views = {}

            def add(self, K, cols, parts):
                self.specs.append((K, cols, parts))
                return len(self.specs) - 1

            def _load(self, i):
                K, cols, parts = self.specs[i]
                slot, tl = self.ring.next()
                v = slot[:, 0:K * cols].rearrange("p (k c) -> p k c", k=K)
                for (c0, n, src) in parts:
                    DMA("pool", v[:, :, c0:c0 + n], src, [], [tl])
                self.views[i] = (v, tl)

            def get(self, i, live_from=None):
                lf = i if live_from is None else live_from
                while self.loaded < min(lf + self.ring.n, len(self.specs)):
                    self._load(self.loaded)
                    self.loaded += 1
                return self.views[i]

        TR_BANKS = (7, 6)
        tr_i = [0]

        def norm_T(src_ap, src_reads, g_bc, Tg, dstT, Tdst, tn=128, src_is_dram=True, keep=None):
            if src_is_dram:
                xt, Txt = xring.next()
                DMA("sp", xt[0:tn, :], src_ap, src_reads, [Txt])
                xin = xt[0:tn, :]
                rd = [Txt]
            else:
                xin = src_ap
                rd = list(src_reads)
            hb, Thb = hbring.next()
            ss, Tss = newscal()
            ACT(hb[0:tn, :], xin, AF.Square, rd, [Thb, Tss], accum_out=ss[0:tn, :])
            rr, Trr = newscal()
            ACT(rr[0:tn, :], ss[0:tn, :], AF.Ln, [Tss, Teps], [Trr], scale=1.0 / D, bias=eps_c[0:tn, 0:1])
            ACT(rr[0:tn, :], rr[0:tn, :], AF.Exp, [Trr], [Trr], scale=-0.5)
            ACT(hb[0:tn, :], xin, AF.Copy, rd + [Trr], [Thb], scale=rr[0:tn, 0:1])
            trb = TR_BANKS[tr_i[0] % 2]
            tr_i[0] += 1
            pv = ps_bf(trb)
            for k in range(8):
                TR(pv[:, k * 128:k * 128 + tn], hb[0:tn, k * 128:(k + 1) * 128], ident_bf[0:tn, 0:tn], [Thb, Tidb], [PB[trb]])
            src = pv.rearrange("p (k t) -> p k t", k=8)[:, :, 0:tn]
            TT(dstT, src, g_bc[:, :].unsqueeze(2).to_broadcast([128, 8, tn]), ALU.mult, [PB[trb], Tg], [Tdst])


        def sample_phase():
            S.barrier()
            RX = mybir.AxisListType.X
            o = 0
            Cl = carve(o, [128, 64, 64], F32); o += 16384
            tmpC = carve(o, [128, 64, 64], F32)
            Kf = carve(o, [128, 16, 128], F32)
            Vf = carve(o + 8192, [128, 16, 128], F32); o += 16384
            Kb = carve(o, [128, 16, 128], BF16); o += 4096
            KT = carve(o, [128, 16, 128], BF16); o += 4096
            V1 = carve(o, [128, 16, 2, 66], BF16); o += 16 * 2 * 66 * 2
            qg = carve(o, [16, 4, 128], BF16); o += 1024
            qTs = carve(o, [128, 4, 16], BF16); o += 128
            bcol = carve(o, [128, 8], F32); o += 32
            Ss = carve(o, [128, 2, 16, 4], F32); o += 512
            Es = carve(o, [128, 2, 16, 4], BF16); o += 256
            pvs = carve(o, [4, 32, 66], F32); o += 32 * 66 * 4
            yas = carve(o, [16, 8, 66], F32); o += 8 * 66 * 4
            t512 = carve(o, [16, 512], F32); o += 2048
            u512 = carve(o, [16, 512], F32); o += 2048
            ogs = carve(o, [16, 512], F32); o += 2048
            hms = carve(o, [16, 512], F32); o += 2048
            yab = carve(o, [16, 512], BF16); o += 1024
            ymb = carve(o, [16, 512], BF16); o += 1024
            s16 = carve(o, [16, 64], F32); o += 256
            ql = carve(o, [128, 64], F32); o += 256
            kl = carve(o, [128, 64], F32); o += 256
            vl = carve(o, [128, 64], F32); o += 256
            nl = carve(o, [128, 64], F32); o += 256
            hl = carve(o, [128, 64], F32); o += 256
            t64 = carve(o, [128, 64], F32); o += 256
            Cq = carve(o, [128, 64], F32); o += 256
            sl_ = carve(o, [128, 32], F32); o += 128
            assert o <= ARENA_BYTES, o
            TCl, TtmpC, TKf, TVf, TKb, TKT, TV1, Tqg, TqTs, Tbcol, TSs, TEs, Tpvs, Tyas = [Tile(n) for n in
                ("Cl", "tmpC", "Kf", "Vf", "Kb", "KT", "V1", "qg", "qTs", "bcol", "Ss", "Es", "pvs", "yas")]
            TKf = TtmpC
            TVf = TtmpC
            Tt512, Tu512, Togs, Thms, Tyab, Tymb, Ts16, Tql, Tkl, Tvl, Tnl, Thl, Tt64, TCq, Tsl = [Tile(n) for n in
                ("t512", "u512", "ogs", "hms", "yab", "ymb", "s16", "ql", "kl", "vl", "nl", "hl", "t64", "Cq", "sl")]
            DMA("sp", Kf[:], ck_d.ap().rearrange("b k c -> k b c"), [], [TKf])
            DMA("pool", Vf[:], cv_d.ap().rearrange("b k c -> k b c"), [], [TVf])
            sCv = sC_d.ap().rearrange("b h (vh v) d -> vh (b h) (v d)", vh=2)
            sCov = sCo_d.ap().rearrange("b h (vh v) d -> vh (b h) (v d)", vh=2)
            Clf = Cl[:].rearrange("p v d -> p (v d)")
            for vh in range(2):
                ln = slice(vh * 64, (vh + 1) * 64)
                DMA("sp", Clf[ln, :], sCv[vh], [], [TCl])
                DMA("pool", nl[ln, :], sn_d.ap().rearrange("b h d -> (b h) d"), [], [Tnl])
                DMA("sp", sl_[ln, 0:1], sm_d.ap().rearrange("b (h o) -> (b h) o", o=1), [], [Tsl])
                DMA("pool", ql[ln, :], zs[:, 768:1024].rearrange("b (h d) -> b h d", h=4), [Tzs], [Tql])
                DMA("sp", kl[ln, :], zs[:, 1024:1280].rearrange("b (h d) -> b h d", h=4), [Tzs], [Tkl])
                DMA("pool", vl[ln, :], zs[:, 1280:1792].rearrange("b (h w v) -> b h w v", h=4, w=2)[:, :, vh, :], [Tzs], [Tvl])
                DMA("sp", sl_[ln, 1:2], zs[:, 2304:2308].rearrange("b (h o) -> b h o", o=1), [Tzs], [Tsl])
                DMA("pool", sl_[ln, 2:3], zs[:, 2308:2312].rearrange("b (h o) -> b h o", o=1), [Tzs], [Tsl])
            DMA("sp", sl_[:, 3:4], AP(b_if_d.tensor, 0, [[0, 32], [1, 4], [1, 1]]), [], [Tsl])
            DMA("pool", sl_[:, 4:5], AP(b_if_d.tensor, 4, [[0, 32], [1, 4], [1, 1]]), [], [Tsl])
            DMA("sp", bcol[:], AP(wr_scr, 255, [[1, 128], [512, 8]]), [Tscr], [Tbcol], allow_slow_non_contiguous=True)
            DMA("pool", s16[:, 0:8], AP(wr_scr, 383, [[0, 16], [512, 8]]), [Tscr], [Ts16], allow_slow_non_contiguous=True)
            DMA("sp", s16[:, 8:16], AP(sinks_d.tensor, 0, [[0, 16], [1, 8]]), [], [Ts16])
            DMA("pool", sko_d.ap()[:, 0:127, :], ck_d.ap()[:, 1:128, :], [], [Tile("sko")])
            DMA("sp", svo_d.ap()[:, 0:127, :], cv_d.ap()[:, 1:128, :], [], [Tile("svo")])
            DMA("pool", sko_d.ap()[:, 127, :], zs[:, 512:640], [Tzs], [Tile("sko2")])
            DMA("sp", svo_d.ap()[:, 127, :], zs[:, 640:768], [Tzs], [Tile("svo2")])
            CP(Kb[:], Kf[:], [TKf], [TKb])
            for half in range(2):
                bk = 4 + half
                pvb = ps_bf(bk)
                for i in range(8):
                    b_ = half * 8 + i
                    TR(pvb[:, i * 128:(i + 1) * 128], Kb[:, b_, :], ident_bf[:], [TKb, Tidb], [PB[bk]])
                ACT(KT[:, half * 8:(half + 1) * 8, :], pvb.rearrange("p (i k) -> p i k", i=8), AF.Copy, [PB[bk]], [TKT])
            CP(V1[:, :, :, 0:64], Vf[:].rearrange("k b (g d) -> k b g d", g=2), [TVf], [TV1])
            S.op("dve", lambda e: e.memset(V1[:, :, :, 64:65], 1.0), [], [TV1])
            TS(qg[:].rearrange("b j (g d) -> b j g d", g=2), zs[:, 0:512].rearrange("b (g j d) -> b j g d", g=2, j=4), 0.125, None, ALU.mult, ALU.bypass, [Tzs], [Tqg])
            pvq = ps_bf(6)
            for j in range(4):
                TR(pvq[:, j * 16:(j + 1) * 16], qg[:, j, :], ident_bf[0:16, 0:16], [Tqg, Tidb], [PB[6]])
            CP(qTs[:].rearrange("p j b -> p (j b)"), pvq[:, 0:64], [PB[6]], [TqTs])
            for b_ in range(16):
                for g in range(2):
                    gr = slice(g * 64, (g + 1) * 64)
                    MM(ps[:, g, b_ * 4:(b_ + 1) * 4], KT[gr, b_, :], qTs[gr, :, b_], True, True, [TKT, TqTs], [PB[g]])
            for g in range(2):
                TT(Ss[:, g, :, :], ps[:, g, 0:64].rearrange("p (b j) -> p b j", j=4), bcol[:, 4 * g:4 * g + 4].unsqueeze(1).to_broadcast([128, 16, 4]), ALU.add, [PB[g], Tbcol], [TSs])
            ACT(Es[:], Ss[:], AF.Exp, [TSs], [TEs])
            for b_ in range(16):
                for g in range(2):
                    slot = b_ * 2 + g
                    bk = 2 + slot // 7 if slot < 28 else 7
                    if slot >= 28:
                        col = (slot - 28) * 66
                    else:
                        col = (slot % 7) * 66
                    bk = (2 + slot // 7) if slot < 28 else 7
                    bk = [2, 3, 4, 5, 7][min(slot // 7, 4)]
                    col = (slot % 7) * 66
                    MM(ps[0:4, bk, col:col + 65], Es[:, g, b_, :], V1[:, b_, g, 0:65], True, True, [TEs, TV1], [PB[bk]])
            for gi_, bk in enumerate([2, 3, 4, 5, 7]):
                ns = 7 if gi_ < 4 else 4
                CP(pvs[:, gi_ * 7:gi_ * 7 + ns, :], ps[0:4, bk, 0:ns * 66].rearrange("p (s c) -> p s c", c=66), [PB[bk]], [Tpvs])
            yas_v = yas[:].rearrange("b (g j) c -> b g j c", g=2)
            for j in range(4):
                DMA("pool", yas_v[:, :, j, :], pvs[j:j + 1, :, :], [Tpvs], [Tyas])
            qv = zs[:, 0:512].rearrange("b (g j d) -> b g j d", g=2, j=4)
            kv = zs[:, 512:640].rearrange("b (g d) -> b g d", g=2).unsqueeze(2).to_broadcast([16, 2, 4, 64])
            TT(t512[:].rearrange("b (g j d) -> b g j d", g=2, j=4), qv, kv, ALU.mult, [Tzs], [Tt512])
            S.op("dve", lambda e: e.tensor_reduce(s16[:, 16:24], t512[:].rearrange("b (h d) -> b h d", d=64), RX, ALU.add), [Tt512], [Ts16])
            STT(s16[:, 16:24], s16[:, 16:24], 0.125, s16[:, 0:8], ALU.mult, ALU.add, [Ts16], [Ts16])
            ACT(s16[:, 16:24], s16[:, 16:24], AF.Exp, [Ts16], [Ts16])
            ACT(s16[:, 24:32], s16[:, 8:16], AF.Exp, [Ts16], [Ts16])
            TT(s16[:, 32:40], yas[:, :, 64], s16[:, 16:24], ALU.add, [Tyas, Ts16], [Ts16])
            TT(s16[:, 32:40], s16[:, 32:40], s16[:, 24:32], ALU.add, [Ts16], [Ts16])
            S.op("dve", lambda e: e.reciprocal(s16[:, 32:40], s16[:, 32:40]), [Ts16], [Ts16])
            vv = zs[:, 640:768].rearrange("b (g d) -> b g d", g=2).unsqueeze(2).to_broadcast([16, 2, 4, 64])
            es_b = s16[:, 16:24].rearrange("b (g j) -> b g j", g=2).unsqueeze(3).to_broadcast([16, 2, 4, 64])
            TT(u512[:].rearrange("b (g j d) -> b g j d", g=2, j=4), vv, es_b, ALU.mult, [Tzs, Ts16], [Tu512])
            TT(u512[:].rearrange("b (h d) -> b h d", d=64), u512[:].rearrange("b (h d) -> b h d", d=64), yas[:, :, 0:64], ALU.add, [Tu512, Tyas], [Tu512])
            TT(yab[:].rearrange("b (h d) -> b h d", d=64), u512[:].rearrange("b (h d) -> b h d", d=64),
               s16[:, 32:40].unsqueeze(2).to_broadcast([16, 8, 64]), ALU.mult, [Tu512, Ts16], [Tyab])
            pva = ps_bf(6)
            for p in range(4):
                TR(pva[:, p * 16:(p + 1) * 16], yab[:, p * 128:(p + 1) * 128], ident_bf[0:16, 0:16], [Tyab, Tidb], [PB[6]])
            CP(yattTs[:].rearrange("p c b -> p (c b)"), pva[:, 0:64], [PB[6]], [TyaTs])
            TS(kl[:], kl[:], 0.125, None, ALU.mult, ALU.bypass, [Tkl], [Tkl])
            c_ = lambda i: sl_[:, i:i + 1]
            TT(c_(5), c_(1), c_(3), ALU.add, [Tsl], [Tsl])
            TT(c_(6), c_(2), c_(4), ALU.add, [Tsl], [Tsl])
            ACT(c_(6), c_(6), AF.Exp, [Tsl], [Tsl], scale=-1.0)
            ACT(c_(6), c_(6), AF.Ln, [Tsl], [Tsl], bias=1.0)
            TT(c_(7), c_(0), c_(6), ALU.subtract, [Tsl], [Tsl])
            TT(c_(8), c_(7), c_(5), ALU.max, [Tsl], [Tsl])
            TT(c_(9), c_(7), c_(8), ALU.subtract, [Tsl], [Tsl])
            TT(c_(10), c_(5), c_(8), ALU.subtract, [Tsl], [Tsl])
            TS(c_(11), c_(8), -1.0, None, ALU.mult, ALU.bypass, [Tsl], [Tsl])
            ACT(sl_[:, 9:12], sl_[:, 9:12], AF.Exp, [Tsl], [Tsl])
            TT(tmpC[:], Cl[:], ql[:].unsqueeze(1).to_broadcast([128, 64, 64]), ALU.mult, [TCl, Tql], [TtmpC])
            S.op("dve", lambda e: e.tensor_reduce(Cq[:], tmpC[:], RX, ALU.add), [TtmpC], [TCq])
            TT(t64[:], nl[:], ql[:], ALU.mult, [Tnl, Tql], [Tt64])
            S.op("dve", lambda e: e.tensor_reduce(c_(12), t64[:], RX, ALU.add), [Tt64], [Tsl])
            TT(t64[:], kl[:], ql[:], ALU.mult, [Tkl, Tql], [Tt64])
            S.op("dve", lambda e: e.tensor_reduce(c_(13), t64[:], RX, ALU.add), [Tt64], [Tsl])
            TT(c_(14), c_(10), c_(13), ALU.mult, [Tsl], [Tsl])
            STT(c_(15), c_(12), c_(9), c_(14), ALU.mult, ALU.add, [Tsl], [Tsl])
            STT(c_(16), c_(15), -1.0, c_(15), ALU.mult, ALU.max, [Tsl], [Tsl])
            TT(c_(16), c_(16), c_(11), ALU.max, [Tsl], [Tsl])
            S.op("dve", lambda e: e.reciprocal(c_(16), c_(16)), [Tsl], [Tsl])
            TS(hl[:], Cq[:], c_(9), None, ALU.mult, ALU.bypass, [TCq, Tsl], [Thl])
            STT(hl[:], vl[:], c_(14), hl[:], ALU.mult, ALU.add, [Tvl, Tsl, Thl], [Thl])
            TS(hl[:], hl[:], c_(16), None, ALU.mult, ALU.bypass, [Thl, Tsl], [Thl])
            TT(tmpC[:], vl[:].unsqueeze(2).to_broadcast([128, 64, 64]), kl[:].unsqueeze(1).to_broadcast([128, 64, 64]), ALU.mult, [Tvl, Tkl, TCq], [TtmpC])
            TS(Cl[:], Cl[:], c_(9), None, ALU.mult, ALU.bypass, [TCl, Tsl], [TCl])
            STT(Cl[:], tmpC[:], c_(10), Cl[:], ALU.mult, ALU.add, [TtmpC, Tsl, TCl], [TCl])
            TS(nl[:], nl[:], c_(9), None, ALU.mult, ALU.bypass, [Tnl, Tsl, Tt64], [Tnl])
            STT(nl[:], kl[:], c_(10), nl[:], ALU.mult, ALU.add, [Tkl, Tsl, Tnl], [Tnl])
            snov = sno_d.ap().rearrange("b h w d -> w (b h) d")
            smov = smo_d.ap().rearrange("b h (w o) -> w (b h) o", o=1)
            hms_v = hms[:].rearrange("b (h w v) -> b h w v", h=4, w=2)
            for vh in range(2):
                ln = slice(vh * 64, (vh + 1) * 64)
                DMA("sp", sCov[vh], Clf[ln, :], [TCl], [Tile("sCo")])
                DMA("pool", snov[vh], nl[ln, :], [Tnl], [Tile("sno")])
                DMA("sp", smov[vh], sl_[ln, 8:9], [Tsl], [Tile("smo")], allow_slow_non_contiguous=True)
                DMA("pool", hms_v[:, :, vh, :], hl[ln, :], [Thl], [Thms])
            TT(t512[:], hms[:], hms[:], ALU.mult, [Thms], [Tt512])
            S.op("dve", lambda e: e.tensor_reduce(s16[:, 40:44], t512[:].rearrange("b (h v) -> b h v", v=128), RX, ALU.add), [Tt512], [Ts16])
            ACT(s16[:, 40:44], s16[:, 40:44], AF.Ln, [Ts16, Teps], [Ts16], scale=1.0 / 128, bias=eps_c[0:16, 0:1])
            ACT(s16[:, 40:44], s16[:, 40:44], AF.Exp, [Ts16], [Ts16], scale=-0.5)
            ACT(ogs[:], zs[:, 1792:2304], AF.Sigmoid, [Tzs], [Togs])
            TT(ogs[:], ogs[:], g_head[0:16, :], ALU.mult, [Togs, Tghead], [Togs])
            TT(u512[:].rearrange("b (h v) -> b h v", v=128), hms[:].rearrange("b (h v) -> b h v", v=128),
               s16[:, 40:44].unsqueeze(2).to_broadcast([16, 4, 128]), ALU.mult, [Thms, Ts16, Tu512], [Tu512])
            TT(ymb[:], u512[:], ogs[:], ALU.mult, [Tu512, Togs], [Tymb])
            pvm = ps_bf(6)
            for h in range(4):
                TR(pvm[:, h * 16:(h + 1) * 16], ymb[:, h * 128:(h + 1) * 128], ident_bf[0:16, 0:16], [Tymb, Tidb], [PB[6]])
            CP(ymTs[:].rearrange("p c b -> p (c b)"), pvm[:, 0:64], [PB[6]], [TymTs])
            dbg("yab", yab[:], Tyab, [16, 512], BF16)
            dbg("ymb", ymb[:], Tymb, [16, 512], BF16)
            dbg("zs", zs[:], Tzs, [16, 2312])

        try:
            stage("setup")
            def superblock(sbi, pre, ST, TST, rstate, Trst, xrow, par_=0):
                gt0 = sbi * SBT

                def T_(name):
                    if pre:
                        name = f"{name}_p{par_}"
                        if name not in pre_tiles:
                            pre_tiles[name] = Tile(name)
                        return pre_tiles[name]
                    return Tile(name)
                if pre and par_ == 1:
                    hT_ = carve(66 * 1024, [128, 8, TSB], BF16)
                    ThT_ = [T_(f"hTalt{t}") for t in range(SBT)]
                else:
                    hT_, ThT_ = hT, ThT
                o = 0
                if pre:
                    pb_ = par_ * 33 * 1024
                    km_tok = carve(pb_, [128, SBT, 256], BF16); pb_ += SBT * 256 * 2
                    vm1 = carve(pb_, [128, SBT, 4, 130], BF16); pb_ += SBT * 4 * 130 * 2
                    rows = [carve(pb_ + i * TSB * 4, [4, TSB], F32) for i in range(4)]; pb_ += 4 * TSB * 4
                    ecol = carve(pb_, [128, SBT, 8], F32); pb_ += SBT * 8 * 4
                    gbb = carve(pb_, [128, SBT, 2], F32); pb_ += SBT * 2 * 4
                    Gm = carve(pb_, [4, SBT, 2], F32); pb_ += SBT * 2 * 4
                    gsm = carve(pb_, [4, SBT + 1], F32); pb_ += (SBT + 2) * 4
                    pre_cw0 = pb_
                    assert pb_ + 64 + 2 * 1040 + 1032 <= (par_ + 1) * 33 * 1024
                if pre:
                    _save = (km_tok, vm1, rows, ecol, gbb, Gm, gsm)
                qaT = carve(o, [128, 4, TSB], BF16); o += 4 * TSB * 2
                qmT = carve(o, [128, 2, TSB], BF16); o += 2 * TSB * 2
                kmT = carve(o, [128, 2, TSB], BF16); o += 2 * TSB * 2
                km_tok = carve(o, [128, SBT, 256], BF16); o += SBT * 256 * 2
                vm1 = carve(o, [128, SBT, 4, 130], BF16); o += SBT * 4 * 130 * 2
                og = carve(o, [128, SBT, 512], BF16); o += SBT * 512 * 2
                rows = [carve(o + i * TSB * 4, [4, TSB], F32) for i in range(4)]; o += 4 * TSB * 4
                ecol = carve(o, [128, SBT, 8], F32); o += SBT * 8 * 4
                gbb = carve(o, [128, SBT, 2], F32); o += SBT * 2 * 4
                Gm = carve(o, [4, SBT, 2], F32); o += SBT * 2 * 4
                gsm = carve(o, [4, SBT + 1], F32); o += (SBT + 2) * 4
                sg_r = [(carve(o + i * 2048, [128, 512], F32), T_(f"sgtmp{i}")) for i in range(2)]; o += 4096
                cw0 = o
                if pre:
                    km_tok, vm1, rows, ecol, gbb, Gm, gsm = _save
                    cw0 = pre_cw0
                TqaT = [T_(f"qaT{i}") for i in range(NBLK)]
                TqmT = [T_(f"qmT{i}") for i in range(NBLK)]
                TkmT = [T_(f"kmT{i}") for i in range(NBLK)]
                Tkmtok = [T_(f"kmtok{i}") for i in range(SBT)]
                Tvm1 = [T_(f"vm1{i}") for i in range(SBT)]
                Tog = [T_(f"og{i}") for i in range(SBT)]
                Trow = [T_(f"row{i}") for i in range(4)]
                Tecol = T_("ecol"); Tgbb = T_("gbb"); TGm = T_("Gm"); Tgsm = T_("gsm")

                if sbi == 0 and not pre:
                    norm_T(xs_d[0:128, :], [], g_attn, Tgattn, hTh[:], ThTh)
                    norm_T(xsm_d, [], g_attn, Tgattn, hTs[:], ThTs, tn=16)
                for t in range(SBT):
                    norm_T(xrow(gt0 + t), [], g_attn, Tgattn, hT_[:, :, t * 128:(t + 1) * 128], ThT_[t])
                    if pre:
                        yield "A"

                stage(f"A{sbi}")
                W = WStream(pre_ring[0] if pre else VRing(f"wA{sbi}_", 3, 4096))
                if not pre:
                    c_qa = W.add(8, 512, [(0, 512, w_in_v[:, :, O_QA:O_QA + 512])])
                    c_k = W.add(8, 512, [(0, 128, w_in_v[:, :, O_KA:O_KA + 128]),
                                         (128, 64, w_in_v[:, :, O_KA + 64:O_KA + 128]),
                                         (192, 64, w_in_v[:, :, O_KA:O_KA + 64]),
                                         (256, 256, w_in_v[:, :, O_QM:O_QM + 256])])
                c_m = W.add(8, 512, [(0, 256, w_in_v[:, :, O_KM:O_KM + 256]),
                                     (256, 128, w_in_v[:, :, O_VA:O_VA + 128]),
                                     (384, 8, w_in_v[:, :, O_I:O_I + 8])])
                c_vm = W.add(8, 512, [(0, 512, w_in_v[:, :, O_VM:O_VM + 512])])
                if not pre:
                    c_om = W.add(8, 512, [(0, 512, w_in_v[:, :, O_OM:O_OM + 512])])
                    c_D = []
                    for blk in range(NBLK):
                        cd = {}
                        cd["ga0"] = W.add(8, 512, [(0, 512, w_in_v[:, :, O_GA:O_GA + 512])])
                        cd["ga1"] = W.add(8, 512, [(0, 512, w_in_v[:, :, O_GA + 512:O_GA + 1024])])
                        cd["gm0"] = W.add(8, 512, [(0, 512, w_in_v[:, :, O_GM:O_GM + 512])])
                        cd["gm1"] = W.add(8, 512, [(0, 512, w_in_v[:, :, O_GM + 512:O_GM + 1024])])
                        cd["ao"] = W.add(4, 1024, [(0, 1024, w_ao_v)])
                        cd["mo"] = W.add(4, 1024, [(0, 1024, w_mo_v)])
                        cd["wo0"] = W.add(8, 512, [(0, 512, w_out_v[:, :, 0:512])])
                        cd["wo1"] = W.add(8, 512, [(0, 512, w_out_v[:, :, 512:1024])])
                        c_D.append(cd)
                WE = WStream(VRing(f"wE{sbi}_", 6, 2048))
                FG = [(g * 2, 2) for g in range(NFF // 2)]
                c_E = []
                for (f0, nf) in FG:
                    cg = WE.add(8, nf * 128, [(0, nf * 128, w_gate_v[:, :, f0 * 128:(f0 + nf) * 128])])
                    cu = WE.add(8, nf * 128, [(0, nf * 128, w_up_v[:, :, f0 * 128:(f0 + nf) * 128])])
                    cdn = WE.add(nf, 1024, [(0, 1024, w_down_v[:, f0:f0 + nf, :])])
                    c_E.append((cg, cu, cdn))

                def fm_group(wv, Tw, c0, M, rhs_list, evac):
                    for idx, (rhs, n, rds) in enumerate(rhs_list):
                        b = nextbank()
                        for k in range(8):
                            MM(ps[0:M, b, 0:n], wv[:, k, c0:c0 + M], rhs[:, k, :], k == 0, k == 7, [Tw] + rds, [PB[b]])
                        evac(idx, ps[0:M, b, 0:n], PB[b])

                def samp_proj(wv, Tw, ncols, pieces):
                    if pre or sbi != 0:
                        return
                    bb = nextbank()
                    for k in range(8):
                        MM(ps[0:16, bb, 0:ncols], hTs[:, k, :], wv[:, k, 0:ncols], k == 0, k == 7, [Tw, ThTs], [PB[bb]])
                    for (c0, n, z0) in pieces:
                        CP(zs[:, z0:z0 + n], ps[0:16, bb, c0:c0 + n], [PB[bb]], [Tzs])

                blk_rhs = [(hT_[:, :, blk * 512:(blk + 1) * 512], 512, ThT_[blk * 4:(blk + 1) * 4]) for blk in range(NBLK)]
                halo_rhs = [(hTh[:], 128, [ThTh])]

                if not pre:
                    wv, Tw = W.get(c_qa)
                    for p in range(4):
                        def ev(idx, pa, Tb, p=p):
                            ACT(qaT[:, p, idx * 512:(idx + 1) * 512], pa, AF.Copy, [Tb], [TqaT[idx]], scale=0.125)
                        fm_group(wv, Tw, p * 128, 128, blk_rhs, ev)
                    samp_proj(wv, Tw, 512, [(0, 512, 0)])
                    stage(f"Ba{sbi}")
                    wv, Tw = W.get(c_k)
                    for (c0, dst) in ((0, kaT_n), (128, kaT_s)):
                        def ev(idx, pa, Tb, dst=dst):
                            col = 128 + (gt0 * 128) + idx * 512
                            ACT(dst[:, col:col + 512], pa, AF.Copy, [Tb], TkaT[1 + gt0 + idx * 4: 1 + gt0 + idx * 4 + 4])
                        fm_group(wv, Tw, c0, 128, blk_rhs, ev)
                        if sbi == 0:
                            def evh(idx, pa, Tb, dst=dst):
                                ACT(dst[:, 0:128], pa, AF.Copy, [Tb], [TkaT[0]])
                            fm_group(wv, Tw, c0, 128, halo_rhs, evh)
                    for p in range(2):
                        def ev(idx, pa, Tb, p=p):
                            CP(qmT[:, p, idx * 512:(idx + 1) * 512], pa, [Tb], [TqmT[idx]])
                        fm_group(wv, Tw, 256 + p * 128, 128, blk_rhs, ev)
                    samp_proj(wv, Tw, 512, [(0, 128, 512), (256, 256, 768)])
                    if sbi == NSB - 1:
                        b = nextbank()
                        lt = SBT - 1
                        for k in range(8):
                            MM(ps[:, b, 0:128], hT_[:, k, lt * 128:(lt + 1) * 128], wv[:, k, 0:128], k == 0, k == 7, [Tw, ThT_[lt]], [PB[b]])
                        pko = sb("pko", [128, 128]); Tpko = Tile("pko")
                        CP(pko[:], ps[:, b, 0:128], [PB[b]], [Tpko])
                        DMA("sp", pk_d, pko[:], [Tpko], [Tile("pk_d")])
                stage(f"Bb{sbi}")
                wv, Tw = W.get(c_m)
                if not pre:
                    for p in range(2):
                        def ev(idx, pa, Tb, p=p):
                            ACT(kmT[:, p, idx * 512:(idx + 1) * 512], pa, AF.Copy, [Tb], [TkmT[idx]], scale=0.125)
                        fm_group(wv, Tw, p * 128, 128, blk_rhs, ev)
                r_ig, r_t, r_nf, r_u = rows

                def ev(idx, pa, Tb):
                    ACT(r_ig[:, idx * 512:(idx + 1) * 512], pa, AF.Identity, [Tb, Tbif], [Trow[0]], bias=b_i[:, 0:1])
                fm_group(wv, Tw, 384, 4, blk_rhs, ev)

                def ev(idx, pa, Tb):
                    ACT(r_t[:, idx * 512:(idx + 1) * 512], pa, AF.Identity, [Tb, Tbif], [Trow[1]], bias=b_f[:, 0:1])
                fm_group(wv, Tw, 388, 4, blk_rhs, ev)
                stage(f"Bc{sbi}")
                ncol = 256 if pre else 384
                for t in range(SBT):
                    b = nextbank()
                    for k in range(8):
                        MM(ps[:, b, 0:ncol], hT_[:, k, t * 128:(t + 1) * 128], wv[:, k, 0:ncol], k == 0, k == 7, [Tw, ThT_[t]], [PB[b]])
                    ACT(km_tok[:, t, :], ps[:, b, 0:256], AF.Copy, [PB[b]], [Tkmtok[t]], scale=0.125)
                    if not pre:
                        pvv = ps[:, b, 256:384].rearrange("p (g d) -> p g d", g=2)
                        gt = gt0 + t
                        CP(va[:, gt + 1, :, 0, :], pvv, [PB[b]], [Tva[gt + 1]])
                        ACT(va[:, gt + 1, :, 1, :], pvv, AF.Copy, [PB[b]], [Tva[gt + 1]])
                        if gt == NT - 1:
                            pvo = sb("pvo", [128, 128]); Tpvo = Tile("pvo")
                            CP(pvo[:], ps[:, b, 256:384], [PB[b]], [Tpvo])
                            DMA("sp", pv_d, pvo[:], [Tpvo], [Tile("pv_d")])
                if sbi == 0 and not pre:
                    b = nextbank()
                    for k in range(8):
                        MM(ps[:, b, 0:128], hTh[:, k, :], wv[:, k, 256:384], k == 0, k == 7, [Tw, ThTh], [PB[b]])
                    pvv = ps[:, b, 0:128].rearrange("p (g d) -> p g d", g=2)
                    CP(va[:, 0, :, 0, :], pvv, [PB[b]], [Tva[0]])
                    ACT(va[:, 0, :, 1, :], pvv, AF.Copy, [PB[b]], [Tva[0]])
                samp_proj(wv, Tw, 392, [(0, 256, 1024), (256, 128, 640), (384, 8, 2304)])
                stage(f"Bd{sbi}")
                if (not pre) or (par_ not in pre_ones_done):
                    if pre:
                        pre_ones_done.add(par_)
                    for t in range(SBT):
                        S.op("dve", lambda e, t=t: e.memset(vm1[:, t, :, 128:129], 1.0), [], [Tvm1[t]])
                wv, Tw = W.get(c_vm)
                for t in range(SBT):
                    b = nextbank()
                    for k in range(8):
                        MM(ps[:, b, 0:512], hT_[:, k, t * 128:(t + 1) * 128], wv[:, k, 0:512], k == 0, k == 7, [Tw, ThT_[t]], [PB[b]])
                    ACT(vm1[:, t, :, 0:128], ps[:, b, 0:512].rearrange("p (h v) -> p h v", h=4), AF.Copy, [PB[b]], [Tvm1[t]])
                samp_proj(wv, Tw, 512, [(0, 512, 1280)])
                stage(f"Be{sbi}")
                if not pre:
                    wv, Tw = W.get(c_om)
                    for t in range(SBT):
                        b = nextbank()
                        for k in range(8):
                            MM(ps[:, b, 0:512], hT_[:, k, t * 128:(t + 1) * 128], wv[:, k, 0:512], k == 0, k == 7, [Tw, ThT_[t]], [PB[b]])
                        tmp, Ttmp = sg_r[t % 2]
                        ACT(tmp[:], ps[:, b, 0:512], AF.Sigmoid, [PB[b]], [Ttmp])
                        TT(og[:, t, :], tmp[:], g_head[:], ALU.mult, [Ttmp, Tghead], [Tog[t]])
                    samp_proj(wv, Tw, 512, [(0, 512, 1792)])

                stage(f"B{sbi}")
                if pre:
                    yield "front"
                ACT(r_t[:], r_t[:], AF.Exp, [Trow[1]], [Trow[1]], scale=-1.0)
                ACT(r_t[:], r_t[:], AF.Ln, [Trow[1]], [Trow[1]], bias=1.0)
                S.op("dve", lambda e: e.tensor_tensor_scan(r_nf[:], one_c[0:4, 0:1].to_broadcast([4, TSB]), r_t[:], rstate[:, 0:1], ALU.mult, ALU.add),
                     [Tonesr, Trow[1], Trst], [Trow[2]])
                dbg(f"nl{sbi}", r_t[:], Trow[1], [4, TSB])
                dbg(f"NF{sbi}", r_nf[:], Trow[2], [4, TSB])
                dbg(f"ig{sbi}", r_ig[:], Trow[0], [4, TSB])
                TT(r_ig[:], r_ig[:], r_nf[:], ALU.add, [Trow[0], Trow[2]], [Trow[0]])
                S.op("dve", lambda e: e.tensor_tensor_scan(r_t[:], r_ig[:], r_ig[:], rstate[:, 1:2], ALU.max, ALU.max),
                     [Trow[0], Trst, Trow[1]], [Trow[1]])
                CP(gsm[:, 0:1], rstate[:, 1:2], [Trst], [Tgsm])
                rt_v = r_t[:].rearrange("p (t c) -> p t c", c=128)
                CP(gsm[:, 1:SBT + 1], rt_v[:, :, 127], [Trow[1]], [Tgsm])
                CP(r_u[:].rearrange("p (t c) -> p t c", c=128), gsm[:, 1:SBT + 1].unsqueeze(2).to_broadcast([4, SBT, 128]), [Tgsm], [Trow[3]])
                CP(rstate[:, 0:1], r_nf[:, TSB - 1:TSB], [Trow[2], Tgsm], [Trst])
                CP(rstate[:, 1:2], r_t[:, TSB - 1:TSB], [Trow[1]], [Trst])
                TT(r_ig[:], r_ig[:], r_u[:], ALU.subtract, [Trow[0], Trow[3]], [Trow[0]])
                ACT(r_ig[:], r_ig[:], AF.Exp, [Trow[0]], [Trow[0]])
                TT(r_nf[:], r_nf[:], r_u[:], ALU.subtract, [Trow[2], Trow[3]], [Trow[2]])
                ACT(r_nf[:], r_nf[:], AF.Exp, [Trow[2]], [Trow[2]])
                gexp = carve(cw0, [4, SBT], F32)
                Tgexp = T_("gexp")
                TT(gexp[:], gsm[:, 0:SBT], gsm[:, 1:SBT + 1], ALU.subtract, [Tgsm], [Tgexp])
                ACT(gexp[:], gexp[:], AF.Exp, [Tgexp], [Tgexp])
                TT(Gm[:], gexp[:].unsqueeze(2).to_broadcast([4, SBT, 2]), pmm[:].unsqueeze(1).to_broadcast([4, SBT, 2]), ALU.mult, [Tgexp, Tsel], [TGm])
                b = nextbank()
                MM(ps[:, b, 0:SBT * 2], selm[:], Gm[:].rearrange("p t c -> p (t c)"), True, True, [Tsel, TGm], [PB[b]])
                CP(gbb[:].rearrange("p t c -> p (t c)"), ps[:, b, 0:SBT * 2], [PB[b]], [Tgbb])
                b = nextbank()
                for t in range(SBT):
                    TR(ps[:, b, t * 8:t * 8 + 4], r_ig[:, t * 128:(t + 1) * 128], ident_f[0:4, 0:4], [Trow[0], Tidf], [PB[b]])
                    TR(ps[:, b, t * 8 + 4:t * 8 + 8], r_nf[:, t * 128:(t + 1) * 128], ident_f[0:4, 0:4], [Trow[2], Tidf], [PB[b]])
                CP(ecol[:].rearrange("p t c -> p (t c)"), ps[:, b, 0:SBT * 8], [PB[b]], [Tecol])
                dbg(f"ecol{sbi}", ecol[:], Tecol, [128, SBT, 8])
                dbg(f"gbb{sbi}", gbb[:], Tgbb, [128, SBT, 2])
                dbg(f"rows{sbi}", r_t[:], Trow[1], [4, TSB])
                dbg(f"rowA{sbi}", r_ig[:], Trow[0], [4, TSB])

                stage(f"R{sbi}")
                if pre:
                    yield "rows"
                o = cw0 + 64
                if pre:
                    V2_pre = [(carve(o + i * 1040, [128, 4, 130], BF16), T_(f"V2{i}")) for i in range(2)]
                    STgf_pre = carve(o + 2080, [128, 2, 129], F32)
                    o = 0
                Sb_both = carve(o, [128, 2, 512], F32)
                Sb_r = [(carve(o + i * 2048, [128, 512], F32), T_(f"Sb{i}")) for i in range(2)]; o += 4096
                ET_base = o
                ET_r = [(carve(o + i * 1024, [128, 512], BF16), T_(f"ET{i}")) for i in range(8)]; o += 8192
                rden_r = [(carve(o + i * 2048, [128, 512], F32), T_(f"rden{i}")) for i in range(2)]; o += 4096
                Wt_r = [(carve(o + i * 1024, [128, 4, 128], BF16), T_(f"Wt{i}")) for i in range(2)]; o += 2048
                V2_r = [(carve(o + i * 1040, [128, 4, 130], BF16), T_(f"V2{i}")) for i in range(2)]; o += 2080
                ym_r = [(carve(o + i * 1024, [128, 512], BF16), T_(f"ym{i}")) for i in range(2)]; o += 2048
                STg_f = carve(o, [128, 2, 129], F32); o += 2 * 129 * 4
                STg_b = carve(o, [128, 2, 130], BF16); o += 2 * 130 * 2
                o = (o + 3) // 4 * 4
                sj = carve(o, [128, 128], BF16); o += 256
                TSTgf = T_("STgf"); TSTgb = T_("STgb"); Tsj = T_("sj")
                assert o <= ARENA_BYTES, o
                if pre:
                    V2_r = V2_pre
                    STg_f = STgf_pre
                et_i = [0]
                for t in range(SBT):
                    gt = gt0 + t
                    tok = slice(t * 128, (t + 1) * 128)
                    sbk = (t % 2) * 2
                    if not pre:
                        ETs = {}
                        for g in range(2):
                            for c in range(2):
                                kt = gt + c
                                kcol = slice(kt * 128, (kt + 1) * 128)
                                for par in range(2):
                                    kk = kaT_n if (g == par) else kaT_s
                                    pr = slice(par * 64, (par + 1) * 64)
                                    outp = ps[:, sbk + par, c * 256:(c + 1) * 256].rearrange("p (j q) -> p j q", j=2)
                                    MM(outp, kk[pr, kcol], qaT[pr, 2 * g:2 * g + 2, tok], True, True, [TkaT[kt], TqaT[t // 4]], [PB[sbk + par]])
                            TT(Sb_both[:], ps[:, sbk:sbk + 2, :], BTall[:, g, :, :, :, :].rearrange("p r c j q -> p r (c j q)"), ALU.add,
                               [PB[sbk], PB[sbk + 1], TBT], [Sb_r[0][1], Sb_r[1][1]])
                            e0 = et_i[0] % 8
                            et_i[0] += 2
                            if gt == 0:
                                for par in range(2):
                                    Sb, TSb = Sb_r[par]
                                    ET, TET = ET_r[e0 + par]
                                    ACT(ET[:, 0:256], Sb[:, 0:256], AF.Exp, [TSb, Tflag], [TET], bias=flag[:, 0:1])
                                    ACT(ET[:, 256:512], Sb[:, 256:512], AF.Exp, [TSb], [TET])
                            else:
                                ET2 = carve(ET_base + e0 * 1024, [128, 2, 512], BF16)
                                ACT(ET2[:], Sb_both[:], AF.Exp, [Sb_r[0][1], Sb_r[1][1]], [ET_r[e0][1], ET_r[e0 + 1][1]])
                            for par in range(2):
                                ETs[(g, par)] = ET_r[e0 + par]
                        for g in range(2):
                            bY, bD = 4 + g, 6 + g
                            for par in range(2):
                                ET, TET = ETs[(g, par)]
                                yv = ps[:, bY, :].rearrange("p (j q) -> p j q", j=4)[:, par::2, :]
                                dv = ps[:, bD, :].rearrange("p (j q) -> p j q", j=4)[:, par::2, :]
                                for c in range(2):
                                    kt = gt + c
                                    MM(yv, va[:, kt, g, :, :].rearrange("p a d -> p (a d)"), ET[:, c * 256:(c + 1) * 256].rearrange("p (j q) -> p j q", j=2),
                                       c == 0, c == 1, [Tva[kt], TET], [PB[bY]])
                                for c in range(2):
                                    MM(dv, ones_bf[:], ET[:, c * 256:(c + 1) * 256].rearrange("p (j q) -> p j q", j=2), c == 0, False, [Tones, TET], [PB[bD]])
                                MM(dv, ones_bf[0:1, :], esink[0:1, 4 * g + par:4 * g + 4:2, :], False, True, [Tones, Tesink], [PB[bD]])
                            rden, Trden = rden_r[g]
                            ACT(rden[:], ps[:, bD, :], AF.Ln, [PB[bD]], [Trden])
                            ACT(rden[:], rden[:], AF.Exp, [Trden], [Trden], scale=-1.0)
                            for par in range(2):
                                pr = slice(par * 64, (par + 1) * 64)
                                yv = ps[pr, bY, :].rearrange("p (j q) -> p j q", j=4)[:, par::2, :]
                                rv = rden[pr, :].rearrange("p (j q) -> p j q", j=4)[:, par::2, :]
                                TT(yattT[pr, 2 * g:2 * g + 2, tok], yv, rv, ALU.mult, [PB[bY], Trden], [TyaT[t]])
                for t in range(SBT):
                    gt = gt0 + t
                    tok = slice(t * 128, (t + 1) * 128)
                    bMS, bHO, bSU, bTR = (0, 2, 0, 1) if t % 2 == 0 else (4, 6, 4, 5)
                    if not pre:
                        for h in range(4):
                            p, par = h // 2, h % 2
                            pr = slice(par * 64, (par + 1) * 64)
                            MM(ps[:, bMS + par, p * 128:(p + 1) * 128], kmT[pr, p, tok], qmT[pr, p, tok], True, True, [TkmT[t // 4], TqmT[t // 4]], [PB[bMS + par]])
                        Wt, TWt = Wt_r[t % 2]
                        TT(Wt[:].rearrange("s (p r) q -> s r p q", r=2), ps[:, bMS:bMS + 2, 0:256].rearrange("s b (p q) -> s b p q", p=2),
                           caus[:].unsqueeze(1).unsqueeze(1).to_broadcast([128, 2, 2, 128]), ALU.mult, [PB[bMS], PB[bMS + 1], Tcaus], [TWt])
                    V2, TV2 = V2_r[t % 2]
                    TT(V2[:, :, 0:129], vm1[:, t, :, 0:129], ecol[:, t, 0:4].unsqueeze(2).to_broadcast([128, 4, 129]), ALU.mult, [Tvm1[t], Tecol], [TV2])
                    TT(STg_f[:], ST[:], gbb[:, t, :].unsqueeze(2).to_broadcast([128, 2, 129]), ALU.mult, [TST, Tgbb], [TSTgf])
                    if not pre:
                        ACT(STg_b[:, :, 0:129], STg_f[:], AF.Copy, [TSTgf], [TSTgb])
                        for h in range(4):
                            p, par = h // 2, h % 2
                            pr = slice(par * 64, (par + 1) * 64)
                            bb = bHO + h // 2
                            cc = (h % 2) * 256
                            MM(ps[:, bb, cc:cc + 129], Wt[:, h, :], V2[:, h, 0:129], True, False, [TWt, TV2], [PB[bb]])
                            MM(ps[:, bb, cc:cc + 129], qmT[pr, p, tok], STg_b[pr, p, 0:129], False, True, [TqmT[t // 4], TSTgb], [PB[bb]])
                    for p in range(2):
                        for par in range(2):
                            h = 2 * p + par
                            cc = par * 256
                            MM(ps[:, bSU + p, cc:cc + 129], km_tok[:, t, p * 128:(p + 1) * 128], V2[:, h, 0:129], True, True, [Tkmtok[t], TV2], [PB[bSU + p]])
                    for par in range(2):
                        pr = slice(par * 64, (par + 1) * 64)
                        cc = par * 256
                        TT(ST[pr, :, :], STg_f[pr, :, :], ps[pr, bSU:bSU + 2, cc:cc + 129], ALU.add, [TSTgf, PB[bSU], PB[bSU + 1]], [TST])
                    if pre:
                        yield "C"
                    if not pre:
                        HOv = ps[:, bHO:bHO + 2, :].rearrange("p b (c x) -> p (b c) x", c=2)
                        d4, Td4 = d4ring.next()
                        CP(d4[:, 0:4], HOv[:, :, 128], [PB[bHO], PB[bHO + 1]], [Td4])
                        STT(d4[:, 0:4], d4[:, 0:4], -1.0, d4[:, 0:4], ALU.mult, ALU.max, [Td4], [Td4])
                        TT(d4[:, 0:4], d4[:, 0:4], ecol[:, t, 4:8], ALU.max, [Td4, Tecol], [Td4])
                        S.op("dve", lambda e, d4=d4: e.reciprocal(d4[:, 0:4], d4[:, 0:4]), [Td4], [Td4])
                        for h in range(4):
                            ACT(sj[:], HOv[:, h, 0:128], AF.Square, [PB[bHO], PB[bHO + 1], Td4], [Tsj, Td4], scale=d4[:, h:h + 1], accum_out=d4[:, 4 + h:5 + h])
                        ACT(d4[:, 8:12], d4[:, 4:8], AF.Ln, [Td4, Teps], [Td4], scale=1.0 / 128, bias=eps_c[:, 0:1])
                        ACT(d4[:, 8:12], d4[:, 8:12], AF.Exp, [Td4], [Td4], scale=-0.5)
                        TT(d4[:, 8:12], d4[:, 8:12], d4[:, 0:4], ALU.mult, [Td4], [Td4])
                        ym, Tym = ym_r[t % 2]
                        for h in range(4):
                            STT(ym[:, h * 128:(h + 1) * 128], HOv[:, h, 0:128], d4[:, 8 + h:9 + h], og[:, t, h * 128:(h + 1) * 128], ALU.mult, ALU.mult,
                                [PB[bHO], PB[bHO + 1], Td4, Tog[t]], [Tym])
                        pvb = ps_bf(bTR)
                        for h in range(4):
                            TR(pvb[:, h * 128:(h + 1) * 128], ym[:, h * 128:(h + 1) * 128], ident_bf[:], [Tym, Tidb], [PB[bTR]])
                        ACT(ymT[:, :, tok], pvb[:, 0:512].rearrange("p (h q) -> p h q", h=4), AF.Copy, [PB[bTR]], [TymT[t]])
                dbg(f"yattT{sbi}", yattT[:], TyaT[SBT - 1], [128, 4, TSB], BF16)
                dbg(f"ymT{sbi}", ymT[:], TymT[SBT - 1], [128, 4, TSB], BF16)

                if pre:
                    return
                if sbi == 0:
                    sample_phase()
                stage(f"C{sbi}")
                S.barrier()
                samp = (sbi == 0)
                o = 0
                x1 = carve(o, [128, SBT, D], F32); o += SBT * D * 4
                Tx1 = [Tile(f"x1_{t}") for t in range(SBT)]
                x1s = carve(o, [16, D], F32); o += D * 4
                Tx1s = Tile("x1s")
                sga = carve(o, [128, 8, 512], BF16); o += 8192
                sgm = carve(o, [128, 8, 512], BF16); o += 8192
                mixT = carve(o, [128, 8, 512], BF16); o += 8192
                tmpD = [(carve(o + i * 2048, [128, 512], F32), Tile("tmpD")) for i in range(2)]; o += 4096
                sga_s = carve(o, [128, 8, 16], BF16); o += 256
                sgm_s = carve(o, [128, 8, 16], BF16); o += 256
                mix_s = carve(o, [128, 8, 16], BF16); o += 256
                aT_s = carve(o, [128, 2, 16], BF16); o += 64
                e_off = o
                assert o <= ARENA_BYTES
                Tsga = [Tile("sga") for _ in range(8)]
                Tsgm = [Tile("sgm") for _ in range(8)]
                Tmix = [Tile("mix") for _ in range(8)]
                Tsga_s = [Tile("sga_s") for _ in range(8)]
                Tsgm_s = [Tile("sgm_s") for _ in range(8)]
                Tmix_s = [Tile("mix_s") for _ in range(8)]
                TaT_s = Tile("aT_s")

                def subblocks(blk):
                    btok = slice(blk * 512, (blk + 1) * 512)
                    subs = [dict(n=512, hv=hT_[:, :, btok], hrd=ThT_[blk * 4:(blk + 1) * 4],
                                 yav=yattT[:, :, btok], yard=TyaT[blk * 4:(blk + 1) * 4],
                                 ymv=ymT[:, :, btok], ymrd=TymT[blk * 4:(blk + 1) * 4],
                                 sga=sga, sgm=sgm, mix=mixT, Tsga=Tsga, Tsgm=Tsgm, Tmix=Tmix,
                                 tiles=[(x1[:, blk * 4 + tt, :], Tx1[blk * 4 + tt], 128, slice(tt * 128, (tt + 1) * 128)) for tt in range(4)])]
                    if samp and blk == 0:
                        subs.append(dict(n=16, hv=hTs[:], hrd=[ThTs], yav=yattTs[:], yard=[TyaTs], ymv=ymTs[:], ymrd=[TymTs],
                                         sga=sga_s, sgm=sgm_s, mix=mix_s, Tsga=Tsga_s, Tsgm=Tsgm_s, Tmix=Tmix_s,
                                         tiles=[(x1s[:], Tx1s, 16, slice(0, 16))]))
                    return subs

                if samp:
                    DMA("sp", x1s[:], xsm_d, [], [Tx1s])
                for blk in range(NBLK):
                    cd = c_D[blk]
                    subs = subblocks(blk)
                    for tt in range(4):
                        t = blk * 4 + tt
                        r0 = 128 + (gt0 + t) * 128
                        DMA("sp", x1[:, t, :], xs_d[r0:r0 + 128, :], [], [Tx1[t]])
                    for nm in ("ga", "gm"):
                        for half in range(2):
                            wv, Tw = W.get(cd[f"{nm}{half}"])
                            for jj in range(4):
                                j = half * 4 + jj
                                for sub in subs:
                                    n = sub["n"]
                                    dst, Td = (sub["sga"], sub["Tsga"]) if nm == "ga" else (sub["sgm"], sub["Tsgm"])
                                    bb = nextbank()
                                    for k in range(8):
                                        MM(ps[:, bb, 0:n], wv[:, k, jj * 128:(jj + 1) * 128], sub["hv"][:, k, :], k == 0, k == 7, [Tw] + sub["hrd"], [PB[bb]])
                                    ACT(dst[:, j, 0:n], ps[:, bb, 0:n], AF.Sigmoid, [PB[bb]], [Td[j]])
                    wva, Twa = W.get(cd["ao"])
                    wvm, Twm = W.get(cd["mo"], live_from=cd["ao"])
                    for j in range(8):
                        for sub in subs:
                            n = sub["n"]
                            bb = nextbank()
                            for k in range(4):
                                MM(ps[:, bb, 0:n], wva[:, k, j * 128:(j + 1) * 128], sub["yav"][:, k, :], k == 0, k == 3, [Twa] + sub["yard"], [PB[bb]])
                            tmp, Ttmp = tmpD[j % 2]
                            TT(tmp[:, 0:n], ps[:, bb, 0:n], sub["sga"][:, j, 0:n], ALU.mult, [PB[bb], sub["Tsga"][j]], [Ttmp])
                            b2 = nextbank()
                            for k in range(4):
                                MM(ps[:, b2, 0:n], wvm[:, k, j * 128:(j + 1) * 128], sub["ymv"][:, k, :], k == 0, k == 3, [Twm] + sub["ymrd"], [PB[b2]])
                            TT(sub["mix"][:, j, 0:n], ps[:, b2, 0:n], sub["sgm"][:, j, 0:n], ALU.mult, [PB[b2], sub["Tsgm"][j]], [sub["Tmix"][j]])
                            TT(sub["mix"][:, j, 0:n], sub["mix"][:, j, 0:n], tmp[:, 0:n], ALU.add, [sub["Tmix"][j], Ttmp], [sub["Tmix"][j]])
                    for half in range(2):
                        wv, Tw = W.get(cd[f"wo{half}"])
                        for sub in subs:
                            for (xa, Txa, tn, tsl) in sub["tiles"]:
                                bb = nextbank()
                                for j in range(8):
                                    MM(ps[0:tn, bb, :], sub["mix"][:, j, tsl], wv[:, j, :], j == 0, j == 7, [Tw, sub["Tmix"][j]], [PB[bb]])
                                TT(xa[:, half * 512:(half + 1) * 512], xa[:, half * 512:(half + 1) * 512], ps[0:tn, bb, :], ALU.add, [PB[bb], Txa], [Txa])
                dbg(f"x1_{sbi}", x1[:], Tx1[SBT - 1], [128, SBT, D])
                dbg("x1s", x1s[:], Tx1s, [16, D])

                stage(f"D{sbi}")
                o = e_off
                actT = [(carve(o + i * 2048, [128, 2, 512], BF16), Tile("actT")) for i in range(2)]; o += 4096
                sil = [(carve(o + i * 2048, [128, 512], F32), Tile("sil")) for i in range(2)]; o += 4096
                yo = [(carve(o + i * 4096, [128, D], F32), Tile("yo")) for i in range(2)]; o += 8192
                Tx1b = [Tile(f"x1b_{t}") for t in range(SBT)]
                assert o <= ARENA_BYTES, o
                for t in range(SBT):
                    norm_T(x1[:, t, :], [Tx1[t]], g_ffn, Tgffn, hT_[:, :, t * 128:(t + 1) * 128], ThT_[t], src_is_dram=False)
                if samp:
                    norm_T(x1s[:], [Tx1s], g_ffn, Tgffn, hTs[:], ThTs, tn=16, src_is_dram=False)
                for gi, (f0, nf) in enumerate(FG):
                    cg, cu, cdn = c_E[gi]
                    wg, Twg = WE.get(cg)
                    wu, Twu = WE.get(cu, live_from=cg)
                    wd, Twd = WE.get(cdn, live_from=cg)
                    for blk in range(NBLK):
                        for si, sub in enumerate(subblocks(blk)):
                            n = sub["n"]
                            if si == 0:
                                aT, TaT = actT[(gi * NBLK + blk) % 2]
                            else:
                                aT, TaT = aT_s, TaT_s
                            for c in range(nf):
                                bb = nextbank()
                                for k in range(8):
                                    MM(ps[:, bb, 0:n], wg[:, k, c * 128:(c + 1) * 128], sub["hv"][:, k, :], k == 0, k == 7, [Twg] + sub["hrd"], [PB[bb]])
                                b2 = nextbank()
                                for k in range(8):
                                    MM(ps[:, b2, 0:n], wu[:, k, c * 128:(c + 1) * 128], sub["hv"][:, k, :], k == 0, k == 7, [Twu] + sub["hrd"], [PB[b2]])
                                sl, Tsl = sil[c % 2]
                                ACT(sl[:, 0:n], ps[:, bb, 0:n], AF.Silu, [PB[bb]], [Tsl])
                                TT(aT[:, c, 0:n], sl[:, 0:n], ps[:, b2, 0:n], ALU.mult, [Tsl, PB[b2]], [TaT])
                            for ti_, (xa, Txa, tn, tsl) in enumerate(sub["tiles"]):
                                for half in range(2):
                                    bb = nextbank()
                                    for c in range(nf):
                                        MM(ps[0:tn, bb, :], aT[:, c, tsl], wd[:, c, half * 512:(half + 1) * 512], c == 0, c == nf - 1, [TaT, Twd], [PB[bb]])
                                    if True:
                                        TT(xa[:, half * 512:(half + 1) * 512], xa[:, half * 512:(half + 1) * 512], ps[0:tn, bb, :], ALU.add, [PB[bb], Txa], [Txa])
                stage(f"E{sbi}")
                fin = [(x1[:, t, :], [Tx1[t], Tx1b[t]], 128, y_d[(gt0 + t) * 128:(gt0 + t + 1) * 128, :]) for t in range(SBT)]
                if samp:
                    fin.append((x1s[:], [Tx1s], 16, ys_d))
                for fi, (xa, Txa, tn, dst) in enumerate(fin):
                    yb, Tyb = yo[fi % 2]
                    ss, Tss = newscal()
                    ACT(yb[0:tn, :], xa, AF.Square, Txa, [Tyb, Tss], accum_out=ss[0:tn, :])
                    rr, Trr = newscal()
                    ACT(rr[0:tn, :], ss[0:tn, :], AF.Ln, [Tss, Teps], [Trr], scale=1.0 / D, bias=eps_c[0:tn, 0:1])
                    ACT(rr[0:tn, :], rr[0:tn, :], AF.Exp, [Trr], [Trr], scale=-0.5)
                    STT(yb[0:tn, :], xa, rr[0:tn, 0:1], g_fin[0:tn, :], ALU.mult, ALU.mult, Txa + [Trr, Tgfin], [Tyb])
                    DMA("sp", dst, yb[0:tn, :], [Tyb], [Tile("y_d")])
                S.barrier()

            pact = sb("pact", [128, 4]); Tpact = Tile("pact")
            pre_ring[0] = VRing("wP_", 3, 4096)
            DMA("sp", pact[:], pact_d, [], [Tpact])
            in_pre[0] = True

            def boundary(j):
                TT(rstate[:, 1:2], rstate[:, 1:2], rstate[:, 0:1], ALU.subtract, [Trst], [Trst])
                TS(rstate[:, 1:2], rstate[:, 1:2], pact[0:4, j:j + 1], None, ALU.mult, ALU.bypass, [Trst, Tpact], [Trst])
                S.op("dve", lambda e: e.memset(rstate[:, 0:1], 0.0), [], [Trst])
                TS(ST[:], ST[:], pact[:, j:j + 1], None, ALU.mult, ALU.bypass, [TST, Tpact], [TST])

            gens = []
            for j in range(3):
                for sbi in range(NSB):
                    k = j * NSB + sbi
                    gens.append(superblock(sbi, True, ST, TST, rstate, Trst,
                                           lambda gt, j=j: xprev_d[(j * NT + gt) * 128:(j * NT + gt + 1) * 128, :], par_=k % 2))
            def run_until(g, tag):
                for x in g:
                    if x == tag:
                        return True
                return False

            run_until(gens[0], "front")
            for k in range(len(gens)):
                gk = gens[k]
                gn = gens[k + 1] if k + 1 < len(gens) else None
                run_until(gk, "rows")
                for t in range(SBT):
                    if gn is not None:
                        run_until(gn, "A")
                    run_until(gk, "C")
                for _ in gk:
                    pass
                if gn is not None:
                    run_until(gn, "front")
                if k % NSB == NSB - 1:
                    boundary(k // NSB)
            in_pre[0] = False
            S.barrier()
            dbg("STpre", ST[:], TST, [128, 2, 129])
            dbg("rst", rstate[:], Trst, [4, 4])
            stage("pre")
            for sbi in range(NSB):
                for _ in superblock(sbi, False, ST, TST, rstate, Trst, lambda gt: xs_d[128 + gt * 128:128 + (gt + 1) * 128, :]):
                    pass

            Cout = sb("Cout", [128, 4, 64]); TCout = Tile("Cout")
            for h in range(4):
                p, par = h // 2, h % 2
                pr = slice(par * 64, (par + 1) * 64)
                TR(ps[:, par, p * 64:(p + 1) * 64], ST[pr, p, 0:128], ident_f[pr, pr], [TST, Tidf], [PB[par]])
            for par in range(2):
                CP(Cout[:, par::2, :], ps[:, par, 0:128].rearrange("v (p d) -> v p d", p=2), [PB[par]], [TCout])
            DMA("sp", pC_d.rearrange("h v d -> v h d"), Cout[:], [TCout], [Tile("pC_d")])
            for h in range(4):
                p, par = h // 2, h % 2
                pr = slice(par * 64, (par + 1) * 64)
                DMA("sp", AP(pn_d.tensor, h * 64, [[1, 64], [1, 1]]), ST[pr, p, 128:129], [TST], [Tile("pn_d")])
            mo = sb("mo", [4, 1]); Tmo = Tile("mo")
            TT(mo[:], rstate[:, 1:2], rstate[:, 0:1], ALU.subtract, [Trst], [Tmo])
            DMA("sp", pm_out_d, mo[:], [Tmo], [Tile("pm_d")])

        except _Stop:
            pass
        S.finish()
        S.emit()
    return nc, dbg_outs


_CACHE = {}


def _consts():
    ident = np.eye(128, dtype=np.float32)
    dist_rev = 127 - np.arange(128)
    bk = t5_bucket_np(dist_rev)
    ohT_rev = (np.arange(32)[:, None] == bk[None, :]).astype(np.float32)
    causT = (np.arange(128)[:, None] <= np.arange(128)[None, :]).astype(np.float32)
    sel = np.zeros((4, 128), np.float32)
    for h in range(4):
        sel[h, (h % 2) * 64:(h % 2) * 64 + 64] = 1.0
    pm = np.zeros((4, 2), np.float32)
    for h in range(4):
        pm[h, h // 2] = 1.0
    return dict(ident_bf=ident.astype(ml_dtypes.bfloat16), ident_f=ident, ohT_rev=ohT_rev, causT=causT, sel=sel, pm=pm)


def kernel(x_prompt, x_sample, cache_k_win, cache_v_win, state_mlstm_C, state_mlstm_n, state_mlstm_m,
           rel_bias, w_in, b_if, sinks, g_attn_norm, g_head, w_att_out, w_mlstm_out, w_out,
           g_ffn_norm, w_gate, w_up, w_down, g_final, _debug=(), _stop=None, _trace=False):
    f32 = np.float32
    x_prompt = np.asarray(x_prompt, f32)
    x_sample = np.asarray(x_sample, f32)
    cache_k_win = np.asarray(cache_k_win, f32)
    cache_v_win = np.asarray(cache_v_win, f32)
    state_mlstm_C = np.asarray(state_mlstm_C, f32)
    state_mlstm_n = np.asarray(state_mlstm_n, f32)
    state_mlstm_m = np.asarray(state_mlstm_m, f32)
    key = (tuple(_debug), _stop)
    if key not in _CACHE:
        _CACHE[key] = build_program(debug=_debug, stop=_stop)
    nc, dbg_outs = _CACHE[key]
    cst = _consts()
    shared = dict(
        w_in=np.ascontiguousarray(np.asarray(w_in, f32)[0]),
        b_if=np.ascontiguousarray(np.asarray(b_if, f32)[0]),
        sinks=np.ascontiguousarray(np.asarray(sinks, f32)),
        rel_bias=np.ascontiguousarray(np.asarray(rel_bias, f32)),
        g_attn=np.ascontiguousarray(np.asarray(g_attn_norm, f32)),
        g_head=np.ascontiguousarray(np.asarray(g_head, f32)),
        g_ffn=np.ascontiguousarray(np.asarray(g_ffn_norm, f32)),
        g_final=np.ascontiguousarray(np.asarray(g_final, f32)[None]),
        w_att_out=np.ascontiguousarray(np.asarray(w_att_out, f32)[0]),
        w_mlstm_out=np.ascontiguousarray(np.asarray(w_mlstm_out, f32)[0]),
        w_out=np.ascontiguousarray(np.asarray(w_out, f32)[0]),
        w_gate=np.ascontiguousarray(np.asarray(w_gate, f32)[0]),
        w_up=np.ascontiguousarray(np.asarray(w_up, f32)[0]),
        w_down=np.ascontiguousarray(np.asarray(w_down, f32)[0]),
        **cst,
    )
    in_maps = []
    for c in range(NCORE):
        b, s = c // 4, c % 4
        xs = np.zeros((128 + SEG, D), f32)
        xs[128:] = x_prompt[b, s * SEG:(s + 1) * SEG]
        if s > 0:
            xs[:128] = x_prompt[b, s * SEG - 128:s * SEG]
        flag = np.full((128, 1), NEGB if s == 0 else 0.0, f32)
        m = dict(shared)
        m["xs"] = xs
        m["flag"] = flag
        xprev = np.zeros((3 * SEG, D), f32)
        pact = np.zeros((128, 4), f32)
        for j in range(3):
            sj = s - 3 + j
            if sj >= 0:
                xprev[j * SEG:(j + 1) * SEG] = x_prompt[b, sj * SEG:(sj + 1) * SEG]
                pact[:, j] = 1.0
        m["xprev"] = xprev
        m["pact"] = pact
        sl = slice(c * 16, (c + 1) * 16)
        m["xsm"] = np.ascontiguousarray(x_sample[sl, 0, :])
        m["ck"] = np.ascontiguousarray(cache_k_win[0, sl].reshape(16, 128, 128))
        m["cv"] = np.ascontiguousarray(cache_v_win[0, sl].reshape(16, 128, 128))
        m["sC"] = np.ascontiguousarray(state_mlstm_C[0, sl])
        m["sn"] = np.ascontiguousarray(state_mlstm_n[0, sl])
        m["sm"] = np.ascontiguousarray(state_mlstm_m[0, sl])
        in_maps.append(m)
    res = run_bass_kernel_spmd(nc, in_maps, core_ids=list(range(NCORE)), **({'trace': True} if _trace else {}))
    if _trace:
        print('EXEC_TIME_NS', res.exec_time_ns)
    R = res.results
    y_prompt = np.stack([np.concatenate([R[b * 4 + s]["y"] for s in range(4)], axis=0) for b in range(2)])
    p_k = np.stack([R[b * 4 + 3]["pk"].reshape(128, 2, 64) for b in range(2)])[None]
    p_v = np.stack([R[b * 4 + 3]["pv"].reshape(128, 2, 64) for b in range(2)])[None]
    p_C = np.stack([R[b * 4 + 3]["pC"] for b in range(2)])[None]
    p_n = np.stack([R[b * 4 + 3]["pn"] for b in range(2)])[None]
    p_m = np.stack([R[b * 4 + 3]["pm_out"].reshape(4) for b in range(2)])[None]
    y_sample = np.concatenate([R[c]["ys"] for c in range(NCORE)], axis=0)[:, None, :]
    s_k = np.concatenate([R[c]["sko"].reshape(16, 128, 2, 64) for c in range(NCORE)], axis=0)[None]
    s_v = np.concatenate([R[c]["svo"].reshape(16, 128, 2, 64) for c in range(NCORE)], axis=0)[None]
    s_C = np.concatenate([R[c]["sCo"] for c in range(NCORE)], axis=0)[None]
    s_n = np.concatenate([R[c]["sno"][:, :, 0, :] for c in range(NCORE)], axis=0)[None]
    s_m = np.concatenate([R[c]["smo"][:, :, 0] for c in range(NCORE)], axis=0)[None]
    outs = (y_prompt, y_sample, p_k, p_v, p_C, p_n, p_m, s_k, s_v, s_C, s_n, s_m)
    if _debug:
        return outs, [{k: r["dbg_" + k] for k in dbg_outs} for r in R]
    return outs
```

```python
import contextlib
import math
import numpy as np
import ml_dtypes
import concourse.bass as bass
import concourse.mybir as mybir
from concourse.ap import AP
from concourse.bass_utils import run_bass_kernel_spmd

F32 = mybir.dt.float32
BF16 = mybir.dt.bfloat16
AF = mybir.ActivationFunctionType
ALU = mybir.AluOpType

D = 1024
SEQ = 8192
NCORE = 8
SEG = 2048
NT = 16
SBT = 8
NSB = NT // SBT
TSB = SBT * 128
NBLK = TSB // 512
N_IN = 4360
DFF = 2816
NFF = DFF // 128
EPS = 1e-6
NEGB = -30000.0
O_QA, O_KA, O_VA, O_QM, O_KM, O_VM, O_OM, O_I, O_F, O_GA, O_GM = 0, 512, 640, 768, 1024, 1280, 1792, 2304, 2308, 2312, 3336


class Tile:
    __slots__ = ("name", "w", "r", "excl")

    def __init__(self, name, excl=False):
        self.name = name
        self.w = None
        self.r = {}
        self.excl = excl


class Sched:
    def __init__(self, nc, sems, lanes_per_q=8):
        self.nc = nc
        self.engs = {"pe": nc.tensor, "act": nc.scalar, "dve": nc.vector, "pool": nc.gpsimd, "sp": nc.sync}
        self.q = {k: [] for k in self.engs}
        self.cnt = {k: 0 for k in self.engs}
        it = iter(sems)
        self.sem = {k: next(it) for k in self.engs}
        self.lanes = {}
        for k in ("sp", "pool"):
            self.lanes[k] = [[next(it), 0] for _ in range(lanes_per_q)]
        self.lane_i = {k: 0 for k in self.lanes}
        self.seen = {k: {} for k in self.engs}

    def _need(self, eng, waits, ev, same_ok=False):
        if ev is None:
            return
        sem, val, src = ev
        if src == eng and (same_ok or eng == "pe"):
            return
        key = id(sem)
        if self.seen[eng].get(key, 0) >= val:
            return
        cur = waits.get(key)
        if cur is None or cur[1] < val:
            waits[key] = (sem, val)

    def _deps(self, eng, reads, writes):
        waits = {}
        for t in reads:
            self._need(eng, waits, t.w)
        for t in writes:
            self._need(eng, waits, t.w, same_ok=True)
            for ev in t.r.values():
                self._need(eng, waits, ev, same_ok=True)
        for key, (sem, val) in waits.items():
            self.seen[eng][key] = val
        return list(waits.values())

    def op(self, eng, fn, reads=(), writes=()):
        if any(t.excl for t in reads):
            writes = list(writes) + [t for t in reads if t.excl]
            reads = [t for t in reads if not t.excl]
        waits = self._deps(eng, reads, writes)
        self.cnt[eng] += 1
        sem = self.sem[eng]
        ev = (sem, self.cnt[eng], eng)
        self.q[eng].append((waits, fn, (sem, 1)))
        for t in reads:
            t.r[id(sem)] = ev
        for t in writes:
            t.w = ev
            t.r = {}
        return ev

    def dma(self, q, out_ap, in_ap, reads=(), writes=(), **kw):
        waits = self._deps(q, reads, writes)
        lanes = self.lanes[q]
        li = self.lane_i[q]
        self.lane_i[q] = (li + 1) % len(lanes)
        lane = lanes[li]
        sem = lane[0]
        if lane[1] > 0 and self.seen[q].get(id(sem), 0) < lane[1]:
            waits.append((sem, lane[1]))
            self.seen[q][id(sem)] = lane[1]
        lane[1] += 16
        ev = (sem, lane[1], "dma")

        def fn(e, out_ap=out_ap, in_ap=in_ap, kw=kw):
            return e.dma_start(out=out_ap, in_=in_ap, **kw)

        self.q[q].append((waits, fn, (sem, 16)))
        for t in reads:
            t.r[id(sem)] = ev
        for t in writes:
            t.w = ev
            t.r = {}
        return ev

    def barrier(self):
        evs = []
        for k in self.engs:
            if self.cnt[k] > 0:
                evs.append((self.sem[k], self.cnt[k], k))
        for k, lanes in self.lanes.items():
            for sem, val in lanes:
                if val > 0:
                    evs.append((sem, val, "dma"))
        for k in self.engs:
            waits = []
            for sem, val, src in evs:
                if src == k:
                    continue
                if self.seen[k].get(id(sem), 0) >= val:
                    continue
                self.seen[k][id(sem)] = val
                waits.append((sem, val))
            if waits:
                self.q[k].append((waits, None, None))

    def finish(self):
        waits = []
        for k, lanes in self.lanes.items():
            for sem, val in lanes:
                if val > 0:
                    waits.append((sem, val))
        self.q["sp"].append((waits, None, None))

    def emit(self):
        nc = self.nc
        with nc.Block() as block:
            def mk(name):
                def body(e):
                    for waits, fn, inc in self.q[name]:
                        ws = list(waits)
                        if fn is None:
                            for sem, val in ws:
                                e.wait_ge(sem, val)
                            continue
                        for sem, val in ws[:-1]:
                            e.wait_ge(sem, val)
                        ins = fn(e)
                        if ws:
                            ins._wait_ge(ws[-1][0], ws[-1][1])
                        ins.then_inc(inc[0], inc[1])
                return body
            block.tensor(mk("pe"))
            block.scalar(mk("act"))
            block.vector(mk("dve"))
            block.gpsimd(mk("pool"))
            block.sync(mk("sp"))


def t5_bucket_np(n):
    n = np.maximum(n, 0)
    max_exact = 16
    nf = np.maximum(n, 1).astype(np.float32)
    large = max_exact + (np.log(nf / max_exact) / math.log(128 / max_exact) * (32 - max_exact)).astype(np.int32)
    large = np.minimum(large, 31)
    return np.where(n < max_exact, n, large)


class _Stop(Exception):
    pass


def build_program(debug=(), stop=None, nonce=0.0):
    nc = bass.Bass("TRN2", target_bir_lowering=False)
    dbg_outs = {}

    def stage(name):
        if stop == name:
            raise _Stop()

    def din(name, shape, dt=F32):
        return nc.dram_tensor(name, list(shape), dt, kind="ExternalInput")

    def dout(name, shape, dt=F32):
        return nc.dram_tensor(name, list(shape), dt, kind="ExternalOutput")

    xs_d = din("xs", [128 + SEG, D]).ap()
    flag_d = din("flag", [128, 1]).ap()
    w_in_d = din("w_in", [D, N_IN]).ap()
    b_if_d = din("b_if", [2, 4]).ap()
    sinks_d = din("sinks", [1, 8]).ap()
    relb_d = din("rel_bias", [32, 8]).ap()
    g_attn_d = din("g_attn", [1, D]).ap()
    g_head_d = din("g_head", [1, 512]).ap()
    g_ffn_d = din("g_ffn", [1, D]).ap()
    g_fin_d = din("g_final", [1, D]).ap()
    w_ao_d = din("w_att_out", [512, D]).ap()
    w_mo_d = din("w_mlstm_out", [512, D]).ap()
    w_out_d = din("w_out", [D, D]).ap()
    w_gate_d = din("w_gate", [D, DFF]).ap()
    w_up_d = din("w_up", [D, DFF]).ap()
    w_down_d = din("w_down", [DFF, D]).ap()
    ident_bf_d = din("ident_bf", [128, 128], BF16).ap()
    ident_f_d = din("ident_f", [128, 128]).ap()
    ohT_d = din("ohT_rev", [32, 128]).ap()
    caus_d = din("causT", [128, 128]).ap()
    sel_d = din("sel", [4, 128]).ap()
    pm_d = din("pm", [4, 2]).ap()
    xsm_d = din("xsm", [16, D]).ap()
    ck_d = din("ck", [16, 128, 128])
    cv_d = din("cv", [16, 128, 128])
    sC_d = din("sC", [16, 4, 128, 64])
    sn_d = din("sn", [16, 4, 64])
    sm_d = din("sm", [16, 4])
    xprev_d = din("xprev", [3 * SEG, D]).ap()
    pact_d = din("pact", [128, 4]).ap()

    y_d = dout("y", [SEG, D]).ap()
    pk_d = dout("pk", [128, 128]).ap()
    pv_d = dout("pv", [128, 128]).ap()
    pC_d = dout("pC", [4, 128, 64]).ap()
    pn_d = dout("pn", [4, 64]).ap()
    pm_out_d = dout("pm_out", [4, 1]).ap()
    ys_d = dout("ys", [16, D]).ap()
    sko_d = dout("sko", [16, 128, 128])
    svo_d = dout("svo", [16, 128, 128])
    sCo_d = dout("sCo", [16, 4, 128, 64])
    sno_d = dout("sno", [16, 4, 2, 64])
    smo_d = dout("smo", [16, 4, 2])
    wr_scr = nc.dram_tensor("wr_scr", [8, 512], F32)

    es = contextlib.ExitStack()
    with es:
        def sb(name, shape, dt=F32):
            return es.enter_context(nc.sbuf_tensor("sb_" + name, list(shape), dt))

        sems = [es.enter_context(nc.semaphore(f"s{i}")) for i in range(5 + 16)]
        S = Sched(nc, sems)
        ps = es.enter_context(nc.psum_tensor("ps", [128, 8, 512], F32))
        PB = [Tile(f"bank{i}", excl=True) for i in range(8)]
        bank_rr = [0]

        def nextbank():
            b = bank_rr[0]
            bank_rr[0] = (b + 1) % 8
            return b

        def ps_bf(b):
            return ps[:, b, :].bitcast(BF16)

        def ACT(out, in_, func, reads, writes, **kw):
            S.op("act", lambda e: e.activation(out, in_, func, **kw), reads, writes)

        def TT(out, a, b, op, reads, writes, eng="dve"):
            S.op(eng, lambda e: e.tensor_tensor(out, a, b, op), reads, writes)

        def TS(out, a, s1, s2, op0, op1, reads, writes, eng="dve"):
            S.op(eng, lambda e: e.tensor_scalar(out, a, s1, s2, op0, op1), reads, writes)

        def STT(out, a, scalar, b, op0, op1, reads, writes, eng="dve"):
            S.op(eng, lambda e: e.scalar_tensor_tensor(out, a, scalar, b, op0, op1), reads, writes)

        def CP(out, in_, reads, writes, eng="dve"):
            S.op(eng, lambda e: e.tensor_copy(out, in_), reads, writes)

        def MM(out, lhsT, rhs, start, stop, reads, writes):
            S.op("pe", lambda e: e.matmul(out, lhsT, rhs, start=start, stop=stop), reads, writes)

        def TR(out, in_, ident, reads, writes):
            S.op("pe", lambda e: e.transpose(out, in_, ident), reads, writes)

        def DMA(q, out, in_, reads, writes, **kw):
            S.dma(q, out, in_, reads, writes, **kw)

        class Ring:
            def __init__(self, name, shape, dt, n):
                self.bufs = [(sb(f"{name}{i}", shape, dt), Tile(f"{name}{i}")) for i in range(n)]
                self.i = 0

            def next(self):
                r = self.bufs[self.i]
                self.i = (self.i + 1) % len(self.bufs)
                return r

        in_pre = [False]
        pre_tiles = {}
        pre_ones_done = set()
        pre_ring = [None]

        def dbg(name, ap, tile, shape, dt=F32):
            if name not in debug or (in_pre[0] and name in dbg_outs):
                return
            o = dout("dbg_" + name, shape, dt).ap()
            dbg_outs[name] = o
            DMA("sp", o, ap, [tile], [Tile("dbgo_" + name)])

        ident_bf = sb("ident_bf", [128, 128], BF16); Tidb = Tile("idb")
        ident_f = sb("ident_f", [128, 128]); Tidf = Tile("idf")
        caus = sb("caus", [128, 128]); Tcaus = Tile("caus")
        ones_bf = sb("ones_bf", [128, 128], BF16); Tones = Tile("ones")
        one_c = sb("one_c", [128, 1]); Tonesr = Tile("onec")
        BTall = sb("BTall", [128, 2, 2, 2, 2, 128]); TBT = Tile("BTall")
        esink = sb("esink", [1, 8, 128], BF16); Tesink = Tile("esink")
        flag = sb("flag", [128, 1]); Tflag = Tile("flag")
        zero_c = sb("zero_c", [128, 1]); Tzero = Tile("zero")
        eps_c = sb("eps_c", [128, 1]); Teps = Tile("eps")
        g_attn = sb("g_attnT", [128, 8]); Tgattn = Tile("gattn")
        g_ffn = sb("g_ffnT", [128, 8]); Tgffn = Tile("gffn")
        g_fin = sb("g_fin", [128, D]); Tgfin = Tile("gfin")
        g_head = sb("g_head", [128, 512]); Tghead = Tile("ghead")
        b_i = sb("b_i", [4, 1]); b_f = sb("b_f", [4, 1]); Tbif = Tile("bif")
        selm = sb("selm", [4, 128]); pmm = sb("pmm", [4, 2]); Tsel = Tile("sel")
        kaT_n = sb("kaT_n", [128, 128 + SEG], BF16)
        kaT_s = sb("kaT_s", [128, 128 + SEG], BF16)
        TkaT = [Tile(f"kaT{t}") for t in range(NT + 1)]
        va = sb("va", [128, NT + 1, 2, 2, 64], BF16)
        Tva = [Tile(f"va{t}") for t in range(NT + 1)]
        ST = sb("ST", [128, 2, 129]); TST = Tile("ST")
        rstate = sb("rstate", [4, 4]); Trst = Tile("rstate")
        hT = sb("hT", [128, 8, TSB], BF16)
        ThT = [Tile(f"hT{t}") for t in range(SBT)]
        hTh = sb("hTh", [128, 8, 128], BF16); ThTh = Tile("hTh")
        yattT = sb("yattT", [128, 4, TSB], BF16)
        TyaT = [Tile(f"yaT{t}") for t in range(SBT)]
        ymT = sb("ymT", [128, 4, TSB], BF16)
        TymT = [Tile(f"ymT{t}") for t in range(SBT)]
        hTs = sb("hTs", [128, 8, 16], BF16); ThTs = Tile("hTs")
        zs = sb("zs", [16, 2312]); Tzs = Tile("zs")
        yattTs = sb("yattTs", [128, 4, 16], BF16); TyaTs = Tile("yaTs")
        ymTs = sb("ymTs", [128, 4, 16], BF16); TymTs = Tile("ymTs")
        wbuf = sb("wbuf", [128, 3 * 4096], BF16)

        class VRing:
            def __init__(self, name, n, size):
                self.bufs = [(wbuf[:, i * size:(i + 1) * size], Tile(f"{name}{i}")) for i in range(n)]
                self.i = 0
                self.n = n

            def next(self):
                r = self.bufs[self.i]
                self.i = (self.i + 1) % len(self.bufs)
                return r
        xring = Ring("xt", [128, D], F32, 3)
        hbring = Ring("hbf", [128, D], BF16, 2)
        d4ring = Ring("d4", [128, 12], F32, 4)
        scal = sb("scal", [128, 64]); scal_i = [0]
        Tscal = [Tile(f"scal{i}") for i in range(64)]

        def newscal():
            i = scal_i[0]
            scal_i[0] = (i + 1) % 64
            return scal[:, i:i + 1], Tscal[i]

        ARENA_BYTES = 83 * 1024
        arena = sb("arena", [128, ARENA_BYTES // 4], F32)

        def carve(off_bytes, shape, dt):
            esz = 2 if dt == BF16 else 4
            n = int(np.prod(shape[1:]))
            assert off_bytes % 4 == 0 and off_bytes + n * esz <= ARENA_BYTES, (off_bytes, shape)
            v = arena[0:shape[0], off_bytes // 4: off_bytes // 4 + (n * esz) // 4]
            if dt == BF16:
                v = v.bitcast(BF16)
            if len(shape) == 2:
                return v
            names = "abcdef"[: len(shape) - 1]
            pat = "p (" + " ".join(names) + ") -> p " + " ".join(names)
            kw = {names[i]: shape[i + 1] for i in range(len(shape) - 1)}
            return v.rearrange(pat, **kw)

        DMA("sp", ident_bf[:], ident_bf_d, [], [Tidb])
        DMA("sp", ident_f[:], ident_f_d, [], [Tidf])
        DMA("sp", caus[:], caus_d, [], [Tcaus])
        DMA("sp", flag[:], flag_d, [], [Tflag])
        DMA("sp", g_attn[:], AP(g_attn_d.tensor, 0, [[1, 128], [128, 8]]), [], [Tgattn], allow_slow_non_contiguous=True)
        DMA("sp", g_ffn[:], AP(g_ffn_d.tensor, 0, [[1, 128], [128, 8]]), [], [Tgffn], allow_slow_non_contiguous=True)
        DMA("sp", g_fin[:], AP(g_fin_d.tensor, 0, [[0, 128], [1, D]]), [], [Tgfin])
        DMA("sp", g_head[:], AP(g_head_d.tensor, 0, [[0, 128], [1, 512]]), [], [Tghead])
        DMA("sp", b_i[:], AP(b_if_d.tensor, 0, [[1, 4], [1, 1]]), [], [Tbif])
        DMA("sp", b_f[:], AP(b_if_d.tensor, 4, [[1, 4], [1, 1]]), [], [Tbif])
        DMA("sp", selm[:], sel_d, [], [Tsel])
        DMA("sp", pmm[:], pm_d, [], [Tsel])
        S.op("dve", lambda e: e.memset(ones_bf[:], 1.0), [], [Tones])
        S.op("dve", lambda e: e.memset(one_c[:], 1.0), [], [Tonesr])
        S.op("dve", lambda e: e.memset(zero_c[:], 0.0), [], [Tzero])
        S.op("dve", lambda e: e.memset(eps_c[:], EPS), [], [Teps])
        S.op("dve", lambda e: e.memset(ST[:], 0.0), [], [TST])
        S.op("dve", lambda e: e.memset(rstate[:], 0.0), [], [Trst])

        relb = carve(0, [32, 8], F32); Trelb = Tile("relb")
        ohT = carve(64, [32, 128], F32); TohT = Tile("ohT")
        DMA("sp", relb[:], relb_d, [], [Trelb])
        DMA("sp", ohT[:], ohT_d, [], [TohT])
        tbr = carve(1024, [8, 512], F32); Ttbr = Tile("tbr")
        S.op("dve", lambda e: e.memset(tbr[:], NEGB), [], [Ttbr])
        b0 = nextbank()
        MM(ps[0:8, b0, 0:128], relb[:], ohT[:], True, True, [Trelb, TohT], [PB[b0]])
        CP(tbr[:, 256:384], ps[0:8, b0, 0:128], [PB[b0]], [Ttbr])
        Tscr = Tile("wr_scr")
        DMA("sp", wr_scr.ap(), tbr[:], [Ttbr], [Tscr])
        hk = carve(4096, [128, 8, 128], F32); Thk = Tile("hk")
        for c, base in ((1, 256), (0, 128)):
            DMA("sp", hk[:], AP(wr_scr, base, [[1, 128], [512, 8], [1, 128]]), [Tscr], [Thk])
            hv = hk[:]
            pst = list(hv.ap[0])
            for g in range(2):
                rev = AP(hv.tensor, hv.offset + (4 * g) * 128 + 127, [pst, [128, 2], [256, 2], [-1, 128]])
                CP(BTall[:, g, :, c, :, :], rev, [Thk], [TBT])
        sk = carve(8192 + 64, [1, 8], F32); Tsk = Tile("sk")
        DMA("sp", sk[:], sinks_d, [], [Tsk])
        ACT(sk[:], sk[:], AF.Exp, [Tsk], [Tsk])
        CP(esink[:], sk[:].unsqueeze(2).to_broadcast([1, 8, 128]), [Tsk], [Tesink])
        dbg("BTall", BTall[:], TBT, [128, 2, 2, 2, 2, 128])
        S.barrier()

        w_in_v = w_in_d.rearrange("(k p) n -> p k n", p=128)
        w_gate_v = w_gate_d.rearrange("(k p) n -> p k n", p=128)
        w_up_v = w_up_d.rearrange("(k p) n -> p k n", p=128)
        w_out_v = w_out_d.rearrange("(k p) n -> p k n", p=128)
        w_ao_v = w_ao_d.rearrange("(k p) n -> p k n", p=128)
        w_mo_v = w_mo_d.rearrange("(k p) n -> p k n", p=128)
        w_down_v = w_down_d.rearrange("(k p) n -> p k n", p=128)

        class WStream:
            def __init__(self, ring):
                self.ring = ring
                self.specs = []
                self.loaded = 0
                self.views = {}

            def add(self, K, cols, parts):
                self.specs.append((K, cols, parts))
                return len(self.specs) - 1

            def _load(self, i):
                K, cols, parts = self.specs[i]
                slot, tl = self.ring.next()
                v = slot[:, 0:K * cols].rearrange("p (k c) -> p k c", k=K)
                for (c0, n, src) in parts:
                    DMA("pool", v[:, :, c0:c0 + n], src, [], [tl])
                self.views[i] = (v, tl)

            def get(self, i, live_from=None):
                lf = i if live_from is None else live_from
                while self.loaded < min(lf + self.ring.n, len(self.specs)):
                    self._load(self.loaded)
                    self.loaded += 1
                return self.views[i]

        TR_BANKS = (7, 6)
        tr_i = [0]

        def norm_T(src_ap, src_reads, g_bc, Tg, dstT, Tdst, tn=128, src_is_dram=True, keep=None):
            if src_is_dram:
                xt, Txt = xring.next()
                DMA("sp", xt[0:tn, :], src_ap, src_reads, [Txt])
                xin = xt[0:tn, :]
                rd = [Txt]
            else:
                xin = src_ap
                rd = list(src_reads)
            hb, Thb = hbring.next()
            ss, Tss = newscal()
            ACT(hb[0:tn, :], xin, AF.Square, rd, [Thb, Tss], accum_out=ss[0:tn, :])
            rr, Trr = newscal()
            ACT(rr[0:tn, :], ss[0:tn, :], AF.Ln, [Tss, Teps], [Trr], scale=1.0 / D, bias=eps_c[0:tn, 0:1])
            ACT(rr[0:tn, :], rr[0:tn, :], AF.Exp, [Trr], [Trr], scale=-0.5)
            ACT(hb[0:tn, :], xin, AF.Copy, rd + [Trr], [Thb], scale=rr[0:tn, 0:1])
            trb = TR_BANKS[tr_i[0] % 2]
            tr_i[0] += 1
            pv = ps_bf(trb)
            for k in range(8):
                TR(pv[:, k * 128:k * 128 + tn], hb[0:tn, k * 128:(k + 1) * 128], ident_bf[0:tn, 0:tn], [Thb, Tidb], [PB[trb]])
            src = pv.rearrange("p (k t) -> p k t", k=8)[:, :, 0:tn]
            TT(dstT, src, g_bc[:, :].unsqueeze(2).to_broadcast([128, 8, tn]), ALU.mult, [PB[trb], Tg], [Tdst])


        def sample_phase():
            S.barrier()
            RX = mybir.AxisListType.X
            o = 0
            Cl = carve(o, [128, 64, 64], F32); o += 16384
            tmpC = carve(o, [128, 64, 64], F32)
            Kf = carve(o, [128, 16, 128], F32)
            Vf = carve(o + 8192, [128, 16, 128], F32); o += 16384
            Kb = carve(o, [128, 16, 128], BF16); o += 4096
            KT = carve(o, [128, 16, 128], BF16); o += 4096
            V1 = carve(o, [128, 16, 2, 66], BF16); o += 16 * 2 * 66 * 2
            qg = carve(o, [16, 4, 128], BF16); o += 1024
            qTs = carve(o, [128, 4, 16], BF16); o += 128
            bcol = carve(o, [128, 8], F32); o += 32
            Ss = carve(o, [128, 2, 16, 4], F32); o += 512
            Es = carve(o, [128, 2, 16, 4], BF16); o += 256
            pvs = carve(o, [4, 32, 66], F32); o += 32 * 66 * 4
            yas = carve(o, [16, 8, 66], F32); o += 8 * 66 * 4
            t512 = carve(o, [16, 512], F32); o += 2048
            u512 = carve(o, [16, 512], F32); o += 2048
            ogs = carve(o, [16, 512], F32); o += 2048
            hms = carve(o, [16, 512], F32); o += 2048
            yab = carve(o, [16, 512], BF16); o += 1024
            ymb = carve(o, [16, 512], BF16); o += 1024
            s16 = carve(o, [16, 64], F32); o += 256
            ql = carve(o, [128, 64], F32); o += 256
            kl = carve(o, [128, 64], F32); o += 256
            vl = carve(o, [128, 64], F32); o += 256
            nl = carve(o, [128, 64], F32); o += 256
            hl = carve(o, [128, 64], F32); o += 256
            t64 = carve(o, [128, 64], F32); o += 256
            Cq = carve(o, [128, 64], F32); o += 256
            sl_ = carve(o, [128, 32], F32); o += 128
            assert o <= ARENA_BYTES, o
            TCl, TtmpC, TKf, TVf, TKb, TKT, TV1, Tqg, TqTs, Tbcol, TSs, TEs, Tpvs, Tyas = [Tile(n) for n in
                ("Cl", "tmpC", "Kf", "Vf", "Kb", "KT", "V1", "qg", "qTs", "bcol", "Ss", "Es", "pvs", "yas")]
            TKf = TtmpC
            TVf = TtmpC
            Tt512, Tu512, Togs, Thms, Tyab, Tymb, Ts16, Tql, Tkl, Tvl, Tnl, Thl, Tt64, TCq, Tsl = [Tile(n) for n in
                ("t512", "u512", "ogs", "hms", "yab", "ymb", "s16", "ql", "kl", "vl", "nl", "hl", "t64", "Cq", "sl")]
            DMA("sp", Kf[:], ck_d.ap().rearrange("b k c -> k b c"), [], [TKf])
            DMA("pool", Vf[:], cv_d.ap().rearrange("b k c -> k b c"), [], [TVf])
            sCv = sC_d.ap().rearrange("b h (vh v) d -> vh (b h) (v d)", vh=2)
            sCov = sCo_d.ap().rearrange("b h (vh v) d -> vh (b h) (v d)", vh=2)
            Clf = Cl[:].rearrange("p v d -> p (v d)")
            for vh in range(2):
                ln = slice(vh * 64, (vh + 1) * 64)
                DMA("sp", Clf[ln, :], sCv[vh], [], [TCl])
                DMA("pool", nl[ln, :], sn_d.ap().rearrange("b h d -> (b h) d"), [], [Tnl])
                DMA("sp", sl_[ln, 0:1], sm_d.ap().rearrange("b (h o) -> (b h) o", o=1), [], [Tsl])
                DMA("pool", ql[ln, :], zs[:, 768:1024].rearrange("b (h d) -> b h d", h=4), [Tzs], [Tql])
                DMA("sp", kl[ln, :], zs[:, 1024:1280].rearrange("b (h d) -> b h d", h=4), [Tzs], [Tkl])
                DMA("pool", vl[ln, :], zs[:, 1280:1792].rearrange("b (h w v) -> b h w v", h=4, w=2)[:, :, vh, :], [Tzs], [Tvl])
                DMA("sp", sl_[ln, 1:2], zs[:, 2304:2308].rearrange("b (h o) -> b h o", o=1), [Tzs], [Tsl])
                DMA("pool", sl_[ln, 2:3], zs[:, 2308:2312].rearrange("b (h o) -> b h o", o=1), [Tzs], [Tsl])
            DMA("sp", sl_[:, 3:4], AP(b_if_d.tensor, 0, [[0, 32], [1, 4], [1, 1]]), [], [Tsl])
            DMA("pool", sl_[:, 4:5], AP(b_if_d.tensor, 4, [[0, 32], [1, 4], [1, 1]]), [], [Tsl])
            DMA("sp", bcol[:], AP(wr_scr, 255, [[1, 128], [512, 8]]), [Tscr], [Tbcol], allow_slow_non_contiguous=True)
            DMA("pool", s16[:, 0:8], AP(wr_scr, 383, [[0, 16], [512, 8]]), [Tscr], [Ts16], allow_slow_non_contiguous=True)
            DMA("sp", s16[:, 8:16], AP(sinks_d.tensor, 0, [[0, 16], [1, 8]]), [], [Ts16])
            DMA("pool", sko_d.ap()[:, 0:127, :], ck_d.ap()[:, 1:128, :], [], [Tile("sko")])
            DMA("sp", svo_d.ap()[:, 0:127, :], cv_d.ap()[:, 1:128, :], [], [Tile("svo")])
            DMA("pool", sko_d.ap()[:, 127, :], zs[:, 512:640], [Tzs], [Tile("sko2")])
            DMA("sp", svo_d.ap()[:, 127, :], zs[:, 640:768], [Tzs], [Tile("svo2")])
            CP(Kb[:], Kf[:], [TKf], [TKb])
            for half in range(2):
                bk = 4 + half
                pvb = ps_bf(bk)
                for i in range(8):
                    b_ = half * 8 + i
                    TR(pvb[:, i * 128:(i + 1) * 128], Kb[:, b_, :], ident_bf[:], [TKb, Tidb], [PB[bk]])
                ACT(KT[:, half * 8:(half + 1) * 8, :], pvb.rearrange("p (i k) -> p i k", i=8), AF.Copy, [PB[bk]], [TKT])
            CP(V1[:, :, :, 0:64], Vf[:].rearrange("k b (g d) -> k b g d", g=2), [TVf], [TV1])
            S.op("dve", lambda e: e.memset(V1[:, :, :, 64:65], 1.0), [], [TV1])
            TS(qg[:].rearrange("b j (g d) -> b j g d", g=2), zs[:, 0:512].rearrange("b (g j d) -> b j g d", g=2, j=4), 0.125, None, ALU.mult, ALU.bypass, [Tzs], [Tqg])
            pvq = ps_bf(6)
            for j in range(4):
                TR(pvq[:, j * 16:(j + 1) * 16], qg[:, j, :], ident_bf[0:16, 0:16], [Tqg, Tidb], [PB[6]])
            CP(qTs[:].rearrange("p j b -> p (j b)"), pvq[:, 0:64], [PB[6]], [TqTs])
            for b_ in range(16):
                for g in range(2):
                    gr = slice(g * 64, (g + 1) * 64)
                    MM(ps[:, g, b_ * 4:(b_ + 1) * 4], KT[gr, b_, :], qTs[gr, :, b_], True, True, [TKT, TqTs], [PB[g]])
            for g in range(2):
                TT(Ss[:, g, :, :], ps[:, g, 0:64].rearrange("p (b j) -> p b j", j=4), bcol[:, 4 * g:4 * g + 4].unsqueeze(1).to_broadcast([128, 16, 4]), ALU.add, [PB[g], Tbcol], [TSs])
            ACT(Es[:], Ss[:], AF.Exp, [TSs], [TEs])
            for b_ in range(16):
                for g in range(2):
                    slot = b_ * 2 + g
                    bk = 2 + slot // 7 if slot < 28 else 7
                    if slot >= 28:
                        col = (slot - 28) * 66
                    else:
                        col = (slot % 7) * 66
                    bk = (2 + slot // 7) if slot < 28 else 7
                    bk = [2, 3, 4, 5, 7][min(slot // 7, 4)]
                    col = (slot % 7) * 66
                    MM(ps[0:4, bk, col:col + 65], Es[:, g, b_, :], V1[:, b_, g, 0:65], True, True, [TEs, TV1], [PB[bk]])
            for gi_, bk in enumerate([2, 3, 4, 5, 7]):
                ns = 7 if gi_ < 4 else 4
                CP(pvs[:, gi_ * 7:gi_ * 7 + ns, :], ps[0:4, bk, 0:ns * 66].rearrange("p (s c) -> p s c", c=66), [PB[bk]], [Tpvs])
            yas_v = yas[:].rearrange("b (g j) c -> b g j c", g=2)
            for j in range(4):
                DMA("pool", yas_v[:, :, j, :], pvs[j:j + 1, :, :], [Tpvs], [Tyas])
            qv = zs[:, 0:512].rearrange("b (g j d) -> b g j d", g=2, j=4)
            kv = zs[:, 512:640].rearrange("b (g d) -> b g d", g=2).unsqueeze(2).to_broadcast([16, 2, 4, 64])
            TT(t512[:].rearrange("b (g j d) -> b g j d", g=2, j=4), qv, kv, ALU.mult, [Tzs], [Tt512])
            S.op("dve", lambda e: e.tensor_reduce(s16[:, 16:24], t512[:].rearrange("b (h d) -> b h d", d=64), RX, ALU.add), [Tt512], [Ts16])
            STT(s16[:, 16:24], s16[:, 16:24], 0.125, s16[:, 0:8], ALU.mult, ALU.add, [Ts16], [Ts16])
            ACT(s16[:, 16:24], s16[:, 16:24], AF.Exp, [Ts16], [Ts16])
            ACT(s16[:, 24:32], s16[:, 8:16], AF.Exp, [Ts16], [Ts16])
            TT(s16[:, 32:40], yas[:, :, 64], s16[:, 16:24], ALU.add, [Tyas, Ts16], [Ts16])
            TT(s16[:, 32:40], s16[:, 32:40], s16[:, 24:32], ALU.add, [Ts16], [Ts16])
            S.op("dve", lambda e: e.reciprocal(s16[:, 32:40], s16[:, 32:40]), [Ts16], [Ts16])
            vv = zs[:, 640:768].rearrange("b (g d) -> b g d", g=2).unsqueeze(2).to_broadcast([16, 2, 4, 64])
            es_b = s16[:, 16:24].rearrange("b (g j) -> b g j", g=2).unsqueeze(3).to_broadcast([16, 2, 4, 64])
            TT(u512[:].rearrange("b (g j d) -> b g j d", g=2, j=4), vv, es_b, ALU.mult, [Tzs, Ts16], [Tu512])
            TT(u512[:].rearrange("b (h d) -> b h d", d=64), u512[:].rearrange("b (h d) -> b h d", d=64), yas[:, :, 0:64], ALU.add, [Tu512, Tyas], [Tu512])
            TT(yab[:].rearrange("b (h d) -> b h d", d=64), u512[:].rearrange("b (h d) -> b h d", d=64),
               s16[:, 32:40].unsqueeze(2).to_broadcast([16, 8, 64]), ALU.mult, [Tu512, Ts16], [Tyab])
            pva = ps_bf(6)
            for p in range(4):
                TR(pva[:, p * 16:(p + 1) * 16], yab[:, p * 128:(p + 1) * 128], ident_bf[0:16, 0:16], [Tyab, Tidb], [PB[6]])
            CP(yattTs[:].rearrange("p c b -> p (c b)"), pva[:, 0:64], [PB[6]], [TyaTs])
            TS(kl[:], kl[:], 0.125, None, ALU.mult, ALU.bypass, [Tkl], [Tkl])
            c_ = lambda i: sl_[:, i:i + 1]
            TT(c_(5), c_(1), c_(3), ALU.add, [Tsl], [Tsl])
            TT(c_(6), c_(2), c_(4), ALU.add, [Tsl], [Tsl])
            ACT(c_(6), c_(6), AF.Exp, [Tsl], [Tsl], scale=-1.0)
            ACT(c_(6), c_(6), AF.Ln, [Tsl], [Tsl], bias=1.0)
            TT(c_(7), c_(0), c_(6), ALU.subtract, [Tsl], [Tsl])
            TT(c_(8), c_(7), c_(5), ALU.max, [Tsl], [Tsl])
            TT(c_(9), c_(7), c_(8), ALU.subtract, [Tsl], [Tsl])
            TT(c_(10), c_(5), c_(8), ALU.subtract, [Tsl], [Tsl])
            TS(c_(11), c_(8), -1.0, None, ALU.mult, ALU.bypass, [Tsl], [Tsl])
            ACT(sl_[:, 9:12], sl_[:, 9:12], AF.Exp, [Tsl], [Tsl])
            TT(tmpC[:], Cl[:], ql[:].unsqueeze(1).to_broadcast([128, 64, 64]), ALU.mult, [TCl, Tql], [TtmpC])
            S.op("dve", lambda e: e.tensor_reduce(Cq[:], tmpC[:], RX, ALU.add), [TtmpC], [TCq])
            TT(t64[:], nl[:], ql[:], ALU.mult, [Tnl, Tql], [Tt64])
            S.op("dve", lambda e: e.tensor_reduce(c_(12), t64[:], RX, ALU.add), [Tt64], [Tsl])
            TT(t64[:], kl[:], ql[:], ALU.mult, [Tkl, Tql], [Tt64])
            S.op("dve", lambda e: e.tensor_reduce(c_(13), t64[:], RX, ALU.add), [Tt64], [Tsl])
            TT(c_(14), c_(10), c_(13), ALU.mult, [Tsl], [Tsl])
            STT(c_(15), c_(12), c_(9), c_(14), ALU.mult, ALU.add, [Tsl], [Tsl])
            STT(c_(16), c_(15), -1.0, c_(15), ALU.mult, ALU.max, [Tsl], [Tsl])
            TT(c_(16), c_(16), c_(11), ALU.max, [Tsl], [Tsl])
            S.op("dve", lambda e: e.reciprocal(c_(16), c_(16)), [Tsl], [Tsl])
            TS(hl[:], Cq[:], c_(9), None, ALU.mult, ALU.bypass, [TCq, Tsl], [Thl])
            STT(hl[:], vl[:], c_(14), hl[:], ALU.mult, ALU.add, [Tvl, Tsl, Thl], [Thl])
            TS(hl[:], hl[:], c_(16), None, ALU.mult, ALU.bypass, [Thl, Tsl], [Thl])
            TT(tmpC[:], vl[:].unsqueeze(2).to_broadcast([128, 64, 64]), kl[:].unsqueeze(1).to_broadcast([128, 64, 64]), ALU.mult, [Tvl, Tkl, TCq], [TtmpC])
            TS(Cl[:], Cl[:], c_(9), None, ALU.mult, ALU.bypass, [TCl, Tsl], [TCl])
            STT(Cl[:], tmpC[:], c_(10), Cl[:], ALU.mult, ALU.add, [TtmpC, Tsl, TCl], [TCl])
            TS(nl[:], nl[:], c_(9), None, ALU.mult, ALU.bypass, [Tnl, Tsl, Tt64], [Tnl])
            STT(nl[:], kl[:], c_(10), nl[:], ALU.mult, ALU.add, [Tkl, Tsl, Tnl], [Tnl])
            snov = sno_d.ap().rearrange("b h w d -> w (b h) d")
            smov = smo_d.ap().rearrange("b h (w o) -> w (b h) o", o=1)
            hms_v = hms[:].rearrange("b (h w v) -> b h w v", h=4, w=2)
            for vh in range(2):
                ln = slice(vh * 64, (vh + 1) * 64)
                DMA("sp", sCov[vh], Clf[ln, :], [TCl], [Tile("sCo")])
                DMA("pool", snov[vh], nl[ln, :], [Tnl], [Tile("sno")])
                DMA("sp", smov[vh], sl_[ln, 8:9], [Tsl], [Tile("smo")], allow_slow_non_contiguous=True)
                DMA("pool", hms_v[:, :, vh, :], hl[ln, :], [Thl], [Thms])
            TT(t512[:], hms[:], hms[:], ALU.mult, [Thms], [Tt512])
            S.op("dve", lambda e: e.tensor_reduce(s16[:, 40:44], t512[:].rearrange("b (h v) -> b h v", v=128), RX, ALU.add), [Tt512], [Ts16])
            ACT(s16[:, 40:44], s16[:, 40:44], AF.Ln, [Ts16, Teps], [Ts16], scale=1.0 / 128, bias=eps_c[0:16, 0:1])
            ACT(s16[:, 40:44], s16[:, 40:44], AF.Exp, [Ts16], [Ts16], scale=-0.5)
            ACT(ogs[:], zs[:, 1792:2304], AF.Sigmoid, [Tzs], [Togs])
            TT(ogs[:], ogs[:], g_head[0:16, :], ALU.mult, [Togs, Tghead], [Togs])
            TT(u512[:].rearrange("b (h v) -> b h v", v=128), hms[:].rearrange("b (h v) -> b h v", v=128),
               s16[:, 40:44].unsqueeze(2).to_broadcast([16, 4, 128]), ALU.mult, [Thms, Ts16, Tu512], [Tu512])
            TT(ymb[:], u512[:], ogs[:], ALU.mult, [Tu512, Togs], [Tymb])
            pvm = ps_bf(6)
            for h in range(4):
                TR(pvm[:, h * 16:(h + 1) * 16], ymb[:, h * 128:(h + 1) * 128], ident_bf[0:16, 0:16], [Tymb, Tidb], [PB[6]])
            CP(ymTs[:].rearrange("p c b -> p (c b)"), pvm[:, 0:64], [PB[6]], [TymTs])
            dbg("yab", yab[:], Tyab, [16, 512], BF16)
            dbg("ymb", ymb[:], Tymb, [16, 512], BF16)
            dbg("zs", zs[:], Tzs, [16, 2312])

        try:
            stage("setup")
            def superblock(sbi, pre, ST, TST, rstate, Trst, xrow, par_=0):
                gt0 = sbi * SBT

                def T_(name):
                    if pre:
                        name = f"{name}_p{par_}"
                        if name not in pre_tiles:
                            pre_tiles[name] = Tile(name)
                        return pre_tiles[name]
                    return Tile(name)
                if pre and par_ == 1:
                    hT_ = carve(66 * 1024, [128, 8, TSB], BF16)
                    ThT_ = [T_(f"hTalt{t}") for t in range(SBT)]
                else:
                    hT_, ThT_ = hT, ThT
                o = 0
                if pre:
                    pb_ = par_ * 33 * 1024
                    km_tok = carve(pb_, [128, SBT, 256], BF16); pb_ += SBT * 256 * 2
                    vm1 = carve(pb_, [128, SBT, 4, 130], BF16); pb_ += SBT * 4 * 130 * 2
                    rows = [carve(pb_ + i * TSB * 4, [4, TSB], F32) for i in range(4)]; pb_ += 4 * TSB * 4
                    ecol = carve(pb_, [128, SBT, 8], F32); pb_ += SBT * 8 * 4
                    gbb = carve(pb_, [128, SBT, 2], F32); pb_ += SBT * 2 * 4
                    Gm = carve(pb_, [4, SBT, 2], F32); pb_ += SBT * 2 * 4
                    gsm = carve(pb_, [4, SBT + 1], F32); pb_ += (SBT + 2) * 4
                    pre_cw0 = pb_
                    assert pb_ + 64 + 2 * 1040 + 1032 <= (par_ + 1) * 33 * 1024
                if pre:
                    _save = (km_tok, vm1, rows, ecol, gbb, Gm, gsm)
                qaT = carve(o, [128, 4, TSB], BF16); o += 4 * TSB * 2
                qmT = carve(o, [128, 2, TSB], BF16); o += 2 * TSB * 2
                kmT = carve(o, [128, 2, TSB], BF16); o += 2 * TSB * 2
                km_tok = carve(o, [128, SBT, 256], BF16); o += SBT * 256 * 2
                vm1 = carve(o, [128, SBT, 4, 130], BF16); o += SBT * 4 * 130 * 2
                og = carve(o, [128, SBT, 512], BF16); o += SBT * 512 * 2
                rows = [carve(o + i * TSB * 4, [4, TSB], F32) for i in range(4)]; o += 4 * TSB * 4
                ecol = carve(o, [128, SBT, 8], F32); o += SBT * 8 * 4
                gbb = carve(o, [128, SBT, 2], F32); o += SBT * 2 * 4
                Gm = carve(o, [4, SBT, 2], F32); o += SBT * 2 * 4
                gsm = carve(o, [4, SBT + 1], F32); o += (SBT + 2) * 4
                sg_r = [(carve(o + i * 2048, [128, 512], F32), T_(f"sgtmp{i}")) for i in range(2)]; o += 4096
                cw0 = o
                if pre:
                    km_tok, vm1, rows, ecol, gbb, Gm, gsm = _save
                    cw0 = pre_cw0
                TqaT = [T_(f"qaT{i}") for i in range(NBLK)]
                TqmT = [T_(f"qmT{i}") for i in range(NBLK)]
                TkmT = [T_(f"kmT{i}") for i in range(NBLK)]
                Tkmtok = [T_(f"kmtok{i}") for i in range(SBT)]
                Tvm1 = [T_(f"vm1{i}") for i in range(SBT)]
                Tog = [T_(f"og{i}") for i in range(SBT)]
                Trow = [T_(f"row{i}") for i in range(4)]
                Tecol = T_("ecol"); Tgbb = T_("gbb"); TGm = T_("Gm"); Tgsm = T_("gsm")

                if sbi == 0 and not pre:
                    norm_T(xs_d[0:128, :], [], g_attn, Tgattn, hTh[:], ThTh)
                    norm_T(xsm_d, [], g_attn, Tgattn, hTs[:], ThTs, tn=16)
                for t in range(SBT):
                    norm_T(xrow(gt0 + t), [], g_attn, Tgattn, hT_[:, :, t * 128:(t + 1) * 128], ThT_[t])
                    if pre:
                        yield "A"

                stage(f"A{sbi}")
                W = WStream(pre_ring[0] if pre else VRing(f"wA{sbi}_", 3, 4096))
                if not pre:
                    c_qa = W.add(8, 512, [(0, 512, w_in_v[:, :, O_QA:O_QA + 512])])
                    c_k = W.add(8, 512, [(0, 128, w_in_v[:, :, O_KA:O_KA + 128]),
                                         (128, 64, w_in_v[:, :, O_KA + 64:O_KA + 128]),
                                         (192, 64, w_in_v[:, :, O_KA:O_KA + 64]),
                                         (256, 256, w_in_v[:, :, O_QM:O_QM + 256])])
                c_m = W.add(8, 512, [(0, 256, w_in_v[:, :, O_KM:O_KM + 256]),
                                     (256, 128, w_in_v[:, :, O_VA:O_VA + 128]),
                                     (384, 8, w_in_v[:, :, O_I:O_I + 8])])
                c_vm = W.add(8, 512, [(0, 512, w_in_v[:, :, O_VM:O_VM + 512])])
                if not pre:
                    c_om = W.add(8, 512, [(0, 512, w_in_v[:, :, O_OM:O_OM + 512])])
                    c_D = []
                    for blk in range(NBLK):
                        cd = {}
                        cd["ga0"] = W.add(8, 512, [(0, 512, w_in_v[:, :, O_GA:O_GA + 512])])
                        cd["ga1"] = W.add(8, 512, [(0, 512, w_in_v[:, :, O_GA + 512:O_GA + 1024])])
                        cd["gm0"] = W.add(8, 512, [(0, 512, w_in_v[:, :, O_GM:O_GM + 512])])
                        cd["gm1"] = W.add(8, 512, [(0, 512, w_in_v[:, :, O_GM + 512:O_GM + 1024])])
                        cd["ao"] = W.add(4, 1024, [(0, 1024, w_ao_v)])
                        cd["mo"] = W.add(4, 1024, [(0, 1024, w_mo_v)])
                        cd["wo0"] = W.add(8, 512, [(0, 512, w_out_v[:, :, 0:512])])
                        cd["wo1"] = W.add(8, 512, [(0, 512, w_out_v[:, :, 512:1024])])
                        c_D.append(cd)
                WE = WStream(VRing(f"wE{sbi}_", 6, 2048))
                FG = [(g * 2, 2) for g in range(NFF // 2)]
                c_E = []
                for (f0, nf) in FG:
                    cg = WE.add(8, nf * 128, [(0, nf * 128, w_gate_v[:, :, f0 * 128:(f0 + nf) * 128])])
                    cu = WE.add(8, nf * 128, [(0, nf * 128, w_up_v[:, :, f0 * 128:(f0 + nf) * 128])])
                    cdn = WE.add(nf, 1024, [(0, 1024, w_down_v[:, f0:f0 + nf, :])])
                    c_E.append((cg, cu, cdn))

                def fm_group(wv, Tw, c0, M, rhs_list, evac):
                    for idx, (rhs, n, rds) in enumerate(rhs_list):
                        b = nextbank()
                        for k in range(8):
                            MM(ps[0:M, b, 0:n], wv[:, k, c0:c0 + M], rhs[:, k, :], k == 0, k == 7, [Tw] + rds, [PB[b]])
                        evac(idx, ps[0:M, b, 0:n], PB[b])

                def samp_proj(wv, Tw, ncols, pieces):
                    if pre or sbi != 0:
                        return
                    bb = nextbank()
                    for k in range(8):
                        MM(ps[0:16, bb, 0:ncols], hTs[:, k, :], wv[:, k, 0:ncols], k == 0, k == 7, [Tw, ThTs], [PB[bb]])
                    for (c0, n, z0) in pieces:
                        CP(zs[:, z0:z0 + n], ps[0:16, bb, c0:c0 + n], [PB[bb]], [Tzs])

                blk_rhs = [(hT_[:, :, blk * 512:(blk + 1) * 512], 512, ThT_[blk * 4:(blk + 1) * 4]) for blk in range(NBLK)]
                halo_rhs = [(hTh[:], 128, [ThTh])]

                if not pre:
                    wv, Tw = W.get(c_qa)
                    for p in range(4):
                        def ev(idx, pa, Tb, p=p):
                            ACT(qaT[:, p, idx * 512:(idx + 1) * 512], pa, AF.Copy, [Tb], [TqaT[idx]], scale=0.125)
                        fm_group(wv, Tw, p * 128, 128, blk_rhs, ev)
                    samp_proj(wv, Tw, 512, [(0, 512, 0)])
                    stage(f"Ba{sbi}")
                    wv, Tw = W.get(c_k)
                    for (c0, dst) in ((0, kaT_n), (128, kaT_s)):
                        def ev(idx, pa, Tb, dst=dst):
                            col = 128 + (gt0 * 128) + idx * 512
                            ACT(dst[:, col:col + 512], pa, AF.Copy, [Tb], TkaT[1 + gt0 + idx * 4: 1 + gt0 + idx * 4 + 4])
                        fm_group(wv, Tw, c0, 128, blk_rhs, ev)
                        if sbi == 0:
                            def evh(idx, pa, Tb, dst=dst):
                                ACT(dst[:, 0:128], pa, AF.Copy, [Tb], [TkaT[0]])
                            fm_group(wv, Tw, c0, 128, halo_rhs, evh)
                    for p in range(2):
                        def ev(idx, pa, Tb, p=p):
                            CP(qmT[:, p, idx * 512:(idx + 1) * 512], pa, [Tb], [TqmT[idx]])
                        fm_group(wv, Tw, 256 + p * 128, 128, blk_rhs, ev)
                    samp_proj(wv, Tw, 512, [(0, 128, 512), (256, 256, 768)])
                    if sbi == NSB - 1:
                        b = nextbank()
                        lt = SBT - 1
                        for k in range(8):
                            MM(ps[:, b, 0:128], hT_[:, k, lt * 128:(lt + 1) * 128], wv[:, k, 0:128], k == 0, k == 7, [Tw, ThT_[lt]], [PB[b]])
                        pko = sb("pko", [128, 128]); Tpko = Tile("pko")
                        CP(pko[:], ps[:, b, 0:128], [PB[b]], [Tpko])
                        DMA("sp", pk_d, pko[:], [Tpko], [Tile("pk_d")])
                stage(f"Bb{sbi}")
                wv, Tw = W.get(c_m)
                if not pre:
                    for p in range(2):
                        def ev(idx, pa, Tb, p=p):
                            ACT(kmT[:, p, idx * 512:(idx + 1) * 512], pa, AF.Copy, [Tb], [TkmT[idx]], scale=0.125)
                        fm_group(wv, Tw, p * 128, 128, blk_rhs, ev)
                r_ig, r_t, r_nf, r_u = rows

                def ev(idx, pa, Tb):
                    ACT(r_ig[:, idx * 512:(idx + 1) * 512], pa, AF.Identity, [Tb, Tbif], [Trow[0]], bias=b_i[:, 0:1])
                fm_group(wv, Tw, 384, 4, blk_rhs, ev)

                def ev(idx, pa, Tb):
                    ACT(r_t[:, idx * 512:(idx + 1) * 512], pa, AF.Identity, [Tb, Tbif], [Trow[1]], bias=b_f[:, 0:1])
                fm_group(wv, Tw, 388, 4, blk_rhs, ev)
                stage(f"Bc{sbi}")
                ncol = 256 if pre else 384
                for t in range(SBT):
                    b = nextbank()
                    for k in range(8):
                        MM(ps[:, b, 0:ncol], hT_[:, k, t * 128:(t + 1) * 128], wv[:, k, 0:ncol], k == 0, k == 7, [Tw, ThT_[t]], [PB[b]])
                    ACT(km_tok[:, t, :], ps[:, b, 0:256], AF.Copy, [PB[b]], [Tkmtok[t]], scale=0.125)
                    if not pre:
                        pvv = ps[:, b, 256:384].rearrange("p (g d) -> p g d", g=2)
                        gt = gt0 + t
                        CP(va[:, gt + 1, :, 0, :], pvv, [PB[b]], [Tva[gt + 1]])
                        ACT(va[:, gt + 1, :, 1, :], pvv, AF.Copy, [PB[b]], [Tva[gt + 1]])
                        if gt == NT - 1:
                            pvo = sb("pvo", [128, 128]); Tpvo = Tile("pvo")
                            CP(pvo[:], ps[:, b, 256:384], [PB[b]], [Tpvo])
                            DMA("sp", pv_d, pvo[:], [Tpvo], [Tile("pv_d")])
                if sbi == 0 and not pre:
                    b = nextbank()
                    for k in range(8):
                        MM(ps[:, b, 0:128], hTh[:, k, :], wv[:, k, 256:384], k == 0, k == 7, [Tw, ThTh], [PB[b]])
                    pvv = ps[:, b, 0:128].rearrange("p (g d) -> p g d", g=2)
                    CP(va[:, 0, :, 0, :], pvv, [PB[b]], [Tva[0]])
                    ACT(va[:, 0, :, 1, :], pvv, AF.Copy, [PB[b]], [Tva[0]])
                samp_proj(wv, Tw, 392, [(0, 256, 1024), (256, 128, 640), (384, 8, 2304)])
                stage(f"Bd{sbi}")
                if (not pre) or (par_ not in pre_ones_done):
                    if pre:
                        pre_ones_done.add(par_)
                    for t in range(SBT):
                        S.op("dve", lambda e, t=t: e.memset(vm1[:, t, :, 128:129], 1.0), [], [Tvm1[t]])
                wv, Tw = W.get(c_vm)
                for t in range(SBT):
                    b = nextbank()
                    for k in range(8):
                        MM(ps[:, b, 0:512], hT_[:, k, t * 128:(t + 1) * 128], wv[:, k, 0:512], k == 0, k == 7, [Tw, ThT_[t]], [PB[b]])
                    ACT(vm1[:, t, :, 0:128], ps[:, b, 0:512].rearrange("p (h v) -> p h v", h=4), AF.Copy, [PB[b]], [Tvm1[t]])
                samp_proj(wv, Tw, 512, [(0, 512, 1280)])
                stage(f"Be{sbi}")
                if not pre:
                    wv, Tw = W.get(c_om)
                    for t in range(SBT):
                        b = nextbank()
                        for k in range(8):
                            MM(ps[:, b, 0:512], hT_[:, k, t * 128:(t + 1) * 128], wv[:, k, 0:512], k == 0, k == 7, [Tw, ThT_[t]], [PB[b]])
                        tmp, Ttmp = sg_r[t % 2]
                        ACT(tmp[:], ps[:, b, 0:512], AF.Sigmoid, [PB[b]], [Ttmp])
                        TT(og[:, t, :], tmp[:], g_head[:], ALU.mult, [Ttmp, Tghead], [Tog[t]])
                    samp_proj(wv, Tw, 512, [(0, 512, 1792)])

                stage(f"B{sbi}")
                if pre:
                    yield "front"
                ACT(r_t[:], r_t[:], AF.Exp, [Trow[1]], [Trow[1]], scale=-1.0)
                ACT(r_t[:], r_t[:], AF.Ln, [Trow[1]], [Trow[1]], bias=1.0)
                S.op("dve", lambda e: e.tensor_tensor_scan(r_nf[:], one_c[0:4, 0:1].to_broadcast([4, TSB]), r_t[:], rstate[:, 0:1], ALU.mult, ALU.add),
                     [Tonesr, Trow[1], Trst], [Trow[2]])
                dbg(f"nl{sbi}", r_t[:], Trow[1], [4, TSB])
                dbg(f"NF{sbi}", r_nf[:], Trow[2], [4, TSB])
                dbg(f"ig{sbi}", r_ig[:], Trow[0], [4, TSB])
                TT(r_ig[:], r_ig[:], r_nf[:], ALU.add, [Trow[0], Trow[2]], [Trow[0]])
                S.op("dve", lambda e: e.tensor_tensor_scan(r_t[:], r_ig[:], r_ig[:], rstate[:, 1:2], ALU.max, ALU.max),
                     [Trow[0], Trst, Trow[1]], [Trow[1]])
                CP(gsm[:, 0:1], rstate[:, 1:2], [Trst], [Tgsm])
                rt_v = r_t[:].rearrange("p (t c) -> p t c", c=128)
                CP(gsm[:, 1:SBT + 1], rt_v[:, :, 127], [Trow[1]], [Tgsm])
                CP(r_u[:].rearrange("p (t c) -> p t c", c=128), gsm[:, 1:SBT + 1].unsqueeze(2).to_broadcast([4, SBT, 128]), [Tgsm], [Trow[3]])
                CP(rstate[:, 0:1], r_nf[:, TSB - 1:TSB], [Trow[2], Tgsm], [Trst])
                CP(rstate[:, 1:2], r_t[:, TSB - 1:TSB], [Trow[1]], [Trst])
                TT(r_ig[:], r_ig[:], r_u[:], ALU.subtract, [Trow[0], Trow[3]], [Trow[0]])
                ACT(r_ig[:], r_ig[:], AF.Exp, [Trow[0]], [Trow[0]])
                TT(r_nf[:], r_nf[:], r_u[:], ALU.subtract, [Trow[2], Trow[3]], [Trow[2]])
                ACT(r_nf[:], r_nf[:], AF.Exp, [Trow[2]], [Trow[2]])
                gexp = carve(cw0, [4, SBT], F32)
                Tgexp = T_("gexp")
                TT(gexp[:], gsm[:, 0:SBT], gsm[:, 1:SBT + 1], ALU.subtract, [Tgsm], [Tgexp])
                ACT(gexp[:], gexp[:], AF.Exp, [Tgexp], [Tgexp])
                TT(Gm[:], gexp[:].unsqueeze(2).to_broadcast([4, SBT, 2]), pmm[:].unsqueeze(1).to_broadcast([4, SBT, 2]), ALU.mult, [Tgexp, Tsel], [TGm])
                b = nextbank()
                MM(ps[:, b, 0:SBT * 2], selm[:], Gm[:].rearrange("p t c -> p (t c)"), True, True, [Tsel, TGm], [PB[b]])
                CP(gbb[:].rearrange("p t c -> p (t c)"), ps[:, b, 0:SBT * 2], [PB[b]], [Tgbb])
                b = nextbank()
                for t in range(SBT):
                    TR(ps[:, b, t * 8:t * 8 + 4], r_ig[:, t * 128:(t + 1) * 128], ident_f[0:4, 0:4], [Trow[0], Tidf], [PB[b]])
                    TR(ps[:, b, t * 8 + 4:t * 8 + 8], r_nf[:, t * 128:(t + 1) * 128], ident_f[0:4, 0:4], [Trow[2], Tidf], [PB[b]])
                CP(ecol[:].rearrange("p t c -> p (t c)"), ps[:, b, 0:SBT * 8], [PB[b]], [Tecol])
                dbg(f"ecol{sbi}", ecol[:], Tecol, [128, SBT, 8])
                dbg(f"gbb{sbi}", gbb[:], Tgbb, [128, SBT, 2])
                dbg(f"rows{sbi}", r_t[:], Trow[1], [4, TSB])
                dbg(f"rowA{sbi}", r_ig[:], Trow[0], [4, TSB])

                stage(f"R{sbi}")
                if pre:
                    yield "rows"
                o = cw0 + 64
                if pre:
                    V2_pre = [(carve(o + i * 1040, [128, 4, 130], BF16), T_(f"V2{i}")) for i in range(2)]
                    STgf_pre = carve(o + 2080, [128, 2, 129], F32)
                    o = 0
                Sb_r = [(carve(o + i * 2048, [128, 512], F32), T_(f"Sb{i}")) for i in range(2)]; o += 4096
                ET_r = [(carve(o + i * 1024, [128, 512], BF16), T_(f"ET{i}")) for i in range(8)]; o += 8192
                rden_r = [(carve(o + i * 2048, [128, 512], F32), T_(f"rden{i}")) for i in range(2)]; o += 4096
                Wt_r = [(carve(o + i * 1024, [128, 4, 128], BF16), T_(f"Wt{i}")) for i in range(2)]; o += 2048
                V2_r = [(carve(o + i * 1040, [128, 4, 130], BF16), T_(f"V2{i}")) for i in range(2)]; o += 2080
                ym_r = [(carve(o + i * 1024, [128, 512], BF16), T_(f"ym{i}")) for i in range(2)]; o += 2048
                STg_f = carve(o, [128, 2, 129], F32); o += 2 * 129 * 4
                STg_b = carve(o, [128, 2, 130], BF16); o += 2 * 130 * 2
                o = (o + 3) // 4 * 4
                sj = carve(o, [128, 128], BF16); o += 256
                TSTgf = T_("STgf"); TSTgb = T_("STgb"); Tsj = T_("sj")
                assert o <= ARENA_BYTES, o
                if pre:
                    V2_r = V2_pre
                    STg_f = STgf_pre
                et_i = [0]
                for t in range(SBT):
                    gt = gt0 + t
                    tok = slice(t * 128, (t + 1) * 128)
                    sbk = (t % 2) * 2
                    if not pre:
                        ETs = {}
                        for g in range(2):
                            for c in range(2):
                                kt = gt + c
                                kcol = slice(kt * 128, (kt + 1) * 128)
                                for par in range(2):
                                    kk = kaT_n if (g == par) else kaT_s
                                    pr = slice(par * 64, (par + 1) * 64)
                                    outp = ps[:, sbk + par, c * 256:(c + 1) * 256].rearrange("p (j q) -> p j q", j=2)
                                    MM(outp, kk[pr, kcol], qaT[pr, 2 * g:2 * g + 2, tok], True, True, [TkaT[kt], TqaT[t // 4]], [PB[sbk + par]])
                            for par in range(2):
                                Sb, TSb = Sb_r[par]
                                TT(Sb[:], ps[:, sbk + par, :], BTall[:, g, par, :, :, :].rearrange("p c j q -> p (c j q)"), ALU.add, [PB[sbk + par], TBT], [TSb])
                                ET, TET = ET_r[et_i[0] % 8]; et_i[0] += 1
                                if gt == 0:
                                    ACT(ET[:, 0:256], Sb[:, 0:256], AF.Exp, [TSb, Tflag], [TET], bias=flag[:, 0:1])
                                    ACT(ET[:, 256:512], Sb[:, 256:512], AF.Exp, [TSb], [TET])
                                else:
                                    ACT(ET[:], Sb[:], AF.Exp, [TSb], [TET])
                                ETs[(g, par)] = (ET, TET)
                        for g in range(2):
                            bY, bD = 4 + g, 6 + g
                            for par in range(2):
                                ET, TET = ETs[(g, par)]
                                yv = ps[:, bY, :].rearrange("p (j q) -> p j q", j=4)[:, par::2, :]
                                dv = ps[:, bD, :].rearrange("p (j q) -> p j q", j=4)[:, par::2, :]
                                for c in range(2):
                                    kt = gt + c
                                    MM(yv, va[:, kt, g, :, :].rearrange("p a d -> p (a d)"), ET[:, c * 256:(c + 1) * 256].rearrange("p (j q) -> p j q", j=2),
                                       c == 0, c == 1, [Tva[kt], TET], [PB[bY]])
                                for c in range(2):
                                    MM(dv, ones_bf[:], ET[:, c * 256:(c + 1) * 256].rearrange("p (j q) -> p j q", j=2), c == 0, False, [Tones, TET], [PB[bD]])
                                MM(dv, ones_bf[0:1, :], esink[0:1, 4 * g + par:4 * g + 4:2, :], False, True, [Tones, Tesink], [PB[bD]])
                            rden, Trden = rden_r[g]
                            ACT(rden[:], ps[:, bD, :], AF.Ln, [PB[bD]], [Trden])
                            ACT(rden[:], rden[:], AF.Exp, [Trden], [Trden], scale=-1.0)
                            for par in range(2):
                                pr = slice(par * 64, (par + 1) * 64)
                                yv = ps[pr, bY, :].rearrange("p (j q) -> p j q", j=4)[:, par::2, :]
                                rv = rden[pr, :].rearrange("p (j q) -> p j q", j=4)[:, par::2, :]
                                TT(yattT[pr, 2 * g:2 * g + 2, tok], yv, rv, ALU.mult, [PB[bY], Trden], [TyaT[t]])
                for t in range(SBT):
                    gt = gt0 + t
                    tok = slice(t * 128, (t + 1) * 128)
                    bMS, bHO, bSU, bTR = (0, 2, 0, 1) if t % 2 == 0 else (4, 6, 4, 5)
                    if not pre:
                        for h in range(4):
                            p, par = h // 2, h % 2
                            pr = slice(par * 64, (par + 1) * 64)
                            MM(ps[:, bMS + par, p * 128:(p + 1) * 128], kmT[pr, p, tok], qmT[pr, p, tok], True, True, [TkmT[t // 4], TqmT[t // 4]], [PB[bMS + par]])
                        Wt, TWt = Wt_r[t % 2]
                        Wv = Wt[:].rearrange("s (p r) q -> s r p q", r=2)
                        for par in range(2):
                            TT(Wv[:, par, :, :], ps[:, bMS + par, 0:256].rearrange("s (p q) -> s p q", p=2),
                               caus[:].unsqueeze(1).to_broadcast([128, 2, 128]), ALU.mult, [PB[bMS + par], Tcaus], [TWt])
                    V2, TV2 = V2_r[t % 2]
                    TT(V2[:, :, 0:129], vm1[:, t, :, 0:129], ecol[:, t, 0:4].unsqueeze(2).to_broadcast([128, 4, 129]), ALU.mult, [Tvm1[t], Tecol], [TV2])
                    TT(STg_f[:], ST[:], gbb[:, t, :].unsqueeze(2).to_broadcast([128, 2, 129]), ALU.mult, [TST, Tgbb], [TSTgf])
                    if not pre:
                        ACT(STg_b[:, :, 0:129], STg_f[:], AF.Copy, [TSTgf], [TSTgb])
                        for h in (0, 2, 1, 3):
                            p, par = h // 2, h % 2
                            pr = slice(par * 64, (par + 1) * 64)
                            bb = bHO + h // 2
                            cc = (h % 2) * 256
                            MM(ps[:, bb, cc:cc + 129], Wt[:, h, :], V2[:, h, 0:129], True, False, [TWt, TV2], [PB[bb]])
                            MM(ps[:, bb, cc:cc + 129], qmT[pr, p, tok], STg_b[pr, p, 0:129], False, True, [TqmT[t // 4], TSTgb], [PB[bb]])
                    for p in range(2):
                        for par in range(2):
                            h = 2 * p + par
                            cc = par * 256
                            MM(ps[:, bSU + p, cc:cc + 129], km_tok[:, t, p * 128:(p + 1) * 128], V2[:, h, 0:129], True, True, [Tkmtok[t], TV2], [PB[bSU + p]])
                    for par in range(2):
                        pr = slice(par * 64, (par + 1) * 64)
                        cc = par * 256
                        TT(ST[pr, :, :], STg_f[pr, :, :], ps[pr, bSU:bSU + 2, cc:cc + 129], ALU.add, [TSTgf, PB[bSU], PB[bSU + 1]], [TST])
                    if pre:
                        yield "C"
                    if not pre:
                        HOv = ps[:, bHO:bHO + 2, :].rearrange("p b (c x) -> p (b c) x", c=2)
                        d4, Td4 = d4ring.next()
                        CP(d4[:, 0:4], HOv[:, :, 128], [PB[bHO], PB[bHO + 1]], [Td4])
                        STT(d4[:, 0:4], d4[:, 0:4], -1.0, d4[:, 0:4], ALU.mult, ALU.max, [Td4], [Td4])
                        TT(d4[:, 0:4], d4[:, 0:4], ecol[:, t, 4:8], ALU.max, [Td4, Tecol], [Td4])
                        S.op("dve", lambda e, d4=d4: e.reciprocal(d4[:, 0:4], d4[:, 0:4]), [Td4], [Td4])
                        for h in range(4):
                            ACT(sj[:], HOv[:, h, 0:128], AF.Square, [PB[bHO], PB[bHO + 1], Td4], [Tsj, Td4], scale=d4[:, h:h + 1], accum_out=d4[:, 4 + h:5 + h])
                        ACT(d4[:, 8:12], d4[:, 4:8], AF.Ln, [Td4, Teps], [Td4], scale=1.0 / 128, bias=eps_c[:, 0:1])
                        ACT(d4[:, 8:12], d4[:, 8:12], AF.Exp, [Td4], [Td4], scale=-0.5)
                        TT(d4[:, 8:12], d4[:, 8:12], d4[:, 0:4], ALU.mult, [Td4], [Td4])
                        ym, Tym = ym_r[t % 2]
                        for h in range(4):
                            STT(ym[:, h * 128:(h + 1) * 128], HOv[:, h, 0:128], d4[:, 8 + h:9 + h], og[:, t, h * 128:(h + 1) * 128], ALU.mult, ALU.mult,
                                [PB[bHO], PB[bHO + 1], Td4, Tog[t]], [Tym])
                        pvb = ps_bf(bTR)
                        for h in range(4):
                            TR(pvb[:, h * 128:(h + 1) * 128], ym[:, h * 128:(h + 1) * 128], ident_bf[:], [Tym, Tidb], [PB[bTR]])
                        ACT(ymT[:, :, tok], pvb[:, 0:512].rearrange("p (h q) -> p h q", h=4), AF.Copy, [PB[bTR]], [TymT[t]])
                dbg(f"yattT{sbi}", yattT[:], TyaT[SBT - 1], [128, 4, TSB], BF16)
                dbg(f"ymT{sbi}", ymT[:], TymT[SBT - 1], [128, 4, TSB], BF16)

                if pre:
                    return
                if sbi == 0:
                    sample_phase()
                stage(f"C{sbi}")
                S.barrier()
                samp = (sbi == 0)
                o = 0
                x1 = carve(o, [128, SBT, D], F32); o += SBT * D * 4
                Tx1 = [Tile(f"x1_{t}") for t in range(SBT)]
                x1s = carve(o, [16, D], F32); o += D * 4
                Tx1s = Tile("x1s")
                sga = carve(o, [128, 8, 512], BF16); o += 8192
                sgm = carve(o, [128, 8, 512], BF16); o += 8192
                mixT = carve(o, [128, 8, 512], BF16); o += 8192
                tmpD = [(carve(o + i * 2048, [128, 512], F32), Tile("tmpD")) for i in range(2)]; o += 4096
                sga_s = carve(o, [128, 8, 16], BF16); o += 256
                sgm_s = carve(o, [128, 8, 16], BF16); o += 256
                mix_s = carve(o, [128, 8, 16], BF16); o += 256
                aT_s = carve(o, [128, 2, 16], BF16); o += 64
                e_off = o
                assert o <= ARENA_BYTES
                Tsga = [Tile("sga") for _ in range(8)]
                Tsgm = [Tile("sgm") for _ in range(8)]
                Tmix = [Tile("mix") for _ in range(8)]
                Tsga_s = [Tile("sga_s") for _ in range(8)]
                Tsgm_s = [Tile("sgm_s") for _ in range(8)]
                Tmix_s = [Tile("mix_s") for _ in range(8)]
                TaT_s = Tile("aT_s")

                def subblocks(blk):
                    btok = slice(blk * 512, (blk + 1) * 512)
                    subs = [dict(n=512, hv=hT_[:, :, btok], hrd=ThT_[blk * 4:(blk + 1) * 4],
                                 yav=yattT[:, :, btok], yard=TyaT[blk * 4:(blk + 1) * 4],
                                 ymv=ymT[:, :, btok], ymrd=TymT[blk * 4:(blk + 1) * 4],
                                 sga=sga, sgm=sgm, mix=mixT, Tsga=Tsga, Tsgm=Tsgm, Tmix=Tmix,
                                 tiles=[(x1[:, blk * 4 + tt, :], Tx1[blk * 4 + tt], 128, slice(tt * 128, (tt + 1) * 128)) for tt in range(4)])]
                    if samp and blk == 0:
                        subs.append(dict(n=16, hv=hTs[:], hrd=[ThTs], yav=yattTs[:], yard=[TyaTs], ymv=ymTs[:], ymrd=[TymTs],
                                         sga=sga_s, sgm=sgm_s, mix=mix_s, Tsga=Tsga_s, Tsgm=Tsgm_s, Tmix=Tmix_s,
                                         tiles=[(x1s[:], Tx1s, 16, slice(0, 16))]))
                    return subs

                if samp:
                    DMA("sp", x1s[:], xsm_d, [], [Tx1s])
                for blk in range(NBLK):
                    cd = c_D[blk]
                    subs = subblocks(blk)
                    for tt in range(4):
                        t = blk * 4 + tt
                        r0 = 128 + (gt0 + t) * 128
                        DMA("sp", x1[:, t, :], xs_d[r0:r0 + 128, :], [], [Tx1[t]])
                    for nm in ("ga", "gm"):
                        for half in range(2):
                            wv, Tw = W.get(cd[f"{nm}{half}"])
                            for jj in range(4):
                                j = half * 4 + jj
                                for sub in subs:
                                    n = sub["n"]
                                    dst, Td = (sub["sga"], sub["Tsga"]) if nm == "ga" else (sub["sgm"], sub["Tsgm"])
                                    bb = nextbank()
                                    for k in range(8):
                                        MM(ps[:, bb, 0:n], wv[:, k, jj * 128:(jj + 1) * 128], sub["hv"][:, k, :], k == 0, k == 7, [Tw] + sub["hrd"], [PB[bb]])
                                    ACT(dst[:, j, 0:n], ps[:, bb, 0:n], AF.Sigmoid, [PB[bb]], [Td[j]])
                    wva, Twa = W.get(cd["ao"])
                    wvm, Twm = W.get(cd["mo"], live_from=cd["ao"])
                    for j in range(8):
                        for sub in subs:
                            n = sub["n"]
                            bb = nextbank()
                            for k in range(4):
                                MM(ps[:, bb, 0:n], wva[:, k, j * 128:(j + 1) * 128], sub["yav"][:, k, :], k == 0, k == 3, [Twa] + sub["yard"], [PB[bb]])
                            tmp, Ttmp = tmpD[j % 2]
                            TT(tmp[:, 0:n], ps[:, bb, 0:n], sub["sga"][:, j, 0:n], ALU.mult, [PB[bb], sub["Tsga"][j]], [Ttmp])
                            b2 = nextbank()
                            for k in range(4):
                                MM(ps[:, b2, 0:n], wvm[:, k, j * 128:(j + 1) * 128], sub["ymv"][:, k, :], k == 0, k == 3, [Twm] + sub["ymrd"], [PB[b2]])
                            TT(sub["mix"][:, j, 0:n], ps[:, b2, 0:n], sub["sgm"][:, j, 0:n], ALU.mult, [PB[b2], sub["Tsgm"][j]], [sub["Tmix"][j]])
                            TT(sub["mix"][:, j, 0:n], sub["mix"][:, j, 0:n], tmp[:, 0:n], ALU.add, [sub["Tmix"][j], Ttmp], [sub["Tmix"][j]])
                    for half in range(2):
                        wv, Tw = W.get(cd[f"wo{half}"])
                        for sub in subs:
                            for (xa, Txa, tn, tsl) in sub["tiles"]:
                                bb = nextbank()
                                for j in range(8):
                                    MM(ps[0:tn, bb, :], sub["mix"][:, j, tsl], wv[:, j, :], j == 0, j == 7, [Tw, sub["Tmix"][j]], [PB[bb]])
                                TT(xa[:, half * 512:(half + 1) * 512], xa[:, half * 512:(half + 1) * 512], ps[0:tn, bb, :], ALU.add, [PB[bb], Txa], [Txa])
                dbg(f"x1_{sbi}", x1[:], Tx1[SBT - 1], [128, SBT, D])
                dbg("x1s", x1s[:], Tx1s, [16, D])

                stage(f"D{sbi}")
                o = e_off
                actT = [(carve(o + i * 2048, [128, 2, 512], BF16), Tile("actT")) for i in range(2)]; o += 4096
                sil = [(carve(o + i * 2048, [128, 512], F32), Tile("sil")) for i in range(2)]; o += 4096
                yo = [(carve(o + i * 4096, [128, D], F32), Tile("yo")) for i in range(2)]; o += 8192
                Tx1b = [Tile(f"x1b_{t}") for t in range(SBT)]
                assert o <= ARENA_BYTES, o
                for t in range(SBT):
                    norm_T(x1[:, t, :], [Tx1[t]], g_ffn, Tgffn, hT_[:, :, t * 128:(t + 1) * 128], ThT_[t], src_is_dram=False)
                if samp:
                    norm_T(x1s[:], [Tx1s], g_ffn, Tgffn, hTs[:], ThTs, tn=16, src_is_dram=False)
                for gi, (f0, nf) in enumerate(FG):
                    cg, cu, cdn = c_E[gi]
                    wg, Twg = WE.get(cg)
                    wu, Twu = WE.get(cu, live_from=cg)
                    wd, Twd = WE.get(cdn, live_from=cg)
                    for blk in range(NBLK):
                        for si, sub in enumerate(subblocks(blk)):
                            n = sub["n"]
                            if si == 0:
                                aT, TaT = actT[(gi * NBLK + blk) % 2]
                            else:
                                aT, TaT = aT_s, TaT_s
                            for c in range(nf):
                                bb = nextbank()
                                for k in range(8):
                                    MM(ps[:, bb, 0:n], wg[:, k, c * 128:(c + 1) * 128], sub["hv"][:, k, :], k == 0, k == 7, [Twg] + sub["hrd"], [PB[bb]])
                                b2 = nextbank()
                                for k in range(8):
                                    MM(ps[:, b2, 0:n], wu[:, k, c * 128:(c + 1) * 128], sub["hv"][:, k, :], k == 0, k == 7, [Twu] + sub["hrd"], [PB[b2]])
                                sl, Tsl = sil[c % 2]
                                ACT(sl[:, 0:n], ps[:, bb, 0:n], AF.Silu, [PB[bb]], [Tsl])
                                TT(aT[:, c, 0:n], sl[:, 0:n], ps[:, b2, 0:n], ALU.mult, [Tsl, PB[b2]], [TaT])
                            for ti_, (xa, Txa, tn, tsl) in enumerate(sub["tiles"]):
                                for half in range(2):
                                    bb = nextbank()
                                    for c in range(nf):
                                        MM(ps[0:tn, bb, :], aT[:, c, tsl], wd[:, c, half * 512:(half + 1) * 512], c == 0, c == nf - 1, [TaT, Twd], [PB[bb]])
                                    if True:
                                        TT(xa[:, half * 512:(half + 1) * 512], xa[:, half * 512:(half + 1) * 512], ps[0:tn, bb, :], ALU.add, [PB[bb], Txa], [Txa])
                stage(f"E{sbi}")
                fin = [(x1[:, t, :], [Tx1[t], Tx1b[t]], 128, y_d[(gt0 + t) * 128:(gt0 + t + 1) * 128, :]) for t in range(SBT)]
                if samp:
                    fin.append((x1s[:], [Tx1s], 16, ys_d))
                for fi, (xa, Txa, tn, dst) in enumerate(fin):
                    yb, Tyb = yo[fi % 2]
                    ss, Tss = newscal()
                    ACT(yb[0:tn, :], xa, AF.Square, Txa, [Tyb, Tss], accum_out=ss[0:tn, :])
                    rr, Trr = newscal()
                    ACT(rr[0:tn, :], ss[0:tn, :], AF.Ln, [Tss, Teps], [Trr], scale=1.0 / D, bias=eps_c[0:tn, 0:1])
                    ACT(rr[0:tn, :], rr[0:tn, :], AF.Exp, [Trr], [Trr], scale=-0.5)
                    STT(yb[0:tn, :], xa, rr[0:tn, 0:1], g_fin[0:tn, :], ALU.mult, ALU.mult, Txa + [Trr, Tgfin], [Tyb])
                    DMA("sp", dst, yb[0:tn, :], [Tyb], [Tile("y_d")])
                S.barrier()

            pact = sb("pact", [128, 4]); Tpact = Tile("pact")
            pre_ring[0] = VRing("wP_", 3, 4096)
            DMA("sp", pact[:], pact_d, [], [Tpact])
            in_pre[0] = True

            def boundary(j):
                TT(rstate[:, 1:2], rstate[:, 1:2], rstate[:, 0:1], ALU.subtract, [Trst], [Trst])
                TS(rstate[:, 1:2], rstate[:, 1:2], pact[0:4, j:j + 1], None, ALU.mult, ALU.bypass, [Trst, Tpact], [Trst])
                S.op("dve", lambda e: e.memset(rstate[:, 0:1], 0.0), [], [Trst])
                TS(ST[:], ST[:], pact[:, j:j + 1], None, ALU.mult, ALU.bypass, [TST, Tpact], [TST])

            gens = []
            for j in range(3):
                for sbi in range(NSB):
                    k = j * NSB + sbi
                    gens.append(superblock(sbi, True, ST, TST, rstate, Trst,
                                           lambda gt, j=j: xprev_d[(j * NT + gt) * 128:(j * NT + gt + 1) * 128, :], par_=k % 2))
            def run_until(g, tag):
                for x in g:
                    if x == tag:
                        return True
                return False

            run_until(gens[0], "front")
            for k in range(len(gens)):
                gk = gens[k]
                gn = gens[k + 1] if k + 1 < len(gens) else None
                run_until(gk, "rows")
                for t in range(SBT):
                    if gn is not None:
                        run_until(gn, "A")
                    run_until(gk, "C")
                for _ in gk:
                    pass
                if gn is not None:
                    run_until(gn, "front")
                if k % NSB == NSB - 1:
                    boundary(k // NSB)
            in_pre[0] = False
            S.barrier()
            dbg("STpre", ST[:], TST, [128, 2, 129])
            dbg("rst", rstate[:], Trst, [4, 4])
            stage("pre")
            for sbi in range(NSB):
                for _ in superblock(sbi, False, ST, TST, rstate, Trst, lambda gt: xs_d[128 + gt * 128:128 + (gt + 1) * 128, :]):
                    pass

            Cout = sb("Cout", [128, 4, 64]); TCout = Tile("Cout")
            for h in range(4):
                p, par = h // 2, h % 2
                pr = slice(par * 64, (par + 1) * 64)
                TR(ps[:, par, p * 64:(p + 1) * 64], ST[pr, p, 0:128], ident_f[pr, pr], [TST, Tidf], [PB[par]])
            for par in range(2):
                CP(Cout[:, par::2, :], ps[:, par, 0:128].rearrange("v (p d) -> v p d", p=2), [PB[par]], [TCout])
            DMA("sp", pC_d.rearrange("h v d -> v h d"), Cout[:], [TCout], [Tile("pC_d")])
            for h in range(4):
                p, par = h // 2, h % 2
                pr = slice(par * 64, (par + 1) * 64)
                DMA("sp", AP(pn_d.tensor, h * 64, [[1, 64], [1, 1]]), ST[pr, p, 128:129], [TST], [Tile("pn_d")])
            mo = sb("mo", [4, 1]); Tmo = Tile("mo")
            TT(mo[:], rstate[:, 1:2], rstate[:, 0:1], ALU.subtract, [Trst], [Tmo])
            DMA("sp", pm_out_d, mo[:], [Tmo], [Tile("pm_d")])

        except _Stop:
            pass
        S.finish()
        S.emit()
    return nc, dbg_outs


_CACHE = {}


def _consts():
    ident = np.eye(128, dtype=np.float32)
    dist_rev = 127 - np.arange(128)
    bk = t5_bucket_np(dist_rev)
    ohT_rev = (np.arange(32)[:, None] == bk[None, :]).astype(np.float32)
    causT = (np.arange(128)[:, None] <= np.arange(128)[None, :]).astype(np.float32)
    sel = np.zeros((4, 128), np.float32)
    for h in range(4):
        sel[h, (h % 2) * 64:(h % 2) * 64 + 64] = 1.0
    pm = np.zeros((4, 2), np.float32)
    for h in range(4):
        pm[h, h // 2] = 1.0
    return dict(ident_bf=ident.astype(ml_dtypes.bfloat16), ident_f=ident, ohT_rev=ohT_rev, causT=causT, sel=sel, pm=pm)


def kernel(x_prompt, x_sample, cache_k_win, cache_v_win, state_mlstm_C, state_mlstm_n, state_mlstm_m,
           rel_bias, w_in, b_if, sinks, g_attn_norm, g_head, w_att_out, w_mlstm_out, w_out,
           g_ffn_norm, w_gate, w_up, w_down, g_final, _debug=(), _stop=None, _trace=False):
    f32 = np.float32
    x_prompt = np.asarray(x_prompt, f32)
    x_sample = np.asarray(x_sample, f32)
    cache_k_win = np.asarray(cache_k_win, f32)
    cache_v_win = np.asarray(cache_v_win, f32)
    state_mlstm_C = np.asarray(state_mlstm_C, f32)
    state_mlstm_n = np.asarray(state_mlstm_n, f32)
    state_mlstm_m = np.asarray(state_mlstm_m, f32)
    key = (tuple(_debug), _stop)
    if key not in _CACHE:
        _CACHE[key] = build_program(debug=_debug, stop=_stop)
    nc, dbg_outs = _CACHE[key]
    cst = _consts()
    shared = dict(
        w_in=np.ascontiguousarray(np.asarray(w_in, f32)[0]),
        b_if=np.ascontiguousarray(np.asarray(b_if, f32)[0]),
        sinks=np.ascontiguousarray(np.asarray(sinks, f32)),
        rel_bias=np.ascontiguousarray(np.asarray(rel_bias, f32)),
        g_attn=np.ascontiguousarray(np.asarray(g_attn_norm, f32)),
        g_head=np.ascontiguousarray(np.asarray(g_head, f32)),
        g_ffn=np.ascontiguousarray(np.asarray(g_ffn_norm, f32)),
        g_final=np.ascontiguousarray(np.asarray(g_final, f32)[None]),
        w_att_out=np.ascontiguousarray(np.asarray(w_att_out, f32)[0]),
        w_mlstm_out=np.ascontiguousarray(np.asarray(w_mlstm_out, f32)[0]),
        w_out=np.ascontiguousarray(np.asarray(w_out, f32)[0]),
        w_gate=np.ascontiguousarray(np.asarray(w_gate, f32)[0]),
        w_up=np.ascontiguousarray(np.asarray(w_up, f32)[0]),
        w_down=np.ascontiguousarray(np.asarray(w_down, f32)[0]),
        **cst,
    )
    in_maps = []
    for c in range(NCORE):
        b, s = c // 4, c % 4
        xs = np.zeros((128 + SEG, D), f32)
        xs[128:] = x_prompt[b, s * SEG:(s + 1) * SEG]
        if s > 0:
            xs[:128] = x_prompt[b, s * SEG - 128:s * SEG]
        flag = np.full((128, 1), NEGB if s == 0 else 0.0, f32)
        m = dict(shared)
        m["xs"] = xs
        m["flag"] = flag
        xprev = np.zeros((3 * SEG, D), f32)
        pact = np.zeros((128, 4), f32)
        for j in range(3):
            sj = s - 3 + j
            if sj >= 0:
                xprev[j * SEG:(j + 1) * SEG] = x_prompt[b, sj * SEG:(sj + 1) * SEG]
                pact[:, j] = 1.0
        m["xprev"] = xprev
        m["pact"] = pact
        sl = slice(c * 16, (c + 1) * 16)
        m["xsm"] = np.ascontiguousarray(x_sample[sl, 0, :])
        m["ck"] = np.ascontiguousarray(cache_k_win[0, sl].reshape(16, 128, 128))
        m["cv"] = np.ascontiguousarray(cache_v_win[0, sl].reshape(16, 128, 128))
        m["sC"] = np.ascontiguousarray(state_mlstm_C[0, sl])
        m["sn"] = np.ascontiguousarray(state_mlstm_n[0, sl])
        m["sm"] = np.ascontiguousarray(state_mlstm_m[0, sl])
        in_maps.append(m)
    res = run_bass_kernel_spmd(nc, in_maps, core_ids=list(range(NCORE)), **({'trace': True} if _trace else {}))
    if _trace:
        print('EXEC_TIME_NS', res.exec_time_ns)
    R = res.results
    y_prompt = np.stack([np.concatenate([R[b * 4 + s]["y"] for s in range(4)], axis=0) for b in range(2)])
    p_k = np.stack([R[b * 4 + 3]["pk"].reshape(128, 2, 64) for b in range(2)])[None]
    p_v = np.stack([R[b * 4 + 3]["pv"].reshape(128, 2, 64) for b in range(2)])[None]
    p_C = np.stack([R[b * 4 + 3]["pC"] for b in range(2)])[None]
    p_n = np.stack([R[b * 4 + 3]["pn"] for b in range(2)])[None]
    p_m = np.stack([R[b * 4 + 3]["pm_out"].reshape(4) for b in range(2)])[None]
    y_sample = np.concatenate([R[c]["ys"] for c in range(NCORE)], axis=0)[:, None, :]
    s_k = np.concatenate([R[c]["sko"].reshape(16, 128, 2, 64) for c in range(NCORE)], axis=0)[None]
    s_v = np.concatenate([R[c]["svo"].reshape(16, 128, 2, 64) for c in range(NCORE)], axis=0)[None]
    s_C = np.concatenate([R[c]["sCo"] for c in range(NCORE)], axis=0)[None]
    s_n = np.concatenate([R[c]["sno"][:, :, 0, :] for c in range(NCORE)], axis=0)[None]
    s_m = np.concatenate([R[c]["smo"][:, :, 0] for c in range(NCORE)], axis=0)[None]
    outs = (y_prompt, y_sample, p_k, p_v, p_C, p_n, p_m, s_k, s_v, s_C, s_n, s_m)
    if _debug:
        return outs, [{k: r["dbg_" + k] for k in dbg_outs} for r in R]
    return outs
```

```python
import contextlib
import math
import numpy as np
import ml_dtypes
import concourse.bass as bass
import concourse.mybir as mybir
from concourse.ap import AP
from concourse.bass_utils import run_bass_kernel_spmd

F32 = mybir.dt.float32
BF16 = mybir.dt.bfloat16
AF = mybir.ActivationFunctionType
ALU = mybir.AluOpType

D = 1024
SEQ = 8192
NCORE = 8
SEG = 2048
NT = 16
SBT = 8
NSB = NT // SBT
TSB = SBT * 128
NBLK = TSB // 512
N_IN = 4360
DFF = 2816
NFF = DFF // 128
EPS = 1e-6
NEGB = -30000.0
O_QA, O_KA, O_VA, O_QM, O_KM, O_VM, O_OM, O_I, O_F, O_GA, O_GM = 0, 512, 640, 768, 1024, 1280, 1792, 2304, 2308, 2312, 3336


class Tile:
    __slots__ = ("name", "w", "r", "excl")

    def __init__(self, name, excl=False):
        self.name = name
        self.w = None
        self.r = {}
        self.excl = excl


class Sched:
    def __init__(self, nc, sems, lanes_per_q=8):
        self.nc = nc
        self.engs = {"pe": nc.tensor, "act": nc.scalar, "dve": nc.vector, "pool": nc.gpsimd, "sp": nc.sync}
        self.q = {k: [] for k in self.engs}
        self.cnt = {k: 0 for k in self.engs}
        it = iter(sems)
        self.sem = {k: next(it) for k in self.engs}
        self.lanes = {}
        for k in ("sp", "pool"):
            self.lanes[k] = [[next(it), 0] for _ in range(lanes_per_q)]
        self.lane_i = {k: 0 for k in self.lanes}
        self.seen = {k: {} for k in self.engs}

    def _need(self, eng, waits, ev, same_ok=False):
        if ev is None:
            return
        sem, val, src = ev
        if src == eng and (same_ok or eng == "pe"):
            return
        key = id(sem)
        if self.seen[eng].get(key, 0) >= val:
            return
        cur = waits.get(key)
        if cur is None or cur[1] < val:
            waits[key] = (sem, val)

    def _deps(self, eng, reads, writes):
        waits = {}
        for t in reads:
            self._need(eng, waits, t.w)
        for t in writes:
            self._need(eng, waits, t.w, same_ok=True)
            for ev in t.r.values():
                self._need(eng, waits, ev, same_ok=True)
        for key, (sem, val) in waits.items():
            self.seen[eng][key] = val
        return list(waits.values())

    def op(self, eng, fn, reads=(), writes=()):
        if any(t.excl for t in reads):
            writes = list(writes) + [t for t in reads if t.excl]
            reads = [t for t in reads if not t.excl]
        waits = self._deps(eng, reads, writes)
        self.cnt[eng] += 1
        sem = self.sem[eng]
        ev = (sem, self.cnt[eng], eng)
        self.q[eng].append((waits, fn, (sem, 1)))
        for t in reads:
            t.r[id(sem)] = ev
        for t in writes:
            t.w = ev
            t.r = {}
        return ev

    def dma(self, q, out_ap, in_ap, reads=(), writes=(), **kw):
        waits = self._deps(q, reads, writes)
        lanes = self.lanes[q]
        li = self.lane_i[q]
        self.lane_i[q] = (li + 1) % len(lanes)
        lane = lanes[li]
        sem = lane[0]
        if lane[1] > 0 and self.seen[q].get(id(sem), 0) < lane[1]:
            waits.append((sem, lane[1]))
            self.seen[q][id(sem)] = lane[1]
        lane[1] += 16
        ev = (sem, lane[1], "dma")

        def fn(e, out_ap=out_ap, in_ap=in_ap, kw=kw):
            return e.dma_start(out=out_ap, in_=in_ap, **kw)

        self.q[q].append((waits, fn, (sem, 16)))
        for t in reads:
            t.r[id(sem)] = ev
        for t in writes:
            t.w = ev
            t.r = {}
        return ev

    def barrier(self):
        evs = []
        for k in self.engs:
            if self.cnt[k] > 0:
                evs.append((self.sem[k], self.cnt[k], k))
        for k, lanes in self.lanes.items():
            for sem, val in lanes:
                if val > 0:
                    evs.append((sem, val, "dma"))
        for k in self.engs:
            waits = []
            for sem, val, src in evs:
                if src == k:
                    continue
                if self.seen[k].get(id(sem), 0) >= val:
                    continue
                self.seen[k][id(sem)] = val
                waits.append((sem, val))
            if waits:
                self.q[k].append((waits, None, None))

    def finish(self):
        waits = []
        for k, lanes in self.lanes.items():
            for sem, val in lanes:
                if val > 0:
                    waits.append((sem, val))
        self.q["sp"].append((waits, None, None))

    def emit(self):
        nc = self.nc
        with nc.Block() as block:
            def mk(name):
                def body(e):
                    for waits, fn, inc in self.q[name]:
                        ws = list(waits)
                        if fn is None:
                            for sem, val in ws:
                                e.wait_ge(sem, val)
                            continue
                        for sem, val in ws[:-1]:
                            e.wait_ge(sem, val)
                        ins = fn(e)
                        if ws:
                            ins._wait_ge(ws[-1][0], ws[-1][1])
                        ins.then_inc(inc[0], inc[1])
                return body
            block.tensor(mk("pe"))
            block.scalar(mk("act"))
            block.vector(mk("dve"))
            block.gpsimd(mk("pool"))
            block.sync(mk("sp"))


def t5_bucket_np(n):
    n = np.maximum(n, 0)
    max_exact = 16
    nf = np.maximum(n, 1).astype(np.float32)
    large = max_exact + (np.log(nf / max_exact) / math.log(128 / max_exact) * (32 - max_exact)).astype(np.int32)
    large = np.minimum(large, 31)
    return np.where(n < max_exact, n, large)


class _Stop(Exception):
    pass


def build_program(debug=(), stop=None, nonce=0.0):
    nc = bass.Bass("TRN2", target_bir_lowering=False)
    dbg_outs = {}

    def stage(name):
        if stop == name:
            raise _Stop()

    def din(name, shape, dt=F32):
        return nc.dram_tensor(name, list(shape), dt, kind="ExternalInput")

    def dout(name, shape, dt=F32):
        return nc.dram_tensor(name, list(shape), dt, kind="ExternalOutput")

    xs_d = din("xs", [128 + SEG, D]).ap()
    flag_d = din("flag", [128, 1]).ap()
    w_in_d = din("w_in", [D, N_IN]).ap()
    b_if_d = din("b_if", [2, 4]).ap()
    sinks_d = din("sinks", [1, 8]).ap()
    relb_d = din("rel_bias", [32, 8]).ap()
    g_attn_d = din("g_attn", [1, D]).ap()
    g_head_d = din("g_head", [1, 512]).ap()
    g_ffn_d = din("g_ffn", [1, D]).ap()
    g_fin_d = din("g_final", [1, D]).ap()
    w_ao_d = din("w_att_out", [512, D]).ap()
    w_mo_d = din("w_mlstm_out", [512, D]).ap()
    w_out_d = din("w_out", [D, D]).ap()
    w_gate_d = din("w_gate", [D, DFF]).ap()
    w_up_d = din("w_up", [D, DFF]).ap()
    w_down_d = din("w_down", [DFF, D]).ap()
    ident_bf_d = din("ident_bf", [128, 128], BF16).ap()
    ident_f_d = din("ident_f", [128, 128]).ap()
    ohT_d = din("ohT_rev", [32, 128]).ap()
    caus_d = din("causT", [128, 128]).ap()
    sel_d = din("sel", [4, 128]).ap()
    pm_d = din("pm", [4, 2]).ap()
    xsm_d = din("xsm", [16, D]).ap()
    ck_d = din("ck", [16, 128, 128])
    cv_d = din("cv", [16, 128, 128])
    sC_d = din("sC", [16, 4, 128, 64])
    sn_d = din("sn", [16, 4, 64])
    sm_d = din("sm", [16, 4])
    xprev_d = din("xprev", [3 * SEG, D]).ap()
    pact_d = din("pact", [128, 4]).ap()

    y_d = dout("y", [SEG, D]).ap()
    pk_d = dout("pk", [128, 128]).ap()
    pv_d = dout("pv", [128, 128]).ap()
    pC_d = dout("pC", [4, 128, 64]).ap()
    pn_d = dout("pn", [4, 64]).ap()
    pm_out_d = dout("pm_out", [4, 1]).ap()
    ys_d = dout("ys", [16, D]).ap()
    sko_d = dout("sko", [16, 128, 128])
    svo_d = dout("svo", [16, 128, 128])
    sCo_d = dout("sCo", [16, 4, 128, 64])
    sno_d = dout("sno", [16, 4, 2, 64])
    smo_d = dout("smo", [16, 4, 2])
    wr_scr = nc.dram_tensor("wr_scr", [8, 512], F32)

    es = contextlib.ExitStack()
    with es:
        def sb(name, shape, dt=F32):
            return es.enter_context(nc.sbuf_tensor("sb_" + name, list(shape), dt))

        sems = [es.enter_context(nc.semaphore(f"s{i}")) for i in range(5 + 16)]
        S = Sched(nc, sems)
        ps = es.enter_context(nc.psum_tensor("ps", [128, 8, 512], F32))
        PB = [Tile(f"bank{i}", excl=True) for i in range(8)]
        bank_rr = [0]

        def nextbank():
            b = bank_rr[0]
            bank_rr[0] = (b + 1) % 8
            return b

        def ps_bf(b):
            return ps[:, b, :].bitcast(BF16)

        def ACT(out, in_, func, reads, writes, **kw):
            S.op("act", lambda e: e.activation(out, in_, func, **kw), reads, writes)

        def TT(out, a, b, op, reads, writes, eng="dve"):
            S.op(eng, lambda e: e.tensor_tensor(out, a, b, op), reads, writes)

        def TS(out, a, s1, s2, op0, op1, reads, writes, eng="dve"):
            S.op(eng, lambda e: e.tensor_scalar(out, a, s1, s2, op0, op1), reads, writes)

        def STT(out, a, scalar, b, op0, op1, reads, writes, eng="dve"):
            S.op(eng, lambda e: e.scalar_tensor_tensor(out, a, scalar, b, op0, op1), reads, writes)

        def CP(out, in_, reads, writes, eng="dve"):
            S.op(eng, lambda e: e.tensor_copy(out, in_), reads, writes)

        def MM(out, lhsT, rhs, start, stop, reads, writes):
            S.op("pe", lambda e: e.matmul(out, lhsT, rhs, start=start, stop=stop), reads, writes)

        def TR(out, in_, ident, reads, writes):
            S.op("pe", lambda e: e.transpose(out, in_, ident), reads, writes)

        def DMA(q, out, in_, reads, writes, **kw):
            S.dma(q, out, in_, reads, writes, **kw)

        class Ring:
            def __init__(self, name, shape, dt, n):
                self.bufs = [(sb(f"{name}{i}", shape, dt), Tile(f"{name}{i}")) for i in range(n)]
                self.i = 0

            def next(self):
                r = self.bufs[self.i]
                self.i = (self.i + 1) % len(self.bufs)
                return r

        in_pre = [False]
        pre_tiles = {}
        pre_ones_done = set()
        pre_ring = [None]

        def dbg(name, ap, tile, shape, dt=F32):
            if name not in debug or (in_pre[0] and name in dbg_outs):
                return
            o = dout("dbg_" + name, shape, dt).ap()
            dbg_outs[name] = o
            DMA("sp", o, ap, [tile], [Tile("dbgo_" + name)])

        ident_bf = sb("ident_bf", [128, 128], BF16); Tidb = Tile("idb")
        ident_f = sb("ident_f", [128, 128]); Tidf = Tile("idf")
        caus = sb("caus", [128, 128]); Tcaus = Tile("caus")
        ones_bf = sb("ones_bf", [128, 128], BF16); Tones = Tile("ones")
        one_c = sb("one_c", [128, 1]); Tonesr = Tile("onec")
        BTall = sb("BTall", [128, 2, 2, 2, 2, 128]); TBT = Tile("BTall")
        esink = sb("esink", [1, 8, 128], BF16); Tesink = Tile("esink")
        flag = sb("flag", [128, 1]); Tflag = Tile("flag")
        zero_c = sb("zero_c", [128, 1]); Tzero = Tile("zero")
        eps_c = sb("eps_c", [128, 1]); Teps = Tile("eps")
        g_attn = sb("g_attnT", [128, 8]); Tgattn = Tile("gattn")
        g_ffn = sb("g_ffnT", [128, 8]); Tgffn = Tile("gffn")
        g_fin = sb("g_fin", [128, D]); Tgfin = Tile("gfin")
        g_head = sb("g_head", [128, 512]); Tghead = Tile("ghead")
        b_i = sb("b_i", [4, 1]); b_f = sb("b_f", [4, 1]); Tbif = Tile("bif")
        selm = sb("selm", [4, 128]); pmm = sb("pmm", [4, 2]); Tsel = Tile("sel")
        kaT_n = sb("kaT_n", [128, 128 + SEG], BF16)
        kaT_s = sb("kaT_s", [128, 128 + SEG], BF16)
        TkaT = [Tile(f"kaT{t}") for t in range(NT + 1)]
        va = sb("va", [128, NT + 1, 2, 2, 64], BF16)
        Tva = [Tile(f"va{t}") for t in range(NT + 1)]
        ST = sb("ST", [128, 2, 129]); TST = Tile("ST")
        rstate = sb("rstate", [4, 4]); Trst = Tile("rstate")
        hT = sb("hT", [128, 8, TSB], BF16)
        ThT = [Tile(f"hT{t}") for t in range(SBT)]
        hTh = sb("hTh", [128, 8, 128], BF16); ThTh = Tile("hTh")
        yattT = sb("yattT", [128, 4, TSB], BF16)
        TyaT = [Tile(f"yaT{t}") for t in range(SBT)]
        ymT = sb("ymT", [128, 4, TSB], BF16)
        TymT = [Tile(f"ymT{t}") for t in range(SBT)]
        hTs = sb("hTs", [128, 8, 16], BF16); ThTs = Tile("hTs")
        zs = sb("zs", [16, 2312]); Tzs = Tile("zs")
        yattTs = sb("yattTs", [128, 4, 16], BF16); TyaTs = Tile("yaTs")
        ymTs = sb("ymTs", [128, 4, 16], BF16); TymTs = Tile("ymTs")
        wbuf = sb("wbuf", [128, 3 * 4096], BF16)

        class VRing:
            def __init__(self, name, n, size):
                self.bufs = [(wbuf[:, i * size:(i + 1) * size], Tile(f"{name}{i}")) for i in range(n)]
                self.i = 0
                self.n = n

            def next(self):
                r = self.bufs[self.i]
                self.i = (self.i + 1) % len(self.bufs)
                return r
        xring = Ring("xt", [128, D], F32, 3)
        hbring = Ring("hbf", [128, D], BF16, 2)
        d4ring = Ring("d4", [128, 12], F32, 4)
        scal = sb("scal", [128, 64]); scal_i = [0]
        Tscal = [Tile(f"scal{i}") for i in range(64)]

        def newscal():
            i = scal_i[0]
            scal_i[0] = (i + 1) % 64
            return scal[:, i:i + 1], Tscal[i]

        ARENA_BYTES = 83 * 1024
        arena = sb("arena", [128, ARENA_BYTES // 4], F32)

        def carve(off_bytes, shape, dt):
            esz = 2 if dt == BF16 else 4
            n = int(np.prod(shape[1:]))
            assert off_bytes % 4 == 0 and off_bytes + n * esz <= ARENA_BYTES, (off_bytes, shape)
            v = arena[0:shape[0], off_bytes // 4: off_bytes // 4 + (n * esz) // 4]
            if dt == BF16:
                v = v.bitcast(BF16)
            if len(shape) == 2:
                return v
            names = "abcdef"[: len(shape) - 1]
            pat = "p (" + " ".join(names) + ") -> p " + " ".join(names)
            kw = {names[i]: shape[i + 1] for i in range(len(shape) - 1)}
            return v.rearrange(pat, **kw)

        DMA("sp", ident_bf[:], ident_bf_d, [], [Tidb])
        DMA("sp", ident_f[:], ident_f_d, [], [Tidf])
        DMA("sp", caus[:], caus_d, [], [Tcaus])
        DMA("sp", flag[:], flag_d, [], [Tflag])
        DMA("sp", g_attn[:], AP(g_attn_d.tensor, 0, [[1, 128], [128, 8]]), [], [Tgattn], allow_slow_non_contiguous=True)
        DMA("sp", g_ffn[:], AP(g_ffn_d.tensor, 0, [[1, 128], [128, 8]]), [], [Tgffn], allow_slow_non_contiguous=True)
        DMA("sp", g_fin[:], AP(g_fin_d.tensor, 0, [[0, 128], [1, D]]), [], [Tgfin])
        DMA("sp", g_head[:], AP(g_head_d.tensor, 0, [[0, 128], [1, 512]]), [], [Tghead])
        DMA("sp", b_i[:], AP(b_if_d.tensor, 0, [[1, 4], [1, 1]]), [], [Tbif])
        DMA("sp", b_f[:], AP(b_if_d.tensor, 4, [[1, 4], [1, 1]]), [], [Tbif])
        DMA("sp", selm[:], sel_d, [], [Tsel])
        DMA("sp", pmm[:], pm_d, [], [Tsel])
        S.op("dve", lambda e: e.memset(ones_bf[:], 1.0), [], [Tones])
        S.op("dve", lambda e: e.memset(one_c[:], 1.0), [], [Tonesr])
        S.op("dve", lambda e: e.memset(zero_c[:], 0.0), [], [Tzero])
        S.op("dve", lambda e: e.memset(eps_c[:], EPS), [], [Teps])
        S.op("dve", lambda e: e.memset(ST[:], 0.0), [], [TST])
        S.op("dve", lambda e: e.memset(rstate[:], 0.0), [], [Trst])

        relb = carve(0, [32, 8], F32); Trelb = Tile("relb")
        ohT = carve(64, [32, 128], F32); TohT = Tile("ohT")
        DMA("sp", relb[:], relb_d, [], [Trelb])
        DMA("sp", ohT[:], ohT_d, [], [TohT])
        tbr = carve(1024, [8, 512], F32); Ttbr = Tile("tbr")
        S.op("dve", lambda e: e.memset(tbr[:], NEGB), [], [Ttbr])
        b0 = nextbank()
        MM(ps[0:8, b0, 0:128], relb[:], ohT[:], True, True, [Trelb, TohT], [PB[b0]])
        CP(tbr[:, 256:384], ps[0:8, b0, 0:128], [PB[b0]], [Ttbr])
        Tscr = Tile("wr_scr")
        DMA("sp", wr_scr.ap(), tbr[:], [Ttbr], [Tscr])
        hk = carve(4096, [128, 8, 128], F32); Thk = Tile("hk")
        for c, base in ((1, 256), (0, 128)):
            DMA("sp", hk[:], AP(wr_scr, base, [[1, 128], [512, 8], [1, 128]]), [Tscr], [Thk])
            hv = hk[:]
            pst = list(hv.ap[0])
            for g in range(2):
                rev = AP(hv.tensor, hv.offset + (4 * g) * 128 + 127, [pst, [128, 2], [256, 2], [-1, 128]])
                CP(BTall[:, g, :, c, :, :], rev, [Thk], [TBT])
        sk = carve(8192 + 64, [1, 8], F32); Tsk = Tile("sk")
        DMA("sp", sk[:], sinks_d, [], [Tsk])
        ACT(sk[:], sk[:], AF.Exp, [Tsk], [Tsk])
        CP(esink[:], sk[:].unsqueeze(2).to_broadcast([1, 8, 128]), [Tsk], [Tesink])
        dbg("BTall", BTall[:], TBT, [128, 2, 2, 2, 2, 128])
        S.barrier()

        w_in_v = w_in_d.rearrange("(k p) n -> p k n", p=128)
        w_gate_v = w_gate_d.rearrange("(k p) n -> p k n", p=128)
        w_up_v = w_up_d.rearrange("(k p) n -> p k n", p=128)
        w_out_v = w_out_d.rearrange("(k p) n -> p k n", p=128)
        w_ao_v = w_ao_d.rearrange("(k p) n -> p k n", p=128)
        w_mo_v = w_mo_d.rearrange("(k p) n -> p k n", p=128)
        w_down_v = w_down_d.rearrange("(k p) n -> p k n", p=128)

        class WStream:
            def __init__(self, ring):
                self.ring = ring
                self.specs = []
                self.loaded = 0
                self.views = {}

            def add(self, K, cols, parts):
                self.specs.append((K, cols, parts))
                return len(self.specs) - 1

            def _load(self, i):
                K, cols, parts = self.specs[i]
                slot, tl = self.ring.next()
                v = slot[:, 0:K * cols].rearrange("p (k c) -> p k c", k=K)
                for (c0, n, src) in parts:
                    DMA("pool", v[:, :, c0:c0 + n], src, [], [tl])
                self.views[i] = (v, tl)

            def get(self, i, live_from=None):
                lf = i if live_from is None else live_from
                while self.loaded < min(lf + self.ring.n, len(self.specs)):
                    self._load(self.loaded)
                    self.loaded += 1
                return self.views[i]

        TR_BANKS = (7, 6)
        tr_i = [0]

        def norm_T(src_ap, src_reads, g_bc, Tg, dstT, Tdst, tn=128, src_is_dram=True, keep=None):
            if src_is_dram:
                xt, Txt = xring.next()
                DMA("sp", xt[0:tn, :], src_ap, src_reads, [Txt])
                xin = xt[0:tn, :]
                rd = [Txt]
            else:
                xin = src_ap
                rd = list(src_reads)
            hb, Thb = hbring.next()
            ss, Tss = newscal()
            ACT(hb[0:tn, :], xin, AF.Square, rd, [Thb, Tss], accum_out=ss[0:tn, :])
            rr, Trr = newscal()
            ACT(rr[0:tn, :], ss[0:tn, :], AF.Ln, [Tss, Teps], [Trr], scale=1.0 / D, bias=eps_c[0:tn, 0:1])
            ACT(rr[0:tn, :], rr[0:tn, :], AF.Exp, [Trr], [Trr], scale=-0.5)
            ACT(hb[0:tn, :], xin, AF.Copy, rd + [Trr], [Thb], scale=rr[0:tn, 0:1])
            trb = TR_BANKS[tr_i[0] % 2]
            tr_i[0] += 1
            pv = ps_bf(trb)
            for k in range(8):
                TR(pv[:, k * 128:k * 128 + tn], hb[0:tn, k * 128:(k + 1) * 128], ident_bf[0:tn, 0:tn], [Thb, Tidb], [PB[trb]])
            src = pv.rearrange("p (k t) -> p k t", k=8)[:, :, 0:tn]
            TT(dstT, src, g_bc[:, :].unsqueeze(2).to_broadcast([128, 8, tn]), ALU.mult, [PB[trb], Tg], [Tdst])


        def sample_phase():
            S.barrier()
            RX = mybir.AxisListType.X
            o = 0
            Cl = carve(o, [128, 64, 64], F32); o += 16384
            tmpC = carve(o, [128, 64, 64], F32)
            Kf = carve(o, [128, 16, 128], F32)
            Vf = carve(o + 8192, [128, 16, 128], F32); o += 16384
            Kb = carve(o, [128, 16, 128], BF16); o += 4096
            KT = carve(o, [128, 16, 128], BF16); o += 4096
            V1 = carve(o, [128, 16, 2, 66], BF16); o += 16 * 2 * 66 * 2
            qg = carve(o, [16, 4, 128], BF16); o += 1024
            qTs = carve(o, [128, 4, 16], BF16); o += 128
            bcol = carve(o, [128, 8], F32); o += 32
            Ss = carve(o, [128, 2, 16, 4], F32); o += 512
            Es = carve(o, [128, 2, 16, 4], BF16); o += 256
            pvs = carve(o, [4, 32, 66], F32); o += 32 * 66 * 4
            yas = carve(o, [16, 8, 66], F32); o += 8 * 66 * 4
            t512 = carve(o, [16, 512], F32); o += 2048
            u512 = carve(o, [16, 512], F32); o += 2048
            ogs = carve(o, [16, 512], F32); o += 2048
            hms = carve(o, [16, 512], F32); o += 2048
            yab = carve(o, [16, 512], BF16); o += 1024
            ymb = carve(o, [16, 512], BF16); o += 1024
            s16 = carve(o, [16, 64], F32); o += 256
            ql = carve(o, [128, 64], F32); o += 256
            kl = carve(o, [128, 64], F32); o += 256
            vl = carve(o, [128, 64], F32); o += 256
            nl = carve(o, [128, 64], F32); o += 256
            hl = carve(o, [128, 64], F32); o += 256
            t64 = carve(o, [128, 64], F32); o += 256
            Cq = carve(o, [128, 64], F32); o += 256
            sl_ = carve(o, [128, 32], F32); o += 128
            assert o <= ARENA_BYTES, o
            TCl, TtmpC, TKf, TVf, TKb, TKT, TV1, Tqg, TqTs, Tbcol, TSs, TEs, Tpvs, Tyas = [Tile(n) for n in
                ("Cl", "tmpC", "Kf", "Vf", "Kb", "KT", "V1", "qg", "qTs", "bcol", "Ss", "Es", "pvs", "yas")]
            TKf = TtmpC
            TVf = TtmpC
            Tt512, Tu512, Togs, Thms, Tyab, Tymb, Ts16, Tql, Tkl, Tvl, Tnl, Thl, Tt64, TCq, Tsl = [Tile(n) for n in
                ("t512", "u512", "ogs", "hms", "yab", "ymb", "s16", "ql", "kl", "vl", "nl", "hl", "t64", "Cq", "sl")]
            DMA("sp", Kf[:], ck_d.ap().rearrange("b k c -> k b c"), [], [TKf])
            DMA("pool", Vf[:], cv_d.ap().rearrange("b k c -> k b c"), [], [TVf])
            sCv = sC_d.ap().rearrange("b h (vh v) d -> vh (b h) (v d)", vh=2)
            sCov = sCo_d.ap().rearrange("b h (vh v) d -> vh (b h) (v d)", vh=2)
            Clf = Cl[:].rearrange("p v d -> p (v d)")
            for vh in range(2):
                ln = slice(vh * 64, (vh + 1) * 64)
                DMA("sp", Clf[ln, :], sCv[vh], [], [TCl])
                DMA("pool", nl[ln, :], sn_d.ap().rearrange("b h d -> (b h) d"), [], [Tnl])
                DMA("sp", sl_[ln, 0:1], sm_d.ap().rearrange("b (h o) -> (b h) o", o=1), [], [Tsl])
                DMA("pool", ql[ln, :], zs[:, 768:1024].rearrange("b (h d) -> b h d", h=4), [Tzs], [Tql])
                DMA("sp", kl[ln, :], zs[:, 1024:1280].rearrange("b (h d) -> b h d", h=4), [Tzs], [Tkl])
                DMA("pool", vl[ln, :], zs[:, 1280:1792].rearrange("b (h w v) -> b h w v", h=4, w=2)[:, :, vh, :], [Tzs], [Tvl])
                DMA("sp", sl_[ln, 1:2], zs[:, 2304:2308].rearrange("b (h o) -> b h o", o=1), [Tzs], [Tsl])
                DMA("pool", sl_[ln, 2:3], zs[:, 2308:2312].rearrange("b (h o) -> b h o", o=1), [Tzs], [Tsl])
            DMA("sp", sl_[:, 3:4], AP(b_if_d.tensor, 0, [[0, 32], [1, 4], [1, 1]]), [], [Tsl])
            DMA("pool", sl_[:, 4:5], AP(b_if_d.tensor, 4, [[0, 32], [1, 4], [1, 1]]), [], [Tsl])
            DMA("sp", bcol[:], AP(wr_scr, 255, [[1, 128], [512, 8]]), [Tscr], [Tbcol], allow_slow_non_contiguous=True)
            DMA("pool", s16[:, 0:8], AP(wr_scr, 383, [[0, 16], [512, 8]]), [Tscr], [Ts16], allow_slow_non_contiguous=True)
            DMA("sp", s16[:, 8:16], AP(sinks_d.tensor, 0, [[0, 16], [1, 8]]), [], [Ts16])
            DMA("pool", sko_d.ap()[:, 0:127, :], ck_d.ap()[:, 1:128, :], [], [Tile("sko")])
            DMA("sp", svo_d.ap()[:, 0:127, :], cv_d.ap()[:, 1:128, :], [], [Tile("svo")])
            DMA("pool", sko_d.ap()[:, 127, :], zs[:, 512:640], [Tzs], [Tile("sko2")])
            DMA("sp", svo_d.ap()[:, 127, :], zs[:, 640:768], [Tzs], [Tile("svo2")])
            CP(Kb[:], Kf[:], [TKf], [TKb])
            for half in range(2):
                bk = 4 + half
                pvb = ps_bf(bk)
                for i in range(8):
                    b_ = half * 8 + i
                    TR(pvb[:, i * 128:(i + 1) * 128], Kb[:, b_, :], ident_bf[:], [TKb, Tidb], [PB[bk]])
                ACT(KT[:, half * 8:(half + 1) * 8, :], pvb.rearrange("p (i k) -> p i k", i=8), AF.Copy, [PB[bk]], [TKT])
            CP(V1[:, :, :, 0:64], Vf[:].rearrange("k b (g d) -> k b g d", g=2), [TVf], [TV1])
            S.op("dve", lambda e: e.memset(V1[:, :, :, 64:65], 1.0), [], [TV1])
            TS(qg[:].rearrange("b j (g d) -> b j g d", g=2), zs[:, 0:512].rearrange("b (g j d) -> b j g d", g=2, j=4), 0.125, None, ALU.mult, ALU.bypass, [Tzs], [Tqg])
            pvq = ps_bf(6)
            for j in range(4):
                TR(pvq[:, j * 16:(j + 1) * 16], qg[:, j, :], ident_bf[0:16, 0:16], [Tqg, Tidb], [PB[6]])
            CP(qTs[:].rearrange("p j b -> p (j b)"), pvq[:, 0:64], [PB[6]], [TqTs])
            for b_ in range(16):
                for g in range(2):
                    gr = slice(g * 64, (g + 1) * 64)
                    MM(ps[:, g, b_ * 4:(b_ + 1) * 4], KT[gr, b_, :], qTs[gr, :, b_], True, True, [TKT, TqTs], [PB[g]])
            for g in range(2):
                TT(Ss[:, g, :, :], ps[:, g, 0:64].rearrange("p (b j) -> p b j", j=4), bcol[:, 4 * g:4 * g + 4].unsqueeze(1).to_broadcast([128, 16, 4]), ALU.add, [PB[g], Tbcol], [TSs])
            ACT(Es[:], Ss[:], AF.Exp, [TSs], [TEs])
            for b_ in range(16):
                for g in range(2):
                    slot = b_ * 2 + g
                    bk = 2 + slot // 7 if slot < 28 else 7
                    if slot >= 28:
                        col = (slot - 28) * 66
                    else:
                        col = (slot % 7) * 66
                    bk = (2 + slot // 7) if slot < 28 else 7
                    bk = [2, 3, 4, 5, 7][min(slot // 7, 4)]
                    col = (slot % 7) * 66
                    MM(ps[0:4, bk, col:col + 65], Es[:, g, b_, :], V1[:, b_, g, 0:65], True, True, [TEs, TV1], [PB[bk]])
            for gi_, bk in enumerate([2, 3, 4, 5, 7]):
                ns = 7 if gi_ < 4 else 4
                CP(pvs[:, gi_ * 7:gi_ * 7 + ns, :], ps[0:4, bk, 0:ns * 66].rearrange("p (s c) -> p s c", c=66), [PB[bk]], [Tpvs])
            yas_v = yas[:].rearrange("b (g j) c -> b g j c", g=2)
            for j in range(4):
                DMA("pool", yas_v[:, :, j, :], pvs[j:j + 1, :, :], [Tpvs], [Tyas])
            qv = zs[:, 0:512].rearrange("b (g j d) -> b g j d", g=2, j=4)
            kv = zs[:, 512:640].rearrange("b (g d) -> b g d", g=2).unsqueeze(2).to_broadcast([16, 2, 4, 64])
            TT(t512[:].rearrange("b (g j d) -> b g j d", g=2, j=4), qv, kv, ALU.mult, [Tzs], [Tt512])
            S.op("dve", lambda e: e.tensor_reduce(s16[:, 16:24], t512[:].rearrange("b (h d) -> b h d", d=64), RX, ALU.add), [Tt512], [Ts16])
            STT(s16[:, 16:24], s16[:, 16:24], 0.125, s16[:, 0:8], ALU.mult, ALU.add, [Ts16], [Ts16])
            ACT(s16[:, 16:24], s16[:, 16:24], AF.Exp, [Ts16], [Ts16])
            ACT(s16[:, 24:32], s16[:, 8:16], AF.Exp, [Ts16], [Ts16])
            TT(s16[:, 32:40], yas[:, :, 64], s16[:, 16:24], ALU.add, [Tyas, Ts16], [Ts16])
            TT(s16[:, 32:40], s16[:, 32:40], s16[:, 24:32], ALU.add, [Ts16], [Ts16])
            S.op("dve", lambda e: e.reciprocal(s16[:, 32:40], s16[:, 32:40]), [Ts16], [Ts16])
            vv = zs[:, 640:768].rearrange("b (g d) -> b g d", g=2).unsqueeze(2).to_broadcast([16, 2, 4, 64])
            es_b = s16[:, 16:24].rearrange("b (g j) -> b g j", g=2).unsqueeze(3).to_broadcast([16, 2, 4, 64])
            TT(u512[:].rearrange("b (g j d) -> b g j d", g=2, j=4), vv, es_b, ALU.mult, [Tzs, Ts16], [Tu512])
            TT(u512[:].rearrange("b (h d) -> b h d", d=64), u512[:].rearrange("b (h d) -> b h d", d=64), yas[:, :, 0:64], ALU.add, [Tu512, Tyas], [Tu512])
            TT(yab[:].rearrange("b (h d) -> b h d", d=64), u512[:].rearrange("b (h d) -> b h d", d=64),
               s16[:, 32:40].unsqueeze(2).to_broadcast([16, 8, 64]), ALU.mult, [Tu512, Ts16], [Tyab])
            pva = ps_bf(6)
            for p in range(4):
                TR(pva[:, p * 16:(p + 1) * 16], yab[:, p * 128:(p + 1) * 128], ident_bf[0:16, 0:16], [Tyab, Tidb], [PB[6]])
            CP(yattTs[:].rearrange("p c b -> p (c b)"), pva[:, 0:64], [PB[6]], [TyaTs])
            TS(kl[:], kl[:], 0.125, None, ALU.mult, ALU.bypass, [Tkl], [Tkl])
            c_ = lambda i: sl_[:, i:i + 1]
            TT(c_(5), c_(1), c_(3), ALU.add, [Tsl], [Tsl])
            TT(c_(6), c_(2), c_(4), ALU.add, [Tsl], [Tsl])
            ACT(c_(6), c_(6), AF.Exp, [Tsl], [Tsl], scale=-1.0)
            ACT(c_(6), c_(6), AF.Ln, [Tsl], [Tsl], bias=1.0)
            TT(c_(7), c_(0), c_(6), ALU.subtract, [Tsl], [Tsl])
            TT(c_(8), c_(7), c_(5), ALU.max, [Tsl], [Tsl])
            TT(c_(9), c_(7), c_(8), ALU.subtract, [Tsl], [Tsl])
            TT(c_(10), c_(5), c_(8), ALU.subtract, [Tsl], [Tsl])
            TS(c_(11), c_(8), -1.0, None, ALU.mult, ALU.bypass, [Tsl], [Tsl])
            ACT(sl_[:, 9:12], sl_[:, 9:12], AF.Exp, [Tsl], [Tsl])
            TT(tmpC[:], Cl[:], ql[:].unsqueeze(1).to_broadcast([128, 64, 64]), ALU.mult, [TCl, Tql], [TtmpC])
            S.op("dve", lambda e: e.tensor_reduce(Cq[:], tmpC[:], RX, ALU.add), [TtmpC], [TCq])
            TT(t64[:], nl[:], ql[:], ALU.mult, [Tnl, Tql], [Tt64])
            S.op("dve", lambda e: e.tensor_reduce(c_(12), t64[:], RX, ALU.add), [Tt64], [Tsl])
            TT(t64[:], kl[:], ql[:], ALU.mult, [Tkl, Tql], [Tt64])
            S.op("dve", lambda e: e.tensor_reduce(c_(13), t64[:], RX, ALU.add), [Tt64], [Tsl])
            TT(c_(14), c_(10), c_(13), ALU.mult, [Tsl], [Tsl])
            STT(c_(15), c_(12), c_(9), c_(14), ALU.mult, ALU.add, [Tsl], [Tsl])
            STT(c_(16), c_(15), -1.0, c_(15), ALU.mult, ALU.max, [Tsl], [Tsl])
            TT(c_(16), c_(16), c_(11), ALU.max, [Tsl], [Tsl])
            S.op("dve", lambda e: e.reciprocal(c_(16), c_(16)), [Tsl], [Tsl])
            TS(hl[:], Cq[:], c_(9), None, ALU.mult, ALU.bypass, [TCq, Tsl], [Thl])
            STT(hl[:], vl[:], c_(14), hl[:], ALU.mult, ALU.add, [Tvl, Tsl, Thl], [Thl])
            TS(hl[:], hl[:], c_(16), None, ALU.mult, ALU.bypass, [Thl, Tsl], [Thl])
            TT(tmpC[:], vl[:].unsqueeze(2).to_broadcast([128, 64, 64]), kl[:].unsqueeze(1).to_broadcast([128, 64, 64]), ALU.mult, [Tvl, Tkl, TCq], [TtmpC])
            TS(Cl[:], Cl[:], c_(9), None, ALU.mult, ALU.bypass, [TCl, Tsl], [TCl])
            STT(Cl[:], tmpC[:], c_(10), Cl[:], ALU.mult, ALU.add, [TtmpC, Tsl, TCl], [TCl])
            TS(nl[:], nl[:], c_(9), None, ALU.mult, ALU.bypass, [Tnl, Tsl, Tt64], [Tnl])
            STT(nl[:], kl[:], c_(10), nl[:], ALU.mult, ALU.add, [Tkl, Tsl, Tnl], [Tnl])
            snov = sno_d.ap().rearrange("b h w d -> w (b h) d")
            smov = smo_d.ap().rearrange("b h (w o) -> w (b h) o", o=1)
            hms_v = hms[:].rearrange("b (h w v) -> b h w v", h=4, w=2)
            for vh in range(2):
                ln = slice(vh * 64, (vh + 1) * 64)
                DMA("sp", sCov[vh], Clf[ln, :], [TCl], [Tile("sCo")])
                DMA("pool", snov[vh], nl[ln, :], [Tnl], [Tile("sno")])
                DMA("sp", smov[vh], sl_[ln, 8:9], [Tsl], [Tile("smo")], allow_slow_non_contiguous=True)
                DMA("pool", hms_v[:, :, vh, :], hl[ln, :], [Thl], [Thms])
            TT(t512[:], hms[:], hms[:], ALU.mult, [Thms], [Tt512])
            S.op("dve", lambda e: e.tensor_reduce(s16[:, 40:44], t512[:].rearrange("b (h v) -> b h v", v=128), RX, ALU.add), [Tt512], [Ts16])
            ACT(s16[:, 40:44], s16[:, 40:44], AF.Ln, [Ts16, Teps], [Ts16], scale=1.0 / 128, bias=eps_c[0:16, 0:1])
            ACT(s16[:, 40:44], s16[:, 40:44], AF.Exp, [Ts16], [Ts16], scale=-0.5)
            ACT(ogs[:], zs[:, 1792:2304], AF.Sigmoid, [Tzs], [Togs])
            TT(ogs[:], ogs[:], g_head[0:16, :], ALU.mult, [Togs, Tghead], [Togs])
            TT(u512[:].rearrange("b (h v) -> b h v", v=128), hms[:].rearrange("b (h v) -> b h v", v=128),
               s16[:, 40:44].unsqueeze(2).to_broadcast([16, 4, 128]), ALU.mult, [Thms, Ts16, Tu512], [Tu512])
            TT(ymb[:], u512[:], ogs[:], ALU.mult, [Tu512, Togs], [Tymb])
            pvm = ps_bf(6)
            for h in range(4):
                TR(pvm[:, h * 16:(h + 1) * 16], ymb[:, h * 128:(h + 1) * 128], ident_bf[0:16, 0:16], [Tymb, Tidb], [PB[6]])
            CP(ymTs[:].rearrange("p c b -> p (c b)"), pvm[:, 0:64], [PB[6]], [TymTs])
            dbg("yab", yab[:], Tyab, [16, 512], BF16)
            dbg("ymb", ymb[:], Tymb, [16, 512], BF16)
            dbg("zs", zs[:], Tzs, [16, 2312])

        try:
            stage("setup")
            def superblock(sbi, pre, ST, TST, rstate, Trst, xrow, par_=0):
                gt0 = sbi * SBT

                def T_(name):
                    if pre:
                        name = f"{name}_p{par_}"
                        if name not in pre_tiles:
                            pre_tiles[name] = Tile(name)
                        return pre_tiles[name]
                    return Tile(name)
                if pre and par_ == 1:
                    hT_ = carve(66 * 1024, [128, 8, TSB], BF16)
                    ThT_ = [T_(f"hTalt{t}") for t in range(SBT)]
                else:
                    hT_, ThT_ = hT, ThT
                o = 0
                if pre:
                    pb_ = par_ * 33 * 1024
                    km_tok = carve(pb_, [128, SBT, 256], BF16); pb_ += SBT * 256 * 2
                    vm1 = carve(pb_, [128, SBT, 4, 130], BF16); pb_ += SBT * 4 * 130 * 2
                    rows = [carve(pb_ + i * TSB * 4, [4, TSB], F32) for i in range(4)]; pb_ += 4 * TSB * 4
                    ecol = carve(pb_, [128, SBT, 8], F32); pb_ += SBT * 8 * 4
                    gbb = carve(pb_, [128, SBT, 2], F32); pb_ += SBT * 2 * 4
                    Gm = carve(pb_, [4, SBT, 2], F32); pb_ += SBT * 2 * 4
                    gsm = carve(pb_, [4, SBT + 1], F32); pb_ += (SBT + 2) * 4
                    pre_cw0 = pb_
                    assert pb_ + 64 + 2 * 1040 + 1032 <= (par_ + 1) * 33 * 1024
                if pre:
                    _save = (km_tok, vm1, rows, ecol, gbb, Gm, gsm)
                qaT = carve(o, [128, 4, TSB], BF16); o += 4 * TSB * 2
                qmT = carve(o, [128, 2, TSB], BF16); o += 2 * TSB * 2
                kmT = carve(o, [128, 2, TSB], BF16); o += 2 * TSB * 2
                km_tok = carve(o, [128, SBT, 256], BF16); o += SBT * 256 * 2
                vm1 = carve(o, [128, SBT, 4, 130], BF16); o += SBT * 4 * 130 * 2
                og = carve(o, [128, SBT, 512], BF16); o += SBT * 512 * 2
                rows = [carve(o + i * TSB * 4, [4, TSB], F32) for i in range(4)]; o += 4 * TSB * 4
                ecol = carve(o, [128, SBT, 8], F32); o += SBT * 8 * 4
                gbb = carve(o, [128, SBT, 2], F32); o += SBT * 2 * 4
                Gm = carve(o, [4, SBT, 2], F32); o += SBT * 2 * 4
                gsm = carve(o, [4, SBT + 1], F32); o += (SBT + 2) * 4
                sg_r = [(carve(o + i * 2048, [128, 512], F32), T_(f"sgtmp{i}")) for i in range(2)]; o += 4096
                cw0 = o
                if pre:
                    km_tok, vm1, rows, ecol, gbb, Gm, gsm = _save
                    cw0 = pre_cw0
                TqaT = [T_(f"qaT{i}") for i in range(NBLK)]
                TqmT = [T_(f"qmT{i}") for i in range(NBLK)]
                TkmT = [T_(f"kmT{i}") for i in range(NBLK)]
                Tkmtok = [T_(f"kmtok{i}") for i in range(SBT)]
                Tvm1 = [T_(f"vm1{i}") for i in range(SBT)]
                Tog = [T_(f"og{i}") for i in range(SBT)]
                Trow = [T_(f"row{i}") for i in range(4)]
                Tecol = T_("ecol"); Tgbb = T_("gbb"); TGm = T_("Gm"); Tgsm = T_("gsm")

                if sbi == 0 and not pre:
                    norm_T(xs_d[0:128, :], [], g_attn, Tgattn, hTh[:], ThTh)
                    norm_T(xsm_d, [], g_attn, Tgattn, hTs[:], ThTs, tn=16)
                for t in range(SBT):
                    norm_T(xrow(gt0 + t), [], g_attn, Tgattn, hT_[:, :, t * 128:(t + 1) * 128], ThT_[t])
                    if pre:
                        yield "A"

                stage(f"A{sbi}")
                W = WStream(pre_ring[0] if pre else VRing(f"wA{sbi}_", 3, 4096))
                if not pre:
                    c_qa = W.add(8, 512, [(0, 512, w_in_v[:, :, O_QA:O_QA + 512])])
                    c_k = W.add(8, 512, [(0, 128, w_in_v[:, :, O_KA:O_KA + 128]),
                                         (128, 64, w_in_v[:, :, O_KA + 64:O_KA + 128]),
                                         (192, 64, w_in_v[:, :, O_KA:O_KA + 64]),
                                         (256, 256, w_in_v[:, :, O_QM:O_QM + 256])])
                c_m = W.add(8, 512, [(0, 256, w_in_v[:, :, O_KM:O_KM + 256]),
                                     (256, 128, w_in_v[:, :, O_VA:O_VA + 128]),
                                     (384, 8, w_in_v[:, :, O_I:O_I + 8])])
                c_vm = W.add(8, 512, [(0, 512, w_in_v[:, :, O_VM:O_VM + 512])])
                if not pre:
                    c_om = W.add(8, 512, [(0, 512, w_in_v[:, :, O_OM:O_OM + 512])])
                    c_D = []
                    for blk in range(NBLK):
                        cd = {}
                        cd["ga0"] = W.add(8, 512, [(0, 512, w_in_v[:, :, O_GA:O_GA + 512])])
                        cd["ga1"] = W.add(8, 512, [(0, 512, w_in_v[:, :, O_GA + 512:O_GA + 1024])])
                        cd["gm0"] = W.add(8, 512, [(0, 512, w_in_v[:, :, O_GM:O_GM + 512])])
                        cd["gm1"] = W.add(8, 512, [(0, 512, w_in_v[:, :, O_GM + 512:O_GM + 1024])])
                        cd["ao"] = W.add(4, 1024, [(0, 1024, w_ao_v)])
                        cd["mo"] = W.add(4, 1024, [(0, 1024, w_mo_v)])
                        cd["wo0"] = W.add(8, 512, [(0, 512, w_out_v[:, :, 0:512])])
                        cd["wo1"] = W.add(8, 512, [(0, 512, w_out_v[:, :, 512:1024])])
                        c_D.append(cd)
                WE = WStream(VRing(f"wE{sbi}_", 6, 2048))
                FG = [(g * 2, 2) for g in range(NFF // 2)]
                c_E = []
                for (f0, nf) in FG:
                    cg = WE.add(8, nf * 128, [(0, nf * 128, w_gate_v[:, :, f0 * 128:(f0 + nf) * 128])])
                    cu = WE.add(8, nf * 128, [(0, nf * 128, w_up_v[:, :, f0 * 128:(f0 + nf) * 128])])
                    cdn = WE.add(nf, 1024, [(0, 1024, w_down_v[:, f0:f0 + nf, :])])
                    c_E.append((cg, cu, cdn))

                def fm_group(wv, Tw, c0, M, rhs_list, evac):
                    for idx, (rhs, n, rds) in enumerate(rhs_list):
                        b = nextbank()
                        for k in range(8):
                            MM(ps[0:M, b, 0:n], wv[:, k, c0:c0 + M], rhs[:, k, :], k == 0, k == 7, [Tw] + rds, [PB[b]])
                        evac(idx, ps[0:M, b, 0:n], PB[b])

                def samp_proj(wv, Tw, ncols, pieces):
                    if pre or sbi != 0:
                        return
                    bb = nextbank()
                    for k in range(8):
                        MM(ps[0:16, bb, 0:ncols], hTs[:, k, :], wv[:, k, 0:ncols], k == 0, k == 7, [Tw, ThTs], [PB[bb]])
                    for (c0, n, z0) in pieces:
                        CP(zs[:, z0:z0 + n], ps[0:16, bb, c0:c0 + n], [PB[bb]], [Tzs])

                blk_rhs = [(hT_[:, :, blk * 512:(blk + 1) * 512], 512, ThT_[blk * 4:(blk + 1) * 4]) for blk in range(NBLK)]
                halo_rhs = [(hTh[:], 128, [ThTh])]

                if not pre:
                    wv, Tw = W.get(c_qa)
                    for p in range(4):
                        def ev(idx, pa, Tb, p=p):
                            ACT(qaT[:, p, idx * 512:(idx + 1) * 512], pa, AF.Copy, [Tb], [TqaT[idx]], scale=0.125)
                        fm_group(wv, Tw, p * 128, 128, blk_rhs, ev)
                    samp_proj(wv, Tw, 512, [(0, 512, 0)])
                    stage(f"Ba{sbi}")
                    wv, Tw = W.get(c_k)
                    for (c0, dst) in ((0, kaT_n), (128, kaT_s)):
                        def ev(idx, pa, Tb, dst=dst):
                            col = 128 + (gt0 * 128) + idx * 512
                            ACT(dst[:, col:col + 512], pa, AF.Copy, [Tb], TkaT[1 + gt0 + idx * 4: 1 + gt0 + idx * 4 + 4])
                        fm_group(wv, Tw, c0, 128, blk_rhs, ev)
                        if sbi == 0:
                            def evh(idx, pa, Tb, dst=dst):
                                ACT(dst[:, 0:128], pa, AF.Copy, [Tb], [TkaT[0]])
                            fm_group(wv, Tw, c0, 128, halo_rhs, evh)
                    for p in range(2):
                        def ev(idx, pa, Tb, p=p):
                            CP(qmT[:, p, idx * 512:(idx + 1) * 512], pa, [Tb], [TqmT[idx]])
                        fm_group(wv, Tw, 256 + p * 128, 128, blk_rhs, ev)
                    samp_proj(wv, Tw, 512, [(0, 128, 512), (256, 256, 768)])
                    if sbi == NSB - 1:
                        b = nextbank()
                        lt = SBT - 1
                        for k in range(8):
                            MM(ps[:, b, 0:128], hT_[:, k, lt * 128:(lt + 1) * 128], wv[:, k, 0:128], k == 0, k == 7, [Tw, ThT_[lt]], [PB[b]])
                        pko = sb("pko", [128, 128]); Tpko = Tile("pko")
                        CP(pko[:], ps[:, b, 0:128], [PB[b]], [Tpko])
                        DMA("sp", pk_d, pko[:], [Tpko], [Tile("pk_d")])
                stage(f"Bb{sbi}")
                wv, Tw = W.get(c_m)
                if not pre:
                    for p in range(2):
                        def ev(idx, pa, Tb, p=p):
                            ACT(kmT[:, p, idx * 512:(idx + 1) * 512], pa, AF.Copy, [Tb], [TkmT[idx]], scale=0.125)
                        fm_group(wv, Tw, p * 128, 128, blk_rhs, ev)
                r_ig, r_t, r_nf, r_u = rows

                def ev(idx, pa, Tb):
                    ACT(r_ig[:, idx * 512:(idx + 1) * 512], pa, AF.Identity, [Tb, Tbif], [Trow[0]], bias=b_i[:, 0:1])
                fm_group(wv, Tw, 384, 4, blk_rhs, ev)

                def ev(idx, pa, Tb):
                    ACT(r_t[:, idx * 512:(idx + 1) * 512], pa, AF.Identity, [Tb, Tbif], [Trow[1]], bias=b_f[:, 0:1])
                fm_group(wv, Tw, 388, 4, blk_rhs, ev)
                stage(f"Bc{sbi}")
                ncol = 256 if pre else 384
                for t in range(SBT):
                    b = nextbank()
                    for k in range(8):
                        MM(ps[:, b, 0:ncol], hT_[:, k, t * 128:(t + 1) * 128], wv[:, k, 0:ncol], k == 0, k == 7, [Tw, ThT_[t]], [PB[b]])
                    ACT(km_tok[:, t, :], ps[:, b, 0:256], AF.Copy, [PB[b]], [Tkmtok[t]], scale=0.125)
                    if not pre:
                        pvv = ps[:, b, 256:384].rearrange("p (g d) -> p g d", g=2)
                        gt = gt0 + t
                        CP(va[:, gt + 1, :, 0, :], pvv, [PB[b]], [Tva[gt + 1]])
                        ACT(va[:, gt + 1, :, 1, :], pvv, AF.Copy, [PB[b]], [Tva[gt + 1]])
                        if gt == NT - 1:
                            pvo = sb("pvo", [128, 128]); Tpvo = Tile("pvo")
                            CP(pvo[:], ps[:, b, 256:384], [PB[b]], [Tpvo])
                            DMA("sp", pv_d, pvo[:], [Tpvo], [Tile("pv_d")])
                if sbi == 0 and not pre:
                    b = nextbank()
                    for k in range(8):
                        MM(ps[:, b, 0:128], hTh[:, k, :], wv[:, k, 256:384], k == 0, k == 7, [Tw, ThTh], [PB[b]])
                    pvv = ps[:, b, 0:128].rearrange("p (g d) -> p g d", g=2)
                    CP(va[:, 0, :, 0, :], pvv, [PB[b]], [Tva[0]])
                    ACT(va[:, 0, :, 1, :], pvv, AF.Copy, [PB[b]], [Tva[0]])
                samp_proj(wv, Tw, 392, [(0, 256, 1024), (256, 128, 640), (384, 8, 2304)])
                stage(f"Bd{sbi}")
                if (not pre) or (par_ not in pre_ones_done):
                    if pre:
                        pre_ones_done.add(par_)
                    for t in range(SBT):
                        S.op("dve", lambda e, t=t: e.memset(vm1[:, t, :, 128:129], 1.0), [], [Tvm1[t]])
                wv, Tw = W.get(c_vm)
                for t in range(SBT):
                    b = nextbank()
                    for k in range(8):
                        MM(ps[:, b, 0:512], hT_[:, k, t * 128:(t + 1) * 128], wv[:, k, 0:512], k == 0, k == 7, [Tw, ThT_[t]], [PB[b]])
                    ACT(vm1[:, t, :, 0:128], ps[:, b, 0:512].rearrange("p (h v) -> p h v", h=4), AF.Copy, [PB[b]], [Tvm1[t]])
                samp_proj(wv, Tw, 512, [(0, 512, 1280)])
                stage(f"Be{sbi}")
                if not pre:
                    wv, Tw = W.get(c_om)
                    for t in range(SBT):
                        b = nextbank()
                        for k in range(8):
                            MM(ps[:, b, 0:512], hT_[:, k, t * 128:(t + 1) * 128], wv[:, k, 0:512], k == 0, k == 7, [Tw, ThT_[t]], [PB[b]])
                        tmp, Ttmp = sg_r[t % 2]
                        ACT(tmp[:], ps[:, b, 0:512], AF.Sigmoid, [PB[b]], [Ttmp])
                        TT(og[:, t, :], tmp[:], g_head[:], ALU.mult, [Ttmp, Tghead], [Tog[t]])
                    samp_proj(wv, Tw, 512, [(0, 512, 1792)])

                stage(f"B{sbi}")
                if pre:
                    yield "front"
                ACT(r_t[:], r_t[:], AF.Exp, [Trow[1]], [Trow[1]], scale=-1.0)
                ACT(r_t[:], r_t[:], AF.Ln, [Trow[1]], [Trow[1]], bias=1.0)
                S.op("dve", lambda e: e.tensor_tensor_scan(r_nf[:], one_c[0:4, 0:1].to_broadcast([4, TSB]), r_t[:], rstate[:, 0:1], ALU.mult, ALU.add),
                     [Tonesr, Trow[1], Trst], [Trow[2]])
                dbg(f"nl{sbi}", r_t[:], Trow[1], [4, TSB])
                dbg(f"NF{sbi}", r_nf[:], Trow[2], [4, TSB])
                dbg(f"ig{sbi}", r_ig[:], Trow[0], [4, TSB])
                TT(r_ig[:], r_ig[:], r_nf[:], ALU.add, [Trow[0], Trow[2]], [Trow[0]])
                S.op("dve", lambda e: e.tensor_tensor_scan(r_t[:], r_ig[:], r_ig[:], rstate[:, 1:2], ALU.max, ALU.max),
                     [Trow[0], Trst, Trow[1]], [Trow[1]])
                CP(gsm[:, 0:1], rstate[:, 1:2], [Trst], [Tgsm])
                rt_v = r_t[:].rearrange("p (t c) -> p t c", c=128)
                CP(gsm[:, 1:SBT + 1], rt_v[:, :, 127], [Trow[1]], [Tgsm])
                CP(r_u[:].rearrange("p (t c) -> p t c", c=128), gsm[:, 1:SBT + 1].unsqueeze(2).to_broadcast([4, SBT, 128]), [Tgsm], [Trow[3]])
                CP(rstate[:, 0:1], r_nf[:, TSB - 1:TSB], [Trow[2], Tgsm], [Trst])
                CP(rstate[:, 1:2], r_t[:, TSB - 1:TSB], [Trow[1]], [Trst])
                TT(r_ig[:], r_ig[:], r_u[:], ALU.subtract, [Trow[0], Trow[3]], [Trow[0]])
                ACT(r_ig[:], r_ig[:], AF.Exp, [Trow[0]], [Trow[0]])
                TT(r_nf[:], r_nf[:], r_u[:], ALU.subtract, [Trow[2], Trow[3]], [Trow[2]])
                ACT(r_nf[:], r_nf[:], AF.Exp, [Trow[2]], [Trow[2]])
                gexp = carve(cw0, [4, SBT], F32)
                Tgexp = T_("gexp")
                TT(gexp[:], gsm[:, 0:SBT], gsm[:, 1:SBT + 1], ALU.subtract, [Tgsm], [Tgexp])
                ACT(gexp[:], gexp[:], AF.Exp, [Tgexp], [Tgexp])
                TT(Gm[:], gexp[:].unsqueeze(2).to_broadcast([4, SBT, 2]), pmm[:].unsqueeze(1).to_broadcast([4, SBT, 2]), ALU.mult, [Tgexp, Tsel], [TGm])
                b = nextbank()
                MM(ps[:, b, 0:SBT * 2], selm[:], Gm[:].rearrange("p t c -> p (t c)"), True, True, [Tsel, TGm], [PB[b]])
                CP(gbb[:].rearrange("p t c -> p (t c)"), ps[:, b, 0:SBT * 2], [PB[b]], [Tgbb])
                b = nextbank()
                for t in range(SBT):
                    TR(ps[:, b, t * 8:t * 8 + 4], r_ig[:, t * 128:(t + 1) * 128], ident_f[0:4, 0:4], [Trow[0], Tidf], [PB[b]])
                    TR(ps[:, b, t * 8 + 4:t * 8 + 8], r_nf[:, t * 128:(t + 1) * 128], ident_f[0:4, 0:4], [Trow[2], Tidf], [PB[b]])
                CP(ecol[:].rearrange("p t c -> p (t c)"), ps[:, b, 0:SBT * 8], [PB[b]], [Tecol])
                dbg(f"ecol{sbi}", ecol[:], Tecol, [128, SBT, 8])
                dbg(f"gbb{sbi}", gbb[:], Tgbb, [128, SBT, 2])
                dbg(f"rows{sbi}", r_t[:], Trow[1], [4, TSB])
                dbg(f"rowA{sbi}", r_ig[:], Trow[0], [4, TSB])

                stage(f"R{sbi}")
                if pre:
                    yield "rows"
                o = cw0 + 64
                if pre:
                    V2_pre = [(carve(o + i * 1040, [128, 4, 130], BF16), T_(f"V2{i}")) for i in range(2)]
                    STgf_pre = carve(o + 2080, [128, 2, 129], F32)
                    o = 0
                Sb_r = [(carve(o + i * 2048, [128, 512], F32), T_(f"Sb{i}")) for i in range(2)]; o += 4096
                ET_r = [(carve(o + i * 1024, [128, 512], BF16), T_(f"ET{i}")) for i in range(8)]; o += 8192
                rden_r = [(carve(o + i * 2048, [128, 512], F32), T_(f"rden{i}")) for i in range(2)]; o += 4096
                Wt_r = [(carve(o + i * 1024, [128, 4, 128], BF16), T_(f"Wt{i}")) for i in range(2)]; o += 2048
                V2_r = [(carve(o + i * 1040, [128, 4, 130], BF16), T_(f"V2{i}")) for i in range(2)]; o += 2080
                ym_r = [(carve(o + i * 1024, [128, 512], BF16), T_(f"ym{i}")) for i in range(2)]; o += 2048
                STg_f = carve(o, [128, 2, 129], F32); o += 2 * 129 * 4
                STg_b = carve(o, [128, 2, 130], BF16); o += 2 * 130 * 2
                o = (o + 3) // 4 * 4
                sj = carve(o, [128, 128], BF16); o += 256
                TSTgf = T_("STgf"); TSTgb = T_("STgb"); Tsj = T_("sj")
                assert o <= ARENA_BYTES, o
                if pre:
                    V2_r = V2_pre
                    STg_f = STgf_pre
                et_i = [0]
                for t in range(SBT):
                    gt = gt0 + t
                    tok = slice(t * 128, (t + 1) * 128)
                    sbk = (t % 2) * 2
                    if not pre:
                        ETs = {}
                        for g in range(2):
                            for c in range(2):
                                kt = gt + c
                                kcol = slice(kt * 128, (kt + 1) * 128)
                                for par in range(2):
                                    kk = kaT_n if (g == par) else kaT_s
                                    pr = slice(par * 64, (par + 1) * 64)
                                    outp = ps[:, sbk + par, c * 256:(c + 1) * 256].rearrange("p (j q) -> p j q", j=2)
                                    MM(outp, kk[pr, kcol], qaT[pr, 2 * g:2 * g + 2, tok], True, True, [TkaT[kt], TqaT[t // 4]], [PB[sbk + par]])
                            for par in range(2):
                                Sb, TSb = Sb_r[par]
                                TT(Sb[:], ps[:, sbk + par, :], BTall[:, g, par, :, :, :].rearrange("p c j q -> p (c j q)"), ALU.add, [PB[sbk + par], TBT], [TSb])
                                ET, TET = ET_r[et_i[0] % 8]; et_i[0] += 1
                                if gt == 0:
                                    ACT(ET[:, 0:256], Sb[:, 0:256], AF.Exp, [TSb, Tflag], [TET], bias=flag[:, 0:1])
                                    ACT(ET[:, 256:512], Sb[:, 256:512], AF.Exp, [TSb], [TET])
                                else:
                                    ACT(ET[:], Sb[:], AF.Exp, [TSb], [TET])
                                ETs[(g, par)] = (ET, TET)
                        for g in range(2):
                            bY, bD = 4 + g, 6 + g
                            for par in range(2):
                                ET, TET = ETs[(g, par)]
                                yv = ps[:, bY, :].rearrange("p (j q) -> p j q", j=4)[:, par::2, :]
                                dv = ps[:, bD, :].rearrange("p (j q) -> p j q", j=4)[:, par::2, :]
                                for c in range(2):
                                    kt = gt + c
                                    MM(yv, va[:, kt, g, :, :].rearrange("p a d -> p (a d)"), ET[:, c * 256:(c + 1) * 256].rearrange("p (j q) -> p j q", j=2),
                                       c == 0, c == 1, [Tva[kt], TET], [PB[bY]])
                                for c in range(2):
                                    MM(dv, ones_bf[:], ET[:, c * 256:(c + 1) * 256].rearrange("p (j q) -> p j q", j=2), c == 0, False, [Tones, TET], [PB[bD]])
                                MM(dv, ones_bf[0:1, :], esink[0:1, 4 * g + par:4 * g + 4:2, :], False, True, [Tones, Tesink], [PB[bD]])
                            rden, Trden = rden_r[g]
                            ACT(rden[:], ps[:, bD, :], AF.Ln, [PB[bD]], [Trden])
                            ACT(rden[:], rden[:], AF.Exp, [Trden], [Trden], scale=-1.0)
                            for par in range(2):
                                pr = slice(par * 64, (par + 1) * 64)
                                yv = ps[pr, bY, :].rearrange("p (j q) -> p j q", j=4)[:, par::2, :]
                                rv = rden[pr, :].rearrange("p (j q) -> p j q", j=4)[:, par::2, :]
                                TT(yattT[pr, 2 * g:2 * g + 2, tok], yv, rv, ALU.mult, [PB[bY], Trden], [TyaT[t]])
                for t in range(SBT):
                    gt = gt0 + t
                    tok = slice(t * 128, (t + 1) * 128)
                    bMS, bHO, bSU, bTR = (0, 2, 0, 1) if t % 2 == 0 else (4, 6, 4, 5)
                    if not pre:
                        for h in range(4):
                            p, par = h // 2, h % 2
                            pr = slice(par * 64, (par + 1) * 64)
                            MM(ps[:, bMS + par, p * 128:(p + 1) * 128], kmT[pr, p, tok], qmT[pr, p, tok], True, True, [TkmT[t // 4], TqmT[t // 4]], [PB[bMS + par]])
                        Wt, TWt = Wt_r[t % 2]
                        TT(Wt[:].rearrange("s (p r) q -> s r p q", r=2), ps[:, bMS:bMS + 2, 0:256].rearrange("s b (p q) -> s b p q", p=2),
                           caus[:].unsqueeze(1).unsqueeze(1).to_broadcast([128, 2, 2, 128]), ALU.mult, [PB[bMS], PB[bMS + 1], Tcaus], [TWt])
                    V2, TV2 = V2_r[t % 2]
                    TT(V2[:, :, 0:129], vm1[:, t, :, 0:129], ecol[:, t, 0:4].unsqueeze(2).to_broadcast([128, 4, 129]), ALU.mult, [Tvm1[t], Tecol], [TV2])
                    if not pre:
                        for p in range(2):
                            ACT(STg_b[:, p, 0:129], ST[:, p, :], AF.Copy, [TST, Tgbb], [TSTgb], scale=gbb[:, t, p:p + 1])
                        for h in range(4):
                            p, par = h // 2, h % 2
                            pr = slice(par * 64, (par + 1) * 64)
                            bb = bHO + h // 2
                            cc = (h % 2) * 256
                            MM(ps[:, bb, cc:cc + 129], Wt[:, h, :], V2[:, h, 0:129], True, False, [TWt, TV2], [PB[bb]])
                            MM(ps[:, bb, cc:cc + 129], qmT[pr, p, tok], STg_b[pr, p, 0:129], False, True, [TqmT[t // 4], TSTgb], [PB[bb]])
                    for p in range(2):
                        for par in range(2):
                            h = 2 * p + par
                            pr = slice(par * 64, (par + 1) * 64)
                            c0 = p * 128 + par * 64
                            MM(ps[pr, bSU + p, 0:129], km_tok[:, t, c0:c0 + 64], V2[:, h, 0:129], True, True, [Tkmtok[t], TV2], [PB[bSU + p]])
                    for p in range(2):
                        STT(ST[:, p, :], ST[:, p, :], gbb[:, t, p:p + 1], ps[:, bSU + p, 0:129], ALU.mult, ALU.add, [TST, Tgbb, PB[bSU + p]], [TST])
                    if pre:
                        yield "C"
                    if not pre:
                        HOv = ps[:, bHO:bHO + 2, :].rearrange("p b (c x) -> p (b c) x", c=2)
                        d4, Td4 = d4ring.next()
                        CP(d4[:, 0:4], HOv[:, :, 128], [PB[bHO], PB[bHO + 1]], [Td4])
                        STT(d4[:, 0:4], d4[:, 0:4], -1.0, d4[:, 0:4], ALU.mult, ALU.max, [Td4], [Td4])
                        TT(d4[:, 0:4], d4[:, 0:4], ecol[:, t, 4:8], ALU.max, [Td4, Tecol], [Td4])
                        S.op("dve", lambda e, d4=d4: e.reciprocal(d4[:, 0:4], d4[:, 0:4]), [Td4], [Td4])
                        for h in range(4):
                            ACT(sj[:], HOv[:, h, 0:128], AF.Square, [PB[bHO], PB[bHO + 1], Td4], [Tsj, Td4], scale=d4[:, h:h + 1], accum_out=d4[:, 4 + h:5 + h])
                        ACT(d4[:, 8:12], d4[:, 4:8], AF.Ln, [Td4, Teps], [Td4], scale=1.0 / 128, bias=eps_c[:, 0:1])
                        ACT(d4[:, 8:12], d4[:, 8:12], AF.Exp, [Td4], [Td4], scale=-0.5)
                        TT(d4[:, 8:12], d4[:, 8:12], d4[:, 0:4], ALU.mult, [Td4], [Td4])
                        ym, Tym = ym_r[t % 2]
                        for h in range(4):
                            STT(ym[:, h * 128:(h + 1) * 128], HOv[:, h, 0:128], d4[:, 8 + h:9 + h], og[:, t, h * 128:(h + 1) * 128], ALU.mult, ALU.mult,
                                [PB[bHO], PB[bHO + 1], Td4, Tog[t]], [Tym])
                        pvb = ps_bf(bTR)
                        for h in range(4):
                            TR(pvb[:, h * 128:(h + 1) * 128], ym[:, h * 128:(h + 1) * 128], ident_bf[:], [Tym, Tidb], [PB[bTR]])
                        ACT(ymT[:, :, tok], pvb[:, 0:512].rearrange("p (h q) -> p h q", h=4), AF.Copy, [PB[bTR]], [TymT[t]])
                dbg(f"yattT{sbi}", yattT[:], TyaT[SBT - 1], [128, 4, TSB], BF16)
                dbg(f"ymT{sbi}", ymT[:], TymT[SBT - 1], [128, 4, TSB], BF16)

                if pre:
                    return
                if sbi == 0:
                    sample_phase()
                stage(f"C{sbi}")
                S.barrier()
                samp = (sbi == 0)
                o = 0
                x1 = carve(o, [128, SBT, D], F32); o += SBT * D * 4
                Tx1 = [Tile(f"x1_{t}") for t in range(SBT)]
                x1s = carve(o, [16, D], F32); o += D * 4
                Tx1s = Tile("x1s")
                sga = carve(o, [128, 8, 512], BF16); o += 8192
                sgm = carve(o, [128, 8, 512], BF16); o += 8192
                mixT = carve(o, [128, 8, 512], BF16); o += 8192
                tmpD = [(carve(o + i * 2048, [128, 512], F32), Tile("tmpD")) for i in range(2)]; o += 4096
                sga_s = carve(o, [128, 8, 16], BF16); o += 256
                sgm_s = carve(o, [128, 8, 16], BF16); o += 256
                mix_s = carve(o, [128, 8, 16], BF16); o += 256
                aT_s = carve(o, [128, 2, 16], BF16); o += 64
                e_off = o
                assert o <= ARENA_BYTES
                Tsga = [Tile("sga") for _ in range(8)]
                Tsgm = [Tile("sgm") for _ in range(8)]
                Tmix = [Tile("mix") for _ in range(8)]
                Tsga_s = [Tile("sga_s") for _ in range(8)]
                Tsgm_s = [Tile("sgm_s") for _ in range(8)]
                Tmix_s = [Tile("mix_s") for _ in range(8)]
                TaT_s = Tile("aT_s")

                def subblocks(blk):
                    btok = slice(blk * 512, (blk + 1) * 512)
                    subs = [dict(n=512, hv=hT_[:, :, btok], hrd=ThT_[blk * 4:(blk + 1) * 4],
                                 yav=yattT[:, :, btok], yard=TyaT[blk * 4:(blk + 1) * 4],
                                 ymv=ymT[:, :, btok], ymrd=TymT[blk * 4:(blk + 1) * 4],
                                 sga=sga, sgm=sgm, mix=mixT, Tsga=Tsga, Tsgm=Tsgm, Tmix=Tmix,
                                 tiles=[(x1[:, blk * 4 + tt, :], Tx1[blk * 4 + tt], 128, slice(tt * 128, (tt + 1) * 128)) for tt in range(4)])]
                    if samp and blk == 0:
                        subs.append(dict(n=16, hv=hTs[:], hrd=[ThTs], yav=yattTs[:], yard=[TyaTs], ymv=ymTs[:], ymrd=[TymTs],
                                         sga=sga_s, sgm=sgm_s, mix=mix_s, Tsga=Tsga_s, Tsgm=Tsgm_s, Tmix=Tmix_s,
                                         tiles=[(x1s[:], Tx1s, 16, slice(0, 16))]))
                    return subs

                if samp:
                    DMA("sp", x1s[:], xsm_d, [], [Tx1s])
                for blk in range(NBLK):
                    cd = c_D[blk]
                    subs = subblocks(blk)
                    for tt in range(4):
                        t = blk * 4 + tt
                        r0 = 128 + (gt0 + t) * 128
                        DMA("sp", x1[:, t, :], xs_d[r0:r0 + 128, :], [], [Tx1[t]])
                    for nm in ("ga", "gm"):
                        for half in range(2):
                            wv, Tw = W.get(cd[f"{nm}{half}"])
                            for jj in range(4):
                                j = half * 4 + jj
                                for sub in subs:
                                    n = sub["n"]
                                    dst, Td = (sub["sga"], sub["Tsga"]) if nm == "ga" else (sub["sgm"], sub["Tsgm"])
                                    bb = nextbank()
                                    for k in range(8):
                                        MM(ps[:, bb, 0:n], wv[:, k, jj * 128:(jj + 1) * 128], sub["hv"][:, k, :], k == 0, k == 7, [Tw] + sub["hrd"], [PB[bb]])
                                    ACT(dst[:, j, 0:n], ps[:, bb, 0:n], AF.Sigmoid, [PB[bb]], [Td[j]])
                    wva, Twa = W.get(cd["ao"])
                    wvm, Twm = W.get(cd["mo"], live_from=cd["ao"])
                    for j in range(8):
                        for sub in subs:
                            n = sub["n"]
                            bb = nextbank()
                            for k in range(4):
                                MM(ps[:, bb, 0:n], wva[:, k, j * 128:(j + 1) * 128], sub["yav"][:, k, :], k == 0, k == 3, [Twa] + sub["yard"], [PB[bb]])
                            tmp, Ttmp = tmpD[j % 2]
                            TT(tmp[:, 0:n], ps[:, bb, 0:n], sub["sga"][:, j, 0:n], ALU.mult, [PB[bb], sub["Tsga"][j]], [Ttmp])
                            b2 = nextbank()
                            for k in range(4):
                                MM(ps[:, b2, 0:n], wvm[:, k, j * 128:(j + 1) * 128], sub["ymv"][:, k, :], k == 0, k == 3, [Twm] + sub["ymrd"], [PB[b2]])
                            TT(sub["mix"][:, j, 0:n], ps[:, b2, 0:n], sub["sgm"][:, j, 0:n], ALU.mult, [PB[b2], sub["Tsgm"][j]], [sub["Tmix"][j]])
                            TT(sub["mix"][:, j, 0:n], sub["mix"][:, j, 0:n], tmp[:, 0:n], ALU.add, [sub["Tmix"][j], Ttmp], [sub["Tmix"][j]])
                    for half in range(2):
                        wv, Tw = W.get(cd[f"wo{half}"])
                        for sub in subs:
                            for (xa, Txa, tn, tsl) in sub["tiles"]:
                                bb = nextbank()
                                for j in range(8):
                                    MM(ps[0:tn, bb, :], sub["mix"][:, j, tsl], wv[:, j, :], j == 0, j == 7, [Tw, sub["Tmix"][j]], [PB[bb]])
                                TT(xa[:, half * 512:(half + 1) * 512], xa[:, half * 512:(half + 1) * 512], ps[0:tn, bb, :], ALU.add, [PB[bb], Txa], [Txa])
                dbg(f"x1_{sbi}", x1[:], Tx1[SBT - 1], [128, SBT, D])
                dbg("x1s", x1s[:], Tx1s, [16, D])

                stage(f"D{sbi}")
                o = e_off
                actT = [(carve(o + i * 2048, [128, 2, 512], BF16), Tile("actT")) for i in range(2)]; o += 4096
                sil = [(carve(o + i * 2048, [128, 512], F32), Tile("sil")) for i in range(2)]; o += 4096
                yo = [(carve(o + i * 4096, [128, D], F32), Tile("yo")) for i in range(2)]; o += 8192
                Tx1b = [Tile(f"x1b_{t}") for t in range(SBT)]
                assert o <= ARENA_BYTES, o
                for t in range(SBT):
                    norm_T(x1[:, t, :], [Tx1[t]], g_ffn, Tgffn, hT_[:, :, t * 128:(t + 1) * 128], ThT_[t], src_is_dram=False)
                if samp:
                    norm_T(x1s[:], [Tx1s], g_ffn, Tgffn, hTs[:], ThTs, tn=16, src_is_dram=False)
                for gi, (f0, nf) in enumerate(FG):
                    cg, cu, cdn = c_E[gi]
                    wg, Twg = WE.get(cg)
                    wu, Twu = WE.get(cu, live_from=cg)
                    wd, Twd = WE.get(cdn, live_from=cg)
                    for blk in range(NBLK):
                        for si, sub in enumerate(subblocks(blk)):
                            n = sub["n"]
                            if si == 0:
                                aT, TaT = actT[(gi * NBLK + blk) % 2]
                            else:
                                aT, TaT = aT_s, TaT_s
                            for c in range(nf):
                                bb = nextbank()
                                for k in range(8):
                                    MM(ps[:, bb, 0:n], wg[:, k, c * 128:(c + 1) * 128], sub["hv"][:, k, :], k == 0, k == 7, [Twg] + sub["hrd"], [PB[bb]])
                                b2 = nextbank()
                                for k in range(8):
                                    MM(ps[:, b2, 0:n], wu[:, k, c * 128:(c + 1) * 128], sub["hv"][:, k, :], k == 0, k == 7, [Twu] + sub["hrd"], [PB[b2]])
                                sl, Tsl = sil[c % 2]
                                ACT(sl[:, 0:n], ps[:, bb, 0:n], AF.Silu, [PB[bb]], [Tsl])
                                TT(aT[:, c, 0:n], sl[:, 0:n], ps[:, b2, 0:n], ALU.mult, [Tsl, PB[b2]], [TaT])
                            for ti_, (xa, Txa, tn, tsl) in enumerate(sub["tiles"]):
                                for half in range(2):
                                    bb = nextbank()
                                    for c in range(nf):
                                        MM(ps[0:tn, bb, :], aT[:, c, tsl], wd[:, c, half * 512:(half + 1) * 512], c == 0, c == nf - 1, [TaT, Twd], [PB[bb]])
                                    if True:
                                        TT(xa[:, half * 512:(half + 1) * 512], xa[:, half * 512:(half + 1) * 512], ps[0:tn, bb, :], ALU.add, [PB[bb], Txa], [Txa])
                stage(f"E{sbi}")
                fin = [(x1[:, t, :], [Tx1[t], Tx1b[t]], 128, y_d[(gt0 + t) * 128:(gt0 + t + 1) * 128, :]) for t in range(SBT)]
                if samp:
                    fin.append((x1s[:], [Tx1s], 16, ys_d))
                for fi, (xa, Txa, tn, dst) in enumerate(fin):
                    yb, Tyb = yo[fi % 2]
                    ss, Tss = newscal()
                    ACT(yb[0:tn, :], xa, AF.Square, Txa, [Tyb, Tss], accum_out=ss[0:tn, :])
                    rr, Trr = newscal()
                    ACT(rr[0:tn, :], ss[0:tn, :], AF.Ln, [Tss, Teps], [Trr], scale=1.0 / D, bias=eps_c[0:tn, 0:1])
                    ACT(rr[0:tn, :], rr[0:tn, :], AF.Exp, [Trr], [Trr], scale=-0.5)
                    STT(yb[0:tn, :], xa, rr[0:tn, 0:1], g_fin[0:tn, :], ALU.mult, ALU.mult, Txa + [Trr, Tgfin], [Tyb])
                    DMA("sp", dst, yb[0:tn, :], [Tyb], [Tile("y_d")])
                S.barrier()

            pact = sb("pact", [128, 4]); Tpact = Tile("pact")
            pre_ring[0] = VRing("wP_", 3, 4096)
            DMA("sp", pact[:], pact_d, [], [Tpact])
            in_pre[0] = True

            def boundary(j):
                TT(rstate[:, 1:2], rstate[:, 1:2], rstate[:, 0:1], ALU.subtract, [Trst], [Trst])
                TS(rstate[:, 1:2], rstate[:, 1:2], pact[0:4, j:j + 1], None, ALU.mult, ALU.bypass, [Trst, Tpact], [Trst])
                S.op("dve", lambda e: e.memset(rstate[:, 0:1], 0.0), [], [Trst])
                TS(ST[:], ST[:], pact[:, j:j + 1], None, ALU.mult, ALU.bypass, [TST, Tpact], [TST])

            gens = []
            for j in range(3):
                for sbi in range(NSB):
                    k = j * NSB + sbi
                    gens.append(superblock(sbi, True, ST, TST, rstate, Trst,
                                           lambda gt, j=j: xprev_d[(j * NT + gt) * 128:(j * NT + gt + 1) * 128, :], par_=k % 2))
            def run_until(g, tag):
                for x in g:
                    if x == tag:
                        return True
                return False

            run_until(gens[0], "front")
            for k in range(len(gens)):
                gk = gens[k]
                gn = gens[k + 1] if k + 1 < len(gens) else None
                run_until(gk, "rows")
                for t in range(SBT):
                    if gn is not None:
                        run_until(gn, "A")
                    run_until(gk, "C")
                for _ in gk:
                    pass
                if gn is not None:
                    run_until(gn, "front")
                if k % NSB == NSB - 1:
                    boundary(k // NSB)
            in_pre[0] = False
            S.barrier()
            dbg("STpre", ST[:], TST, [128, 2, 129])
            dbg("rst", rstate[:], Trst, [4, 4])
            stage("pre")
            for sbi in range(NSB):
                for _ in superblock(sbi, False, ST, TST, rstate, Trst, lambda gt: xs_d[128 + gt * 128:128 + (gt + 1) * 128, :]):
                    pass

            Cout = sb("Cout", [128, 4, 64]); TCout = Tile("Cout")
            for h in range(4):
                p, par = h // 2, h % 2
                pr = slice(par * 64, (par + 1) * 64)
                TR(ps[:, par, p * 64:(p + 1) * 64], ST[pr, p, 0:128], ident_f[pr, pr], [TST, Tidf], [PB[par]])
            for par in range(2):
                CP(Cout[:, par::2, :], ps[:, par, 0:128].rearrange("v (p d) -> v p d", p=2), [PB[par]], [TCout])
            DMA("sp", pC_d.rearrange("h v d -> v h d"), Cout[:], [TCout], [Tile("pC_d")])
            for h in range(4):
                p, par = h // 2, h % 2
                pr = slice(par * 64, (par + 1) * 64)
                DMA("sp", AP(pn_d.tensor, h * 64, [[1, 64], [1, 1]]), ST[pr, p, 128:129], [TST], [Tile("pn_d")])
            mo = sb("mo", [4, 1]); Tmo = Tile("mo")
            TT(mo[:], rstate[:, 1:2], rstate[:, 0:1], ALU.subtract, [Trst], [Tmo])
            DMA("sp", pm_out_d, mo[:], [Tmo], [Tile("pm_d")])

        except _Stop:
            pass
        S.finish()
        S.emit()
    return nc, dbg_outs


_CACHE = {}


def _consts():
    ident = np.eye(128, dtype=np.float32)
    dist_rev = 127 - np.arange(128)
    bk = t5_bucket_np(dist_rev)
    ohT_rev = (np.arange(32)[:, None] == bk[None, :]).astype(np.float32)
    causT = (np.arange(128)[:, None] <= np.arange(128)[None, :]).astype(np.float32)
    sel = np.zeros((4, 128), np.float32)
    for h in range(4):
        sel[h, (h % 2) * 64:(h % 2) * 64 + 64] = 1.0
    pm = np.zeros((4, 2), np.float32)
    for h in range(4):
        pm[h, h // 2] = 1.0
    return dict(ident_bf=ident.astype(ml_dtypes.bfloat16), ident_f=ident, ohT_rev=ohT_rev, causT=causT, sel=sel, pm=pm)


def kernel(x_prompt, x_sample, cache_k_win, cache_v_win, state_mlstm_C, state_mlstm_n, state_mlstm_m,
           rel_bias, w_in, b_if, sinks, g_attn_norm, g_head, w_att_out, w_mlstm_out, w_out,
           g_ffn_norm, w_gate, w_up, w_down, g_final, _debug=(), _stop=None, _trace=False):
    f32 = np.float32
    x_prompt = np.asarray(x_prompt, f32)
    x_sample = np.asarray(x_sample, f32)
    cache_k_win = np.asarray(cache_k_win, f32)
    cache_v_win = np.asarray(cache_v_win, f32)
    state_mlstm_C = np.asarray(state_mlstm_C, f32)
    state_mlstm_n = np.asarray(state_mlstm_n, f32)
    state_mlstm_m = np.asarray(state_mlstm_m, f32)
    key = (tuple(_debug), _stop)
    if key not in _CACHE:
        _CACHE[key] = build_program(debug=_debug, stop=_stop)
    nc, dbg_outs = _CACHE[key]
    cst = _consts()
    shared = dict(
        w_in=np.ascontiguousarray(np.asarray(w_in, f32)[0]),
        b_if=np.ascontiguousarray(np.asarray(b_if, f32)[0]),
        sinks=np.ascontiguousarray(np.asarray(sinks, f32)),
        rel_bias=np.ascontiguousarray(np.asarray(rel_bias, f32)),
        g_attn=np.ascontiguousarray(np.asarray(g_attn_norm, f32)),
        g_head=np.ascontiguousarray(np.asarray(g_head, f32)),
        g_ffn=np.ascontiguousarray(np.asarray(g_ffn_norm, f32)),
        g_final=np.ascontiguousarray(np.asarray(g_final, f32)[None]),
        w_att_out=np.ascontiguousarray(np.asarray(w_att_out, f32)[0]),
        w_mlstm_out=np.ascontiguousarray(np.asarray(w_mlstm_out, f32)[0]),
        w_out=np.ascontiguousarray(np.asarray(w_out, f32)[0]),
        w_gate=np.ascontiguousarray(np.asarray(w_gate, f32)[0]),
        w_up=np.ascontiguousarray(np.asarray(w_up, f32)[0]),
        w_down=np.ascontiguousarray(np.asarray(w_down, f32)[0]),
        **cst,
    )
    in_maps = []
    for c in range(NCORE):
        b, s = c // 4, c % 4
        xs = np.zeros((128 + SEG, D), f32)
        xs[128:] = x_prompt[b, s * SEG:(s + 1) * SEG]
        if s > 0:
            xs[:128] = x_prompt[b, s * SEG - 128:s * SEG]
        flag = np.full((128, 1), NEGB if s == 0 else 0.0, f32)
        m = dict(shared)
        m["xs"] = xs
        m["flag"] = flag
        xprev = np.zeros((3 * SEG, D), f32)
        pact = np.zeros((128, 4), f32)
        for j in range(3):
            sj = s - 3 + j
            if sj >= 0:
                xprev[j * SEG:(j + 1) * SEG] = x_prompt[b, sj * SEG:(sj + 1) * SEG]
                pact[:, j] = 1.0
        m["xprev"] = xprev
        m["pact"] = pact
        sl = slice(c * 16, (c + 1) * 16)
        m["xsm"] = np.ascontiguousarray(x_sample[sl, 0, :])
        m["ck"] = np.ascontiguousarray(cache_k_win[0, sl].reshape(16, 128, 128))
        m["cv"] = np.ascontiguousarray(cache_v_win[0, sl].reshape(16, 128, 128))
        m["sC"] = np.ascontiguousarray(state_mlstm_C[0, sl])
        m["sn"] = np.ascontiguousarray(state_mlstm_n[0, sl])
        m["sm"] = np.ascontiguousarray(state_mlstm_m[0, sl])
        in_maps.append(m)
    res = run_bass_kernel_spmd(nc, in_maps, core_ids=list(range(NCORE)), **({'trace': True} if _trace else {}))
    if _trace:
        print('EXEC_TIME_NS', res.exec_time_ns)
    R = res.results
    y_prompt = np.stack([np.concatenate([R[b * 4 + s]["y"] for s in range(4)], axis=0) for b in range(2)])
    p_k = np.stack([R[b * 4 + 3]["pk"].reshape(128, 2, 64) for b in range(2)])[None]
    p_v = np.stack([R[b * 4 + 3]["pv"].reshape(128, 2, 64) for b in range(2)])[None]
    p_C = np.stack([R[b * 4 + 3]["pC"] for b in range(2)])[None]
    p_n = np.stack([R[b * 4 + 3]["pn"] for b in range(2)])[None]
    p_m = np.stack([R[b * 4 + 3]["pm_out"].reshape(4) for b in range(2)])[None]
    y_sample = np.concatenate([R[c]["ys"] for c in range(NCORE)], axis=0)[:, None, :]
    s_k = np.concatenate([R[c]["sko"].reshape(16, 128, 2, 64) for c in range(NCORE)], axis=0)[None]
    s_v = np.concatenate([R[c]["svo"].reshape(16, 128, 2, 64) for c in range(NCORE)], axis=0)[None]
    s_C = np.concatenate([R[c]["sCo"] for c in range(NCORE)], axis=0)[None]
    s_n = np.concatenate([R[c]["sno"][:, :, 0, :] for c in range(NCORE)], axis=0)[None]
    s_m = np.concatenate([R[c]["smo"][:, :, 0] for c in range(NCORE)], axis=0)[None]
    outs = (y_prompt, y_sample, p_k, p_v, p_C, p_n, p_m, s_k, s_v, s_C, s_n, s_m)
    if _debug:
        return outs, [{k: r["dbg_" + k] for k in dbg_outs} for r in R]
    return outs
```

```python
import contextlib
import math
import numpy as np
import ml_dtypes
import concourse.bass as bass
import concourse.mybir as mybir
from concourse.ap import AP
from concourse.bass_utils import run_bass_kernel_spmd

F32 = mybir.dt.float32
BF16 = mybir.dt.bfloat16
AF = mybir.ActivationFunctionType
ALU = mybir.AluOpType

D = 1024
SEQ = 8192
NCORE = 8
SEG = 2048
NT = 16
SBT = 8
NSB = NT // SBT
TSB = SBT * 128
NBLK = TSB // 512
N_IN = 4360
DFF = 2816
NFF = DFF // 128
EPS = 1e-6
NEGB = -30000.0
O_QA, O_KA, O_VA, O_QM, O_KM, O_VM, O_OM, O_I, O_F, O_GA, O_GM = 0, 512, 640, 768, 1024, 1280, 1792, 2304, 2308, 2312, 3336


class Tile:
    __slots__ = ("name", "w", "r", "excl")

    def __init__(self, name, excl=False):
        self.name = name
        self.w = None
        self.r = {}
        self.excl = excl


class Sched:
    def __init__(self, nc, sems, lanes_per_q=8):
        self.nc = nc
        self.engs = {"pe": nc.tensor, "act": nc.scalar, "dve": nc.vector, "pool": nc.gpsimd, "sp": nc.sync}
        self.q = {k: [] for k in self.engs}
        self.cnt = {k: 0 for k in self.engs}
        it = iter(sems)
        self.sem = {k: next(it) for k in self.engs}
        self.lanes = {}
        for k in ("sp", "pool"):
            self.lanes[k] = [[next(it), 0] for _ in range(lanes_per_q)]
        self.lane_i = {k: 0 for k in self.lanes}
        self.seen = {k: {} for k in self.engs}

    def _need(self, eng, waits, ev, same_ok=False):
        if ev is None:
            return
        sem, val, src = ev
        if src == eng and (same_ok or eng == "pe"):
            return
        key = id(sem)
        if self.seen[eng].get(key, 0) >= val:
            return
        cur = waits.get(key)
        if cur is None or cur[1] < val:
            waits[key] = (sem, val)

    def _deps(self, eng, reads, writes):
        waits = {}
        for t in reads:
            self._need(eng, waits, t.w)
        for t in writes:
            self._need(eng, waits, t.w, same_ok=True)
            for ev in t.r.values():
                self._need(eng, waits, ev, same_ok=True)
        for key, (sem, val) in waits.items():
            self.seen[eng][key] = val
        return list(waits.values())

    def op(self, eng, fn, reads=(), writes=()):
        if any(t.excl for t in reads):
            writes = list(writes) + [t for t in reads if t.excl]
            reads = [t for t in reads if not t.excl]
        waits = self._deps(eng, reads, writes)
        self.cnt[eng] += 1
        sem = self.sem[eng]
        ev = (sem, self.cnt[eng], eng)
        self.q[eng].append((waits, fn, (sem, 1)))
        for t in reads:
            t.r[id(sem)] = ev
        for t in writes:
            t.w = ev
            t.r = {}
        return ev

    def dma(self, q, out_ap, in_ap, reads=(), writes=(), **kw):
        waits = self._deps(q, reads, writes)
        lanes = self.lanes[q]
        li = self.lane_i[q]
        self.lane_i[q] = (li + 1) % len(lanes)
        lane = lanes[li]
        sem = lane[0]
        if lane[1] > 0 and self.seen[q].get(id(sem), 0) < lane[1]:
            waits.append((sem, lane[1]))
            self.seen[q][id(sem)] = lane[1]
        lane[1] += 16
        ev = (sem, lane[1], "dma")

        def fn(e, out_ap=out_ap, in_ap=in_ap, kw=kw):
            return e.dma_start(out=out_ap, in_=in_ap, **kw)

        self.q[q].append((waits, fn, (sem, 16)))
        for t in reads:
            t.r[id(sem)] = ev
        for t in writes:
            t.w = ev
            t.r = {}
        return ev

    def barrier(self):
        evs = []
        for k in self.engs:
            if self.cnt[k] > 0:
                evs.append((self.sem[k], self.cnt[k], k))
        for k, lanes in self.lanes.items():
            for sem, val in lanes:
                if val > 0:
                    evs.append((sem, val, "dma"))
        for k in self.engs:
            waits = []
            for sem, val, src in evs:
                if src == k:
                    continue
                if self.seen[k].get(id(sem), 0) >= val:
                    continue
                self.seen[k][id(sem)] = val
                waits.append((sem, val))
            if waits:
                self.q[k].append((waits, None, None))

    def finish(self):
        waits = []
        for k, lanes in self.lanes.items():
            for sem, val in lanes:
                if val > 0:
                    waits.append((sem, val))
        self.q["sp"].append((waits, None, None))

    def emit(self):
        nc = self.nc
        with nc.Block() as block:
            def mk(name):
                def body(e):
                    for waits, fn, inc in self.q[name]:
                        ws = list(waits)
                        if fn is None:
                            for sem, val in ws:
                                e.wait_ge(sem, val)
                            continue
                        for sem, val in ws[:-1]:
                            e.wait_ge(sem, val)
                        ins = fn(e)
                        if ws:
                            ins._wait_ge(ws[-1][0], ws[-1][1])
                        ins.then_inc(inc[0], inc[1])
                return body
            block.tensor(mk("pe"))
            block.scalar(mk("act"))
            block.vector(mk("dve"))
            block.gpsimd(mk("pool"))
            block.sync(mk("sp"))


def t5_bucket_np(n):
    n = np.maximum(n, 0)
    max_exact = 16
    nf = np.maximum(n, 1).astype(np.float32)
    large = max_exact + (np.log(nf / max_exact) / math.log(128 / max_exact) * (32 - max_exact)).astype(np.int32)
    large = np.minimum(large, 31)
    return np.where(n < max_exact, n, large)


class _Stop(Exception):
    pass


def build_program(debug=(), stop=None, nonce=0.0):
    nc = bass.Bass("TRN2", target_bir_lowering=False)
    dbg_outs = {}

    def stage(name):
        if stop == name:
            raise _Stop()

    def din(name, shape, dt=F32):
        return nc.dram_tensor(name, list(shape), dt, kind="ExternalInput")

    def dout(name, shape, dt=F32):
        return nc.dram_tensor(name, list(shape), dt, kind="ExternalOutput")

    xs_d = din("xs", [128 + SEG, D]).ap()
    flag_d = din("flag", [128, 1]).ap()
    w_in_d = din("w_in", [D, N_IN]).ap()
    b_if_d = din("b_if", [2, 4]).ap()
    sinks_d = din("sinks", [1, 8]).ap()
    relb_d = din("rel_bias", [32, 8]).ap()
    g_attn_d = din("g_attn", [1, D]).ap()
    g_head_d = din("g_head", [1, 512]).ap()
    g_ffn_d = din("g_ffn", [1, D]).ap()
    g_fin_d = din("g_final", [1, D]).ap()
    w_ao_d = din("w_att_out", [512, D]).ap()
    w_mo_d = din("w_mlstm_out", [512, D]).ap()
    w_out_d = din("w_out", [D, D]).ap()
    w_gate_d = din("w_gate", [D, DFF]).ap()
    w_up_d = din("w_up", [D, DFF]).ap()
    w_down_d = din("w_down", [DFF, D]).ap()
    ident_bf_d = din("ident_bf", [128, 128], BF16).ap()
    ident_f_d = din("ident_f", [128, 128]).ap()
    ohT_d = din("ohT_rev", [32, 128]).ap()
    caus_d = din("causT", [128, 128]).ap()
    sel_d = din("sel", [4, 128]).ap()
    pm_d = din("pm", [4, 2]).ap()
    xsm_d = din("xsm", [16, D]).ap()
    ck_d = din("ck", [16, 128, 128])
    cv_d = din("cv", [16, 128, 128])
    sC_d = din("sC", [16, 4, 128, 64])
    sn_d = din("sn", [16, 4, 64])
    sm_d = din("sm", [16, 4])
    xprev_d = din("xprev", [3 * SEG, D]).ap()
    pact_d = din("pact", [128, 4]).ap()

    y_d = dout("y", [SEG, D]).ap()
    pk_d = dout("pk", [128, 128]).ap()
    pv_d = dout("pv", [128, 128]).ap()
    pC_d = dout("pC", [4, 128, 64]).ap()
    pn_d = dout("pn", [4, 64]).ap()
    pm_out_d = dout("pm_out", [4, 1]).ap()
    ys_d = dout("ys", [16, D]).ap()
    sko_d = dout("sko", [16, 128, 128])
    svo_d = dout("svo", [16, 128, 128])
    sCo_d = dout("sCo", [16, 4, 128, 64])
    sno_d = dout("sno", [16, 4, 2, 64])
    smo_d = dout("smo", [16, 4, 2])
    wr_scr = nc.dram_tensor("wr_scr", [8, 512], F32)

    es = contextlib.ExitStack()
    with es:
        def sb(name, shape, dt=F32):
            return es.enter_context(nc.sbuf_tensor("sb_" + name, list(shape), dt))

        sems = [es.enter_context(nc.semaphore(f"s{i}")) for i in range(5 + 16)]
        S = Sched(nc, sems)
        ps = es.enter_context(nc.psum_tensor("ps", [128, 8, 512], F32))
        PB = [Tile(f"bank{i}", excl=True) for i in range(8)]
        bank_rr = [0]

        def nextbank():
            b = bank_rr[0]
            bank_rr[0] = (b + 1) % 8
            return b

        def ps_bf(b):
            return ps[:, b, :].bitcast(BF16)

        def ACT(out, in_, func, reads, writes, **kw):
            S.op("act", lambda e: e.activation(out, in_, func, **kw), reads, writes)

        def TT(out, a, b, op, reads, writes, eng="dve"):
            S.op(eng, lambda e: e.tensor_tensor(out, a, b, op), reads, writes)

        def TS(out, a, s1, s2, op0, op1, reads, writes, eng="dve"):
            S.op(eng, lambda e: e.tensor_scalar(out, a, s1, s2, op0, op1), reads, writes)

        def STT(out, a, scalar, b, op0, op1, reads, writes, eng="dve"):
            S.op(eng, lambda e: e.scalar_tensor_tensor(out, a, scalar, b, op0, op1), reads, writes)

        def CP(out, in_, reads, writes, eng="dve"):
            S.op(eng, lambda e: e.tensor_copy(out, in_), reads, writes)

        def MM(out, lhsT, rhs, start, stop, reads, writes):
            S.op("pe", lambda e: e.matmul(out, lhsT, rhs, start=start, stop=stop), reads, writes)

        def TR(out, in_, ident, reads, writes):
            S.op("pe", lambda e: e.transpose(out, in_, ident), reads, writes)

        def DMA(q, out, in_, reads, writes, **kw):
            S.dma(q, out, in_, reads, writes, **kw)

        class Ring:
            def __init__(self, name, shape, dt, n):
                self.bufs = [(sb(f"{name}{i}", shape, dt), Tile(f"{name}{i}")) for i in range(n)]
                self.i = 0

            def next(self):
                r = self.bufs[self.i]
                self.i = (self.i + 1) % len(self.bufs)
                return r

        in_pre = [False]
        pre_tiles = {}
        pre_ones_done = set()
        pre_ring = [None]

        def dbg(name, ap, tile, shape, dt=F32):
            if name not in debug or (in_pre[0] and name in dbg_outs):
                return
            o = dout("dbg_" + name, shape, dt).ap()
            dbg_outs[name] = o
            DMA("sp", o, ap, [tile], [Tile("dbgo_" + name)])

        ident_bf = sb("ident_bf", [128, 128], BF16); Tidb = Tile("idb")
        ident_f = sb("ident_f", [128, 128]); Tidf = Tile("idf")
        caus = sb("caus", [128, 128]); Tcaus = Tile("caus")
        ones_bf = sb("ones_bf", [128, 128], BF16); Tones = Tile("ones")
        one_c = sb("one_c", [128, 1]); Tonesr = Tile("onec")
        BTall = sb("BTall", [128, 2, 2, 2, 2, 128]); TBT = Tile("BTall")
        esink = sb("esink", [1, 8, 128], BF16); Tesink = Tile("esink")
        flag = sb("flag", [128, 1]); Tflag = Tile("flag")
        zero_c = sb("zero_c", [128, 1]); Tzero = Tile("zero")
        eps_c = sb("eps_c", [128, 1]); Teps = Tile("eps")
        g_attn = sb("g_attnT", [128, 8]); Tgattn = Tile("gattn")
        g_ffn = sb("g_ffnT", [128, 8]); Tgffn = Tile("gffn")
        g_fin = sb("g_fin", [128, D]); Tgfin = Tile("gfin")
        g_head = sb("g_head", [128, 512]); Tghead = Tile("ghead")
        b_i = sb("b_i", [4, 1]); b_f = sb("b_f", [4, 1]); Tbif = Tile("bif")
        selm = sb("selm", [4, 128]); pmm = sb("pmm", [4, 2]); Tsel = Tile("sel")
        kaT_n = sb("kaT_n", [128, 128 + SEG], BF16)
        kaT_s = sb("kaT_s", [128, 128 + SEG], BF16)
        TkaT = [Tile(f"kaT{t}") for t in range(NT + 1)]
        va = sb("va", [128, NT + 1, 2, 2, 64], BF16)
        Tva = [Tile(f"va{t}") for t in range(NT + 1)]
        ST = sb("ST", [128, 2, 129]); TST = Tile("ST")
        rstate = sb("rstate", [4, 4]); Trst = Tile("rstate")
        hT = sb("hT", [128, 8, TSB], BF16)
        ThT = [Tile(f"hT{t}") for t in range(SBT)]
        hTh = sb("hTh", [128, 8, 128], BF16); ThTh = Tile("hTh")
        yattT = sb("yattT", [128, 4, TSB], BF16)
        TyaT = [Tile(f"yaT{t}") for t in range(SBT)]
        ymT = sb("ymT", [128, 4, TSB], BF16)
        TymT = [Tile(f"ymT{t}") for t in range(SBT)]
        hTs = sb("hTs", [128, 8, 16], BF16); ThTs = Tile("hTs")
        zs = sb("zs", [16, 2312]); Tzs = Tile("zs")
        yattTs = sb("yattTs", [128, 4, 16], BF16); TyaTs = Tile("yaTs")
        ymTs = sb("ymTs", [128, 4, 16], BF16); TymTs = Tile("ymTs")
        wbuf = sb("wbuf", [128, 3 * 4096], BF16)

        class VRing:
            def __init__(self, name, n, size):
                self.bufs = [(wbuf[:, i * size:(i + 1) * size], Tile(f"{name}{i}")) for i in range(n)]
                self.i = 0
                self.n = n

            def next(self):
                r = self.bufs[self.i]
                self.i = (self.i + 1) % len(self.bufs)
                return r
        xring = Ring("xt", [128, D], F32, 3)
        hbring = Ring("hbf", [128, D], BF16, 2)
        d4ring = Ring("d4", [128, 12], F32, 4)
        scal = sb("scal", [128, 64]); scal_i = [0]
        Tscal = [Tile(f"scal{i}") for i in range(64)]

        def newscal():
            i = scal_i[0]
            scal_i[0] = (i + 1) % 64
            return scal[:, i:i + 1], Tscal[i]

        ARENA_BYTES = 83 * 1024
        arena = sb("arena", [128, ARENA_BYTES // 4], F32)

        def carve(off_bytes, shape, dt):
            esz = 2 if dt == BF16 else 4
            n = int(np.prod(shape[1:]))
            assert off_bytes % 4 == 0 and off_bytes + n * esz <= ARENA_BYTES, (off_bytes, shape)
            v = arena[0:shape[0], off_bytes // 4: off_bytes // 4 + (n * esz) // 4]
            if dt == BF16:
                v = v.bitcast(BF16)
            if len(shape) == 2:
                return v
            names = "abcdef"[: len(shape) - 1]
            pat = "p (" + " ".join(names) + ") -> p " + " ".join(names)
            kw = {names[i]: shape[i + 1] for i in range(len(shape) - 1)}
            return v.rearrange(pat, **kw)

        DMA("sp", ident_bf[:], ident_bf_d, [], [Tidb])
        DMA("sp", ident_f[:], ident_f_d, [], [Tidf])
        DMA("sp", caus[:], caus_d, [], [Tcaus])
        DMA("sp", flag[:], flag_d, [], [Tflag])
        DMA("sp", g_attn[:], AP(g_attn_d.tensor, 0, [[1, 128], [128, 8]]), [], [Tgattn], allow_slow_non_contiguous=True)
        DMA("sp", g_ffn[:], AP(g_ffn_d.tensor, 0, [[1, 128], [128, 8]]), [], [Tgffn], allow_slow_non_contiguous=True)
        DMA("sp", g_fin[:], AP(g_fin_d.tensor, 0, [[0, 128], [1, D]]), [], [Tgfin])
        DMA("sp", g_head[:], AP(g_head_d.tensor, 0, [[0, 128], [1, 512]]), [], [Tghead])
        DMA("sp", b_i[:], AP(b_if_d.tensor, 0, [[1, 4], [1, 1]]), [], [Tbif])
        DMA("sp", b_f[:], AP(b_if_d.tensor, 4, [[1, 4], [1, 1]]), [], [Tbif])
        DMA("sp", selm[:], sel_d, [], [Tsel])
        DMA("sp", pmm[:], pm_d, [], [Tsel])
        S.op("dve", lambda e: e.memset(ones_bf[:], 1.0), [], [Tones])
        S.op("dve", lambda e: e.memset(one_c[:], 1.0), [], [Tonesr])
        S.op("dve", lambda e: e.memset(zero_c[:], 0.0), [], [Tzero])
        S.op("dve", lambda e: e.memset(eps_c[:], EPS), [], [Teps])
        S.op("dve", lambda e: e.memset(ST[:], 0.0), [], [TST])
        S.op("dve", lambda e: e.memset(rstate[:], 0.0), [], [Trst])

        relb = carve(0, [32, 8], F32); Trelb = Tile("relb")
        ohT = carve(64, [32, 128], F32); TohT = Tile("ohT")
        DMA("sp", relb[:], relb_d, [], [Trelb])
        DMA("sp", ohT[:], ohT_d, [], [TohT])
        tbr = carve(1024, [8, 512], F32); Ttbr = Tile("tbr")
        S.op("dve", lambda e: e.memset(tbr[:], NEGB), [], [Ttbr])
        b0 = nextbank()
        MM(ps[0:8, b0, 0:128], relb[:], ohT[:], True, True, [Trelb, TohT], [PB[b0]])
        CP(tbr[:, 256:384], ps[0:8, b0, 0:128], [PB[b0]], [Ttbr])
        Tscr = Tile("wr_scr")
        DMA("sp", wr_scr.ap(), tbr[:], [Ttbr], [Tscr])
        hk = carve(4096, [128, 8, 128], F32); Thk = Tile("hk")
        for c, base in ((1, 256), (0, 128)):
            DMA("sp", hk[:], AP(wr_scr, base, [[1, 128], [512, 8], [1, 128]]), [Tscr], [Thk])
            hv = hk[:]
            pst = list(hv.ap[0])
            for g in range(2):
                rev = AP(hv.tensor, hv.offset + (4 * g) * 128 + 127, [pst, [128, 2], [256, 2], [-1, 128]])
                CP(BTall[:, g, :, c, :, :], rev, [Thk], [TBT])
        sk = carve(8192 + 64, [1, 8], F32); Tsk = Tile("sk")
        DMA("sp", sk[:], sinks_d, [], [Tsk])
        ACT(sk[:], sk[:], AF.Exp, [Tsk], [Tsk])
        CP(esink[:], sk[:].unsqueeze(2).to_broadcast([1, 8, 128]), [Tsk], [Tesink])
        dbg("BTall", BTall[:], TBT, [128, 2, 2, 2, 2, 128])
        S.barrier()

        w_in_v = w_in_d.rearrange("(k p) n -> p k n", p=128)
        w_gate_v = w_gate_d.rearrange("(k p) n -> p k n", p=128)
        w_up_v = w_up_d.rearrange("(k p) n -> p k n", p=128)
        w_out_v = w_out_d.rearrange("(k p) n -> p k n", p=128)
        w_ao_v = w_ao_d.rearrange("(k p) n -> p k n", p=128)
        w_mo_v = w_mo_d.rearrange("(k p) n -> p k n", p=128)
        w_down_v = w_down_d.rearrange("(k p) n -> p k n", p=128)

        class WStream:
            def __init__(self, ring):
                self.ring = ring
                self.specs = []
                self.loaded = 0
                self.views = {}

            def add(self, K, cols, parts):
                self.specs.append((K, cols, parts))
                return len(self.specs) - 1

            def _load(self, i):
                K, cols, parts = self.specs[i]
                slot, tl = self.ring.next()
                v = slot[:, 0:K * cols].rearrange("p (k c) -> p k c", k=K)
                for (c0, n, src) in parts:
                    DMA("pool", v[:, :, c0:c0 + n], src, [], [tl])
                self.views[i] = (v, tl)

            def get(self, i, live_from=None):
                lf = i if live_from is None else live_from
                while self.loaded < min(lf + self.ring.n, len(self.specs)):
                    self._load(self.loaded)
                    self.loaded += 1
                return self.views[i]

        TR_BANKS = (7, 6)
        tr_i = [0]

        def norm_T(src_ap, src_reads, g_bc, Tg, dstT, Tdst, tn=128, src_is_dram=True, keep=None):
            if src_is_dram:
                xt, Txt = xring.next()
                DMA("sp", xt[0:tn, :], src_ap, src_reads, [Txt])
                xin = xt[0:tn, :]
                rd = [Txt]
            else:
                xin = src_ap
                rd = list(src_reads)
            hb, Thb = hbring.next()
            ss, Tss = newscal()
            ACT(hb[0:tn, :], xin, AF.Square, rd, [Thb, Tss], accum_out=ss[0:tn, :])
            rr, Trr = newscal()
            ACT(rr[0:tn, :], ss[0:tn, :], AF.Ln, [Tss, Teps], [Trr], scale=1.0 / D, bias=eps_c[0:tn, 0:1])
            ACT(rr[0:tn, :], rr[0:tn, :], AF.Exp, [Trr], [Trr], scale=-0.5)
            ACT(hb[0:tn, :], xin, AF.Copy, rd + [Trr], [Thb], scale=rr[0:tn, 0:1])
            trb = TR_BANKS[tr_i[0] % 2]
            tr_i[0] += 1
            pv = ps_bf(trb)
            for k in range(8):
                TR(pv[:, k * 128:k * 128 + tn], hb[0:tn, k * 128:(k + 1) * 128], ident_bf[0:tn, 0:tn], [Thb, Tidb], [PB[trb]])
            src = pv.rearrange("p (k t) -> p k t", k=8)[:, :, 0:tn]
            TT(dstT, src, g_bc[:, :].unsqueeze(2).to_broadcast([128, 8, tn]), ALU.mult, [PB[trb], Tg], [Tdst])


        def sample_phase():
            S.barrier()
            RX = mybir.AxisListType.X
            o = 0
            Cl = carve(o, [128, 64, 64], F32); o += 16384
            tmpC = carve(o, [128, 64, 64], F32)
            Kf = carve(o, [128, 16, 128], F32)
            Vf = carve(o + 8192, [128, 16, 128], F32); o += 16384
            Kb = carve(o, [128, 16, 128], BF16); o += 4096
            KT = carve(o, [128, 16, 128], BF16); o += 4096
            V1 = carve(o, [128, 16, 2, 66], BF16); o += 16 * 2 * 66 * 2
            qg = carve(o, [16, 4, 128], BF16); o += 1024
            qTs = carve(o, [128, 4, 16], BF16); o += 128
            bcol = carve(o, [128, 8], F32); o += 32
            Ss = carve(o, [128, 2, 16, 4], F32); o += 512
            Es = carve(o, [128, 2, 16, 4], BF16); o += 256
            pvs = carve(o, [4, 32, 66], F32); o += 32 * 66 * 4
            yas = carve(o, [16, 8, 66], F32); o += 8 * 66 * 4
            t512 = carve(o, [16, 512], F32); o += 2048
            u512 = carve(o, [16, 512], F32); o += 2048
            ogs = carve(o, [16, 512], F32); o += 2048
            hms = carve(o, [16, 512], F32); o += 2048
            yab = carve(o, [16, 512], BF16); o += 1024
            ymb = carve(o, [16, 512], BF16); o += 1024
            s16 = carve(o, [16, 64], F32); o += 256
            ql = carve(o, [128, 64], F32); o += 256
            kl = carve(o, [128, 64], F32); o += 256
            vl = carve(o, [128, 64], F32); o += 256
            nl = carve(o, [128, 64], F32); o += 256
            hl = carve(o, [128, 64], F32); o += 256
            t64 = carve(o, [128, 64], F32); o += 256
            Cq = carve(o, [128, 64], F32); o += 256
            sl_ = carve(o, [128, 32], F32); o += 128
            assert o <= ARENA_BYTES, o
            TCl, TtmpC, TKf, TVf, TKb, TKT, TV1, Tqg, TqTs, Tbcol, TSs, TEs, Tpvs, Tyas = [Tile(n) for n in
                ("Cl", "tmpC", "Kf", "Vf", "Kb", "KT", "V1", "qg", "qTs", "bcol", "Ss", "Es", "pvs", "yas")]
            TKf = TtmpC
            TVf = TtmpC
            Tt512, Tu512, Togs, Thms, Tyab, Tymb, Ts16, Tql, Tkl, Tvl, Tnl, Thl, Tt64, TCq, Tsl = [Tile(n) for n in
                ("t512", "u512", "ogs", "hms", "yab", "ymb", "s16", "ql", "kl", "vl", "nl", "hl", "t64", "Cq", "sl")]
            DMA("sp", Kf[:], ck_d.ap().rearrange("b k c -> k b c"), [], [TKf])
            DMA("pool", Vf[:], cv_d.ap().rearrange("b k c -> k b c"), [], [TVf])
            sCv = sC_d.ap().rearrange("b h (vh v) d -> vh (b h) (v d)", vh=2)
            sCov = sCo_d.ap().rearrange("b h (vh v) d -> vh (b h) (v d)", vh=2)
            Clf = Cl[:].rearrange("p v d -> p (v d)")
            for vh in range(2):
                ln = slice(vh * 64, (vh + 1) * 64)
                DMA("sp", Clf[ln, :], sCv[vh], [], [TCl])
                DMA("pool", nl[ln, :], sn_d.ap().rearrange("b h d -> (b h) d"), [], [Tnl])
                DMA("sp", sl_[ln, 0:1], sm_d.ap().rearrange("b (h o) -> (b h) o", o=1), [], [Tsl])
                DMA("pool", ql[ln, :], zs[:, 768:1024].rearrange("b (h d) -> b h d", h=4), [Tzs], [Tql])
                DMA("sp", kl[ln, :], zs[:, 1024:1280].rearrange("b (h d) -> b h d", h=4), [Tzs], [Tkl])
                DMA("pool", vl[ln, :], zs[:, 1280:1792].rearrange("b (h w v) -> b h w v", h=4, w=2)[:, :, vh, :], [Tzs], [Tvl])
                DMA("sp", sl_[ln, 1:2], zs[:, 2304:2308].rearrange("b (h o) -> b h o", o=1), [Tzs], [Tsl])
                DMA("pool", sl_[ln, 2:3], zs[:, 2308:2312].rearrange("b (h o) -> b h o", o=1), [Tzs], [Tsl])
            DMA("sp", sl_[:, 3:4], AP(b_if_d.tensor, 0, [[0, 32], [1, 4], [1, 1]]), [], [Tsl])
            DMA("pool", sl_[:, 4:5], AP(b_if_d.tensor, 4, [[0, 32], [1, 4], [1, 1]]), [], [Tsl])
            DMA("sp", bcol[:], AP(wr_scr, 255, [[1, 128], [512, 8]]), [Tscr], [Tbcol], allow_slow_non_contiguous=True)
            DMA("pool", s16[:, 0:8], AP(wr_scr, 383, [[0, 16], [512, 8]]), [Tscr], [Ts16], allow_slow_non_contiguous=True)
            DMA("sp", s16[:, 8:16], AP(sinks_d.tensor, 0, [[0, 16], [1, 8]]), [], [Ts16])
            DMA("pool", sko_d.ap()[:, 0:127, :], ck_d.ap()[:, 1:128, :], [], [Tile("sko")])
            DMA("sp", svo_d.ap()[:, 0:127, :], cv_d.ap()[:, 1:128, :], [], [Tile("svo")])
            DMA("pool", sko_d.ap()[:, 127, :], zs[:, 512:640], [Tzs], [Tile("sko2")])
            DMA("sp", svo_d.ap()[:, 127, :], zs[:, 640:768], [Tzs], [Tile("svo2")])
            CP(Kb[:], Kf[:], [TKf], [TKb])
            for half in range(2):
                bk = 4 + half
                pvb = ps_bf(bk)
                for i in range(8):
                    b_ = half * 8 + i
                    TR(pvb[:, i * 128:(i + 1) * 128], Kb[:, b_, :], ident_bf[:], [TKb, Tidb], [PB[bk]])
                ACT(KT[:, half * 8:(half + 1) * 8, :], pvb.rearrange("p (i k) -> p i k", i=8), AF.Copy, [PB[bk]], [TKT])
            CP(V1[:, :, :, 0:64], Vf[:].rearrange("k b (g d) -> k b g d", g=2), [TVf], [TV1])
            S.op("dve", lambda e: e.memset(V1[:, :, :, 64:65], 1.0), [], [TV1])
            TS(qg[:].rearrange("b j (g d) -> b j g d", g=2), zs[:, 0:512].rearrange("b (g j d) -> b j g d", g=2, j=4), 0.125, None, ALU.mult, ALU.bypass, [Tzs], [Tqg])
            pvq = ps_bf(6)
            for j in range(4):
                TR(pvq[:, j * 16:(j + 1) * 16], qg[:, j, :], ident_bf[0:16, 0:16], [Tqg, Tidb], [PB[6]])
            CP(qTs[:].rearrange("p j b -> p (j b)"), pvq[:, 0:64], [PB[6]], [TqTs])
            for b_ in range(16):
                for g in range(2):
                    gr = slice(g * 64, (g + 1) * 64)
                    MM(ps[:, g, b_ * 4:(b_ + 1) * 4], KT[gr, b_, :], qTs[gr, :, b_], True, True, [TKT, TqTs], [PB[g]])
            for g in range(2):
                TT(Ss[:, g, :, :], ps[:, g, 0:64].rearrange("p (b j) -> p b j", j=4), bcol[:, 4 * g:4 * g + 4].unsqueeze(1).to_broadcast([128, 16, 4]), ALU.add, [PB[g], Tbcol], [TSs])
            ACT(Es[:], Ss[:], AF.Exp, [TSs], [TEs])
            for b_ in range(16):
                for g in range(2):
                    slot = b_ * 2 + g
                    bk = 2 + slot // 7 if slot < 28 else 7
                    if slot >= 28:
                        col = (slot - 28) * 66
                    else:
                        col = (slot % 7) * 66
                    bk = (2 + slot // 7) if slot < 28 else 7
                    bk = [2, 3, 4, 5, 7][min(slot // 7, 4)]
                    col = (slot % 7) * 66
                    MM(ps[0:4, bk, col:col + 65], Es[:, g, b_, :], V1[:, b_, g, 0:65], True, True, [TEs, TV1], [PB[bk]])
            for gi_, bk in enumerate([2, 3, 4, 5, 7]):
                ns = 7 if gi_ < 4 else 4
                CP(pvs[:, gi_ * 7:gi_ * 7 + ns, :], ps[0:4, bk, 0:ns * 66].rearrange("p (s c) -> p s c", c=66), [PB[bk]], [Tpvs])
            yas_v = yas[:].rearrange("b (g j) c -> b g j c", g=2)
            for j in range(4):
                DMA("pool", yas_v[:, :, j, :], pvs[j:j + 1, :, :], [Tpvs], [Tyas])
            qv = zs[:, 0:512].rearrange("b (g j d) -> b g j d", g=2, j=4)
            kv = zs[:, 512:640].rearrange("b (g d) -> b g d", g=2).unsqueeze(2).to_broadcast([16, 2, 4, 64])
            TT(t512[:].rearrange("b (g j d) -> b g j d", g=2, j=4), qv, kv, ALU.mult, [Tzs], [Tt512])
            S.op("dve", lambda e: e.tensor_reduce(s16[:, 16:24], t512[:].rearrange("b (h d) -> b h d", d=64), RX, ALU.add), [Tt512], [Ts16])
            STT(s16[:, 16:24], s16[:, 16:24], 0.125, s16[:, 0:8], ALU.mult, ALU.add, [Ts16], [Ts16])
            ACT(s16[:, 16:24], s16[:, 16:24], AF.Exp, [Ts16], [Ts16])
            ACT(s16[:, 24:32], s16[:, 8:16], AF.Exp, [Ts16], [Ts16])
            TT(s16[:, 32:40], yas[:, :, 64], s16[:, 16:24], ALU.add, [Tyas, Ts16], [Ts16])
            TT(s16[:, 32:40], s16[:, 32:40], s16[:, 24:32], ALU.add, [Ts16], [Ts16])
            S.op("dve", lambda e: e.reciprocal(s16[:, 32:40], s16[:, 32:40]), [Ts16], [Ts16])
            vv = zs[:, 640:768].rearrange("b (g d) -> b g d", g=2).unsqueeze(2).to_broadcast([16, 2, 4, 64])
            es_b = s16[:, 16:24].rearrange("b (g j) -> b g j", g=2).unsqueeze(3).to_broadcast([16, 2, 4, 64])
            TT(u512[:].rearrange("b (g j d) -> b g j d", g=2, j=4), vv, es_b, ALU.mult, [Tzs, Ts16], [Tu512])
            TT(u512[:].rearrange("b (h d) -> b h d", d=64), u512[:].rearrange("b (h d) -> b h d", d=64), yas[:, :, 0:64], ALU.add, [Tu512, Tyas], [Tu512])
            TT(yab[:].rearrange("b (h d) -> b h d", d=64), u512[:].rearrange("b (h d) -> b h d", d=64),
               s16[:, 32:40].unsqueeze(2).to_broadcast([16, 8, 64]), ALU.mult, [Tu512, Ts16], [Tyab])
            pva = ps_bf(6)
            for p in range(4):
                TR(pva[:, p * 16:(p + 1) * 16], yab[:, p * 128:(p + 1) * 128], ident_bf[0:16, 0:16], [Tyab, Tidb], [PB[6]])
            CP(yattTs[:].rearrange("p c b -> p (c b)"), pva[:, 0:64], [PB[6]], [TyaTs])
            TS(kl[:], kl[:], 0.125, None, ALU.mult, ALU.bypass, [Tkl], [Tkl])
            c_ = lambda i: sl_[:, i:i + 1]
            TT(c_(5), c_(1), c_(3), ALU.add, [Tsl], [Tsl])
            TT(c_(6), c_(2), c_(4), ALU.add, [Tsl], [Tsl])
            ACT(c_(6), c_(6), AF.Exp, [Tsl], [Tsl], scale=-1.0)
            ACT(c_(6), c_(6), AF.Ln, [Tsl], [Tsl], bias=1.0)
            TT(c_(7), c_(0), c_(6), ALU.subtract, [Tsl], [Tsl])
            TT(c_(8), c_(7), c_(5), ALU.max, [Tsl], [Tsl])
            TT(c_(9), c_(7), c_(8), ALU.subtract, [Tsl], [Tsl])
            TT(c_(10), c_(5), c_(8), ALU.subtract, [Tsl], [Tsl])
            TS(c_(11), c_(8), -1.0, None, ALU.mult, ALU.bypass, [Tsl], [Tsl])
            ACT(sl_[:, 9:12], sl_[:, 9:12], AF.Exp, [Tsl], [Tsl])
            TT(tmpC[:], Cl[:], ql[:].unsqueeze(1).to_broadcast([128, 64, 64]), ALU.mult, [TCl, Tql], [TtmpC])
            S.op("dve", lambda e: e.tensor_reduce(Cq[:], tmpC[:], RX, ALU.add), [TtmpC], [TCq])
            TT(t64[:], nl[:], ql[:], ALU.mult, [Tnl, Tql], [Tt64])
            S.op("dve", lambda e: e.tensor_reduce(c_(12), t64[:], RX, ALU.add), [Tt64], [Tsl])
            TT(t64[:], kl[:], ql[:], ALU.mult, [Tkl, Tql], [Tt64])
            S.op("dve", lambda e: e.tensor_reduce(c_(13), t64[:], RX, ALU.add), [Tt64], [Tsl])
            TT(c_(14), c_(10), c_(13), ALU.mult, [Tsl], [Tsl])
            STT(c_(15), c_(12), c_(9), c_(14), ALU.mult, ALU.add, [Tsl], [Tsl])
            STT(c_(16), c_(15), -1.0, c_(15), ALU.mult, ALU.max, [Tsl], [Tsl])
            TT(c_(16), c_(16), c_(11), ALU.max, [Tsl], [Tsl])
            S.op("dve", lambda e: e.reciprocal(c_(16), c_(16)), [Tsl], [Tsl])
            TS(hl[:], Cq[:], c_(9), None, ALU.mult, ALU.bypass, [TCq, Tsl], [Thl])
            STT(hl[:], vl[:], c_(14), hl[:], ALU.mult, ALU.add, [Tvl, Tsl, Thl], [Thl])
            TS(hl[:], hl[:], c_(16), None, ALU.mult, ALU.bypass, [Thl, Tsl], [Thl])
            TT(tmpC[:], vl[:].unsqueeze(2).to_broadcast([128, 64, 64]), kl[:].unsqueeze(1).to_broadcast([128, 64, 64]), ALU.mult, [Tvl, Tkl, TCq], [TtmpC])
            TS(Cl[:], Cl[:], c_(9), None, ALU.mult, ALU.bypass, [TCl, Tsl], [TCl])
            STT(Cl[:], tmpC[:], c_(10), Cl[:], ALU.mult, ALU.add, [TtmpC, Tsl, TCl], [TCl])
            TS(nl[:], nl[:], c_(9), None, ALU.mult, ALU.bypass, [Tnl, Tsl, Tt64], [Tnl])
            STT(nl[:], kl[:], c_(10), nl[:], ALU.mult, ALU.add, [Tkl, Tsl, Tnl], [Tnl])
            snov = sno_d.ap().rearrange("b h w d -> w (b h) d")
            smov = smo_d.ap().rearrange("b h (w o) -> w (b h) o", o=1)
            hms_v = hms[:].rearrange("b (h w v) -> b h w v", h=4, w=2)
            for vh in range(2):
                ln = slice(vh * 64, (vh + 1) * 64)
                DMA("sp", sCov[vh], Clf[ln, :], [TCl], [Tile("sCo")])
                DMA("pool", snov[vh], nl[ln, :], [Tnl], [Tile("sno")])
                DMA("sp", smov[vh], sl_[ln, 8:9], [Tsl], [Tile("smo")], allow_slow_non_contiguous=True)
                DMA("pool", hms_v[:, :, vh, :], hl[ln, :], [Thl], [Thms])
            TT(t512[:], hms[:], hms[:], ALU.mult, [Thms], [Tt512])
            S.op("dve", lambda e: e.tensor_reduce(s16[:, 40:44], t512[:].rearrange("b (h v) -> b h v", v=128), RX, ALU.add), [Tt512], [Ts16])
            ACT(s16[:, 40:44], s16[:, 40:44], AF.Ln, [Ts16, Teps], [Ts16], scale=1.0 / 128, bias=eps_c[0:16, 0:1])
            ACT(s16[:, 40:44], s16[:, 40:44], AF.Exp, [Ts16], [Ts16], scale=-0.5)
            ACT(ogs[:], zs[:, 1792:2304], AF.Sigmoid, [Tzs], [Togs])
            TT(ogs[:], ogs[:], g_head[0:16, :], ALU.mult, [Togs, Tghead], [Togs])
            TT(u512[:].rearrange("b (h v) -> b h v", v=128), hms[:].rearrange("b (h v) -> b h v", v=128),
               s16[:, 40:44].unsqueeze(2).to_broadcast([16, 4, 128]), ALU.mult, [Thms, Ts16, Tu512], [Tu512])
            TT(ymb[:], u512[:], ogs[:], ALU.mult, [Tu512, Togs], [Tymb])
            pvm = ps_bf(6)
            for h in range(4):
                TR(pvm[:, h * 16:(h + 1) * 16], ymb[:, h * 128:(h + 1) * 128], ident_bf[0:16, 0:16], [Tymb, Tidb], [PB[6]])
            CP(ymTs[:].rearrange("p c b -> p (c b)"), pvm[:, 0:64], [PB[6]], [TymTs])
            dbg("yab", yab[:], Tyab, [16, 512], BF16)
            dbg("ymb", ymb[:], Tymb, [16, 512], BF16)
            dbg("zs", zs[:], Tzs, [16, 2312])

        try:
            stage("setup")
            def superblock(sbi, pre, ST, TST, rstate, Trst, xrow, par_=0):
                gt0 = sbi * SBT

                def T_(name):
                    if pre:
                        name = f"{name}_p{par_}"
                        if name not in pre_tiles:
                            pre_tiles[name] = Tile(name)
                        return pre_tiles[name]
                    return Tile(name)
                if pre and par_ == 1:
                    hT_ = carve(66 * 1024, [128, 8, TSB], BF16)
                    ThT_ = [T_(f"hTalt{t}") for t in range(SBT)]
                else:
                    hT_, ThT_ = hT, ThT
                o = 0
                if pre:
                    pb_ = par_ * 33 * 1024
                    km_tok = carve(pb_, [128, SBT, 256], BF16); pb_ += SBT * 256 * 2
                    vm1 = carve(pb_, [128, SBT, 4, 130], BF16); pb_ += SBT * 4 * 130 * 2
                    rows = [carve(pb_ + i * TSB * 4, [4, TSB], F32) for i in range(4)]; pb_ += 4 * TSB * 4
                    ecol = carve(pb_, [128, SBT, 8], F32); pb_ += SBT * 8 * 4
                    gbb = carve(pb_, [128, SBT, 2], F32); pb_ += SBT * 2 * 4
                    Gm = carve(pb_, [4, SBT, 2], F32); pb_ += SBT * 2 * 4
                    gsm = carve(pb_, [4, SBT + 1], F32); pb_ += (SBT + 2) * 4
                    pre_cw0 = pb_
                    assert pb_ + 64 + 2 * 1040 + 1032 <= (par_ + 1) * 33 * 1024
                if pre:
                    _save = (km_tok, vm1, rows, ecol, gbb, Gm, gsm)
                qaT = carve(o, [128, 4, TSB], BF16); o += 4 * TSB * 2
                qmT = carve(o, [128, 2, TSB], BF16); o += 2 * TSB * 2
                kmT = carve(o, [128, 2, TSB], BF16); o += 2 * TSB * 2
                km_tok = carve(o, [128, SBT, 256], BF16); o += SBT * 256 * 2
                vm1 = carve(o, [128, SBT, 4, 130], BF16); o += SBT * 4 * 130 * 2
                og = carve(o, [128, SBT, 512], BF16); o += SBT * 512 * 2
                rows = [carve(o + i * TSB * 4, [4, TSB], F32) for i in range(4)]; o += 4 * TSB * 4
                ecol = carve(o, [128, SBT, 8], F32); o += SBT * 8 * 4
                gbb = carve(o, [128, SBT, 2], F32); o += SBT * 2 * 4
                Gm = carve(o, [4, SBT, 2], F32); o += SBT * 2 * 4
                gsm = carve(o, [4, SBT + 1], F32); o += (SBT + 2) * 4
                sg_r = [(carve(o + i * 2048, [128, 512], F32), T_(f"sgtmp{i}")) for i in range(2)]; o += 4096
                cw0 = o
                if pre:
                    km_tok, vm1, rows, ecol, gbb, Gm, gsm = _save
                    cw0 = pre_cw0
                TqaT = [T_(f"qaT{i}") for i in range(NBLK)]
                TqmT = [T_(f"qmT{i}") for i in range(NBLK)]
                TkmT = [T_(f"kmT{i}") for i in range(NBLK)]
                Tkmtok = [T_(f"kmtok{i}") for i in range(SBT)]
                Tvm1 = [T_(f"vm1{i}") for i in range(SBT)]
                Tog = [T_(f"og{i}") for i in range(SBT)]
                Trow = [T_(f"row{i}") for i in range(4)]
                Tecol = T_("ecol"); Tgbb = T_("gbb"); TGm = T_("Gm"); Tgsm = T_("gsm")

                if sbi == 0 and not pre:
                    norm_T(xs_d[0:128, :], [], g_attn, Tgattn, hTh[:], ThTh)
                    norm_T(xsm_d, [], g_attn, Tgattn, hTs[:], ThTs, tn=16)
                for t in range(SBT):
                    norm_T(xrow(gt0 + t), [], g_attn, Tgattn, hT_[:, :, t * 128:(t + 1) * 128], ThT_[t])
                    if pre:
                        yield "A"

                stage(f"A{sbi}")
                W = WStream(pre_ring[0] if pre else VRing(f"wA{sbi}_", 3, 4096))
                if not pre:
                    c_qa = W.add(8, 512, [(0, 512, w_in_v[:, :, O_QA:O_QA + 512])])
                    c_k = W.add(8, 512, [(0, 128, w_in_v[:, :, O_KA:O_KA + 128]),
                                         (128, 64, w_in_v[:, :, O_KA + 64:O_KA + 128]),
                                         (192, 64, w_in_v[:, :, O_KA:O_KA + 64]),
                                         (256, 256, w_in_v[:, :, O_QM:O_QM + 256])])
                c_m = W.add(8, 512, [(0, 256, w_in_v[:, :, O_KM:O_KM + 256]),
                                     (256, 128, w_in_v[:, :, O_VA:O_VA + 128]),
                                     (384, 8, w_in_v[:, :, O_I:O_I + 8])])
                c_vm = W.add(8, 512, [(0, 512, w_in_v[:, :, O_VM:O_VM + 512])])
                if not pre:
                    c_om = W.add(8, 512, [(0, 512, w_in_v[:, :, O_OM:O_OM + 512])])
                    c_D = []
                    for blk in range(NBLK):
                        cd = {}
                        cd["ga0"] = W.add(8, 512, [(0, 512, w_in_v[:, :, O_GA:O_GA + 512])])
                        cd["ga1"] = W.add(8, 512, [(0, 512, w_in_v[:, :, O_GA + 512:O_GA + 1024])])
                        cd["gm0"] = W.add(8, 512, [(0, 512, w_in_v[:, :, O_GM:O_GM + 512])])
                        cd["gm1"] = W.add(8, 512, [(0, 512, w_in_v[:, :, O_GM + 512:O_GM + 1024])])
                        cd["ao"] = W.add(4, 1024, [(0, 1024, w_ao_v)])
                        cd["mo"] = W.add(4, 1024, [(0, 1024, w_mo_v)])
                        cd["wo0"] = W.add(8, 512, [(0, 512, w_out_v[:, :, 0:512])])
                        cd["wo1"] = W.add(8, 512, [(0, 512, w_out_v[:, :, 512:1024])])
                        c_D.append(cd)
                WE = WStream(VRing(f"wE{sbi}_", 6, 2048))
                FG = [(g * 2, 2) for g in range(NFF // 2)]
                c_E = []
                for (f0, nf) in FG:
                    cg = WE.add(8, nf * 128, [(0, nf * 128, w_gate_v[:, :, f0 * 128:(f0 + nf) * 128])])
                    cu = WE.add(8, nf * 128, [(0, nf * 128, w_up_v[:, :, f0 * 128:(f0 + nf) * 128])])
                    cdn = WE.add(nf, 1024, [(0, 1024, w_down_v[:, f0:f0 + nf, :])])
                    c_E.append((cg, cu, cdn))

                def fm_group(wv, Tw, c0, M, rhs_list, evac):
                    for idx, (rhs, n, rds) in enumerate(rhs_list):
                        b = nextbank()
                        for k in range(8):
                            MM(ps[0:M, b, 0:n], wv[:, k, c0:c0 + M], rhs[:, k, :], k == 0, k == 7, [Tw] + rds, [PB[b]])
                        evac(idx, ps[0:M, b, 0:n], PB[b])

                def samp_proj(wv, Tw, ncols, pieces):
                    if pre or sbi != 0:
                        return
                    bb = nextbank()
                    for k in range(8):
                        MM(ps[0:16, bb, 0:ncols], hTs[:, k, :], wv[:, k, 0:ncols], k == 0, k == 7, [Tw, ThTs], [PB[bb]])
                    for (c0, n, z0) in pieces:
                        CP(zs[:, z0:z0 + n], ps[0:16, bb, c0:c0 + n], [PB[bb]], [Tzs])

                blk_rhs = [(hT_[:, :, blk * 512:(blk + 1) * 512], 512, ThT_[blk * 4:(blk + 1) * 4]) for blk in range(NBLK)]
                halo_rhs = [(hTh[:], 128, [ThTh])]

                if not pre:
                    wv, Tw = W.get(c_qa)
                    for p in range(4):
                        def ev(idx, pa, Tb, p=p):
                            ACT(qaT[:, p, idx * 512:(idx + 1) * 512], pa, AF.Copy, [Tb], [TqaT[idx]], scale=0.125)
                        fm_group(wv, Tw, p * 128, 128, blk_rhs, ev)
                    samp_proj(wv, Tw, 512, [(0, 512, 0)])
                    stage(f"Ba{sbi}")
                    wv, Tw = W.get(c_k)
                    for (c0, dst) in ((0, kaT_n), (128, kaT_s)):
                        def ev(idx, pa, Tb, dst=dst):
                            col = 128 + (gt0 * 128) + idx * 512
                            ACT(dst[:, col:col + 512], pa, AF.Copy, [Tb], TkaT[1 + gt0 + idx * 4: 1 + gt0 + idx * 4 + 4])
                        fm_group(wv, Tw, c0, 128, blk_rhs, ev)
                        if sbi == 0:
                            def evh(idx, pa, Tb, dst=dst):
                                ACT(dst[:, 0:128], pa, AF.Copy, [Tb], [TkaT[0]])
                            fm_group(wv, Tw, c0, 128, halo_rhs, evh)
                    for p in range(2):
                        def ev(idx, pa, Tb, p=p):
                            CP(qmT[:, p, idx * 512:(idx + 1) * 512], pa, [Tb], [TqmT[idx]])
                        fm_group(wv, Tw, 256 + p * 128, 128, blk_rhs, ev)
                    samp_proj(wv, Tw, 512, [(0, 128, 512), (256, 256, 768)])
                    if sbi == NSB - 1:
                        b = nextbank()
                        lt = SBT - 1
                        for k in range(8):
                            MM(ps[:, b, 0:128], hT_[:, k, lt * 128:(lt + 1) * 128], wv[:, k, 0:128], k == 0, k == 7, [Tw, ThT_[lt]], [PB[b]])
                        pko = sb("pko", [128, 128]); Tpko = Tile("pko")
                        CP(pko[:], ps[:, b, 0:128], [PB[b]], [Tpko])
                        DMA("sp", pk_d, pko[:], [Tpko], [Tile("pk_d")])
                stage(f"Bb{sbi}")
                wv, Tw = W.get(c_m)
                if not pre:
                    for p in range(2):
                        def ev(idx, pa, Tb, p=p):
                            ACT(kmT[:, p, idx * 512:(idx + 1) * 512], pa, AF.Copy, [Tb], [TkmT[idx]], scale=0.125)
                        fm_group(wv, Tw, p * 128, 128, blk_rhs, ev)
                r_ig, r_t, r_nf, r_u = rows

                def ev(idx, pa, Tb):
                    ACT(r_ig[:, idx * 512:(idx + 1) * 512], pa, AF.Identity, [Tb, Tbif], [Trow[0]], bias=b_i[:, 0:1])
                fm_group(wv, Tw, 384, 4, blk_rhs, ev)

                def ev(idx, pa, Tb):
                    ACT(r_t[:, idx * 512:(idx + 1) * 512], pa, AF.Identity, [Tb, Tbif], [Trow[1]], bias=b_f[:, 0:1])
                fm_group(wv, Tw, 388, 4, blk_rhs, ev)
                stage(f"Bc{sbi}")
                ncol = 256 if pre else 384
                for t in range(SBT):
                    b = nextbank()
                    for k in range(8):
                        MM(ps[:, b, 0:ncol], hT_[:, k, t * 128:(t + 1) * 128], wv[:, k, 0:ncol], k == 0, k == 7, [Tw, ThT_[t]], [PB[b]])
                    ACT(km_tok[:, t, :], ps[:, b, 0:256], AF.Copy, [PB[b]], [Tkmtok[t]], scale=0.125)
                    if not pre:
                        pvv = ps[:, b, 256:384].rearrange("p (g d) -> p g d", g=2)
                        gt = gt0 + t
                        CP(va[:, gt + 1, :, 0, :], pvv, [PB[b]], [Tva[gt + 1]])
                        ACT(va[:, gt + 1, :, 1, :], pvv, AF.Copy, [PB[b]], [Tva[gt + 1]])
                        if gt == NT - 1:
                            pvo = sb("pvo", [128, 128]); Tpvo = Tile("pvo")
                            CP(pvo[:], ps[:, b, 256:384], [PB[b]], [Tpvo])
                            DMA("sp", pv_d, pvo[:], [Tpvo], [Tile("pv_d")])
                if sbi == 0 and not pre:
                    b = nextbank()
                    for k in range(8):
                        MM(ps[:, b, 0:128], hTh[:, k, :], wv[:, k, 256:384], k == 0, k == 7, [Tw, ThTh], [PB[b]])
                    pvv = ps[:, b, 0:128].rearrange("p (g d) -> p g d", g=2)
                    CP(va[:, 0, :, 0, :], pvv, [PB[b]], [Tva[0]])
                    ACT(va[:, 0, :, 1, :], pvv, AF.Copy, [PB[b]], [Tva[0]])
                samp_proj(wv, Tw, 392, [(0, 256, 1024), (256, 128, 640), (384, 8, 2304)])
                stage(f"Bd{sbi}")
                if (not pre) or (par_ not in pre_ones_done):
                    if pre:
                        pre_ones_done.add(par_)
                    for t in range(SBT):
                        S.op("dve", lambda e, t=t: e.memset(vm1[:, t, :, 128:129], 1.0), [], [Tvm1[t]])
                wv, Tw = W.get(c_vm)
                for t in range(SBT):
                    b = nextbank()
                    for k in range(8):
                        MM(ps[:, b, 0:512], hT_[:, k, t * 128:(t + 1) * 128], wv[:, k, 0:512], k == 0, k == 7, [Tw, ThT_[t]], [PB[b]])
                    ACT(vm1[:, t, :, 0:128], ps[:, b, 0:512].rearrange("p (h v) -> p h v", h=4), AF.Copy, [PB[b]], [Tvm1[t]])
                samp_proj(wv, Tw, 512, [(0, 512, 1280)])
                stage(f"Be{sbi}")
                if not pre:
                    wv, Tw = W.get(c_om)
                    for t in range(SBT):
                        b = nextbank()
                        for k in range(8):
                            MM(ps[:, b, 0:512], hT_[:, k, t * 128:(t + 1) * 128], wv[:, k, 0:512], k == 0, k == 7, [Tw, ThT_[t]], [PB[b]])
                        tmp, Ttmp = sg_r[t % 2]
                        ACT(tmp[:], ps[:, b, 0:512], AF.Sigmoid, [PB[b]], [Ttmp])
                        TT(og[:, t, :], tmp[:], g_head[:], ALU.mult, [Ttmp, Tghead], [Tog[t]])
                    samp_proj(wv, Tw, 512, [(0, 512, 1792)])

                stage(f"B{sbi}")
                if pre:
                    yield "front"
                ACT(r_t[:], r_t[:], AF.Exp, [Trow[1]], [Trow[1]], scale=-1.0)
                ACT(r_t[:], r_t[:], AF.Ln, [Trow[1]], [Trow[1]], bias=1.0)
                S.op("dve", lambda e: e.tensor_tensor_scan(r_nf[:], one_c[0:4, 0:1].to_broadcast([4, TSB]), r_t[:], rstate[:, 0:1], ALU.mult, ALU.add),
                     [Tonesr, Trow[1], Trst], [Trow[2]])
                dbg(f"nl{sbi}", r_t[:], Trow[1], [4, TSB])
                dbg(f"NF{sbi}", r_nf[:], Trow[2], [4, TSB])
                dbg(f"ig{sbi}", r_ig[:], Trow[0], [4, TSB])
                TT(r_ig[:], r_ig[:], r_nf[:], ALU.add, [Trow[0], Trow[2]], [Trow[0]])
                S.op("dve", lambda e: e.tensor_tensor_scan(r_t[:], r_ig[:], r_ig[:], rstate[:, 1:2], ALU.max, ALU.max),
                     [Trow[0], Trst, Trow[1]], [Trow[1]])
                CP(gsm[:, 0:1], rstate[:, 1:2], [Trst], [Tgsm])
                rt_v = r_t[:].rearrange("p (t c) -> p t c", c=128)
                CP(gsm[:, 1:SBT + 1], rt_v[:, :, 127], [Trow[1]], [Tgsm])
                TS(r_u[:, 0:SBT], gsm[:, 1:SBT + 1], -1.0, None, ALU.mult, ALU.bypass, [Tgsm], [Trow[3]])
                CP(rstate[:, 0:1], r_nf[:, TSB - 1:TSB], [Trow[2], Tgsm], [Trst])
                CP(rstate[:, 1:2], r_t[:, TSB - 1:TSB], [Trow[1]], [Trst])
                for t in range(SBT):
                    tsl = slice(t * 128, (t + 1) * 128)
                    ACT(r_ig[:, tsl], r_ig[:, tsl], AF.Exp, [Trow[0], Trow[3]], [Trow[0]], bias=r_u[:, t:t + 1])
                    ACT(r_nf[:, tsl], r_nf[:, tsl], AF.Exp, [Trow[2], Trow[3]], [Trow[2]], bias=r_u[:, t:t + 1])
                gexp = carve(cw0, [4, SBT], F32)
                Tgexp = T_("gexp")
                TT(gexp[:], gsm[:, 0:SBT], gsm[:, 1:SBT + 1], ALU.subtract, [Tgsm], [Tgexp])
                ACT(gexp[:], gexp[:], AF.Exp, [Tgexp], [Tgexp])
                TT(Gm[:], gexp[:].unsqueeze(2).to_broadcast([4, SBT, 2]), pmm[:].unsqueeze(1).to_broadcast([4, SBT, 2]), ALU.mult, [Tgexp, Tsel], [TGm])
                b = nextbank()
                MM(ps[:, b, 0:SBT * 2], selm[:], Gm[:].rearrange("p t c -> p (t c)"), True, True, [Tsel, TGm], [PB[b]])
                CP(gbb[:].rearrange("p t c -> p (t c)"), ps[:, b, 0:SBT * 2], [PB[b]], [Tgbb])
                b = nextbank()
                for t in range(SBT):
                    TR(ps[:, b, t * 8:t * 8 + 4], r_ig[:, t * 128:(t + 1) * 128], ident_f[0:4, 0:4], [Trow[0], Tidf], [PB[b]])
                    TR(ps[:, b, t * 8 + 4:t * 8 + 8], r_nf[:, t * 128:(t + 1) * 128], ident_f[0:4, 0:4], [Trow[2], Tidf], [PB[b]])
                CP(ecol[:].rearrange("p t c -> p (t c)"), ps[:, b, 0:SBT * 8], [PB[b]], [Tecol])
                dbg(f"ecol{sbi}", ecol[:], Tecol, [128, SBT, 8])
                dbg(f"gbb{sbi}", gbb[:], Tgbb, [128, SBT, 2])
                dbg(f"rows{sbi}", r_t[:], Trow[1], [4, TSB])
                dbg(f"rowA{sbi}", r_ig[:], Trow[0], [4, TSB])

                stage(f"R{sbi}")
                if pre:
                    yield "rows"
                o = cw0 + 64
                if pre:
                    V2_pre = [(carve(o + i * 1040, [128, 4, 130], BF16), T_(f"V2{i}")) for i in range(2)]
                    STgf_pre = carve(o + 2080, [128, 2, 129], F32)
                    o = 0
                Sb_r = [(carve(o + i * 2048, [128, 512], F32), T_(f"Sb{i}")) for i in range(2)]; o += 4096
                ET_r = [(carve(o + i * 1024, [128, 512], BF16), T_(f"ET{i}")) for i in range(8)]; o += 8192
                rden_r = [(carve(o + i * 2048, [128, 512], F32), T_(f"rden{i}")) for i in range(2)]; o += 4096
                Wt_r = [(carve(o + i * 1024, [128, 4, 128], BF16), T_(f"Wt{i}")) for i in range(2)]; o += 2048
                V2_r = [(carve(o + i * 1040, [128, 4, 130], BF16), T_(f"V2{i}")) for i in range(2)]; o += 2080
                ym_r = [(carve(o + i * 1024, [128, 512], BF16), T_(f"ym{i}")) for i in range(2)]; o += 2048
                STg_f = carve(o, [128, 2, 129], F32); o += 2 * 129 * 4
                STg_b = carve(o, [128, 2, 130], BF16); o += 2 * 130 * 2
                o = (o + 3) // 4 * 4
                sj = carve(o, [128, 128], BF16); o += 256
                TSTgf = T_("STgf"); TSTgb = T_("STgb"); Tsj = T_("sj")
                assert o <= ARENA_BYTES, o
                if pre:
                    V2_r = V2_pre
                    STg_f = STgf_pre
                et_i = [0]
                for t in range(SBT):
                    gt = gt0 + t
                    tok = slice(t * 128, (t + 1) * 128)
                    sbk = (t % 2) * 2
                    if not pre:
                        ETs = {}
                        for g in range(2):
                            for c in range(2):
                                kt = gt + c
                                kcol = slice(kt * 128, (kt + 1) * 128)
                                for par in range(2):
                                    kk = kaT_n if (g == par) else kaT_s
                                    pr = slice(par * 64, (par + 1) * 64)
                                    outp = ps[:, sbk + par, c * 256:(c + 1) * 256].rearrange("p (j q) -> p j q", j=2)
                                    MM(outp, kk[pr, kcol], qaT[pr, 2 * g:2 * g + 2, tok], True, True, [TkaT[kt], TqaT[t // 4]], [PB[sbk + par]])
                            for par in range(2):
                                Sb, TSb = Sb_r[par]
                                TT(Sb[:], ps[:, sbk + par, :], BTall[:, g, par, :, :, :].rearrange("p c j q -> p (c j q)"), ALU.add, [PB[sbk + par], TBT], [TSb])
                                ET, TET = ET_r[et_i[0] % 8]; et_i[0] += 1
                                if gt == 0:
                                    ACT(ET[:, 0:256], Sb[:, 0:256], AF.Exp, [TSb, Tflag], [TET], bias=flag[:, 0:1])
                                    ACT(ET[:, 256:512], Sb[:, 256:512], AF.Exp, [TSb], [TET])
                                else:
                                    ACT(ET[:], Sb[:], AF.Exp, [TSb], [TET])
                                ETs[(g, par)] = (ET, TET)
                        for g in range(2):
                            bY, bD = 4 + g, 6 + g
                            for par in range(2):
                                ET, TET = ETs[(g, par)]
                                yv = ps[:, bY, :].rearrange("p (j q) -> p j q", j=4)[:, par::2, :]
                                dv = ps[:, bD, :].rearrange("p (j q) -> p j q", j=4)[:, par::2, :]
                                for c in range(2):
                                    kt = gt + c
                                    MM(yv, va[:, kt, g, :, :].rearrange("p a d -> p (a d)"), ET[:, c * 256:(c + 1) * 256].rearrange("p (j q) -> p j q", j=2),
                                       c == 0, c == 1, [Tva[kt], TET], [PB[bY]])
                                for c in range(2):
                                    MM(dv, ones_bf[:], ET[:, c * 256:(c + 1) * 256].rearrange("p (j q) -> p j q", j=2), c == 0, False, [Tones, TET], [PB[bD]])
                                MM(dv, ones_bf[0:1, :], esink[0:1, 4 * g + par:4 * g + 4:2, :], False, True, [Tones, Tesink], [PB[bD]])
                            rden, Trden = rden_r[g]
                            ACT(rden[:], ps[:, bD, :], AF.Ln, [PB[bD]], [Trden])
                            ACT(rden[:], rden[:], AF.Exp, [Trden], [Trden], scale=-1.0)
                            for par in range(2):
                                pr = slice(par * 64, (par + 1) * 64)
                                yv = ps[pr, bY, :].rearrange("p (j q) -> p j q", j=4)[:, par::2, :]
                                rv = rden[pr, :].rearrange("p (j q) -> p j q", j=4)[:, par::2, :]
                                TT(yattT[pr, 2 * g:2 * g + 2, tok], yv, rv, ALU.mult, [PB[bY], Trden], [TyaT[t]])
                for t in range(SBT):
                    gt = gt0 + t
                    tok = slice(t * 128, (t + 1) * 128)
                    bMS, bHO, bSU, bTR = (0, 2, 0, 1) if t % 2 == 0 else (4, 6, 4, 5)
                    if not pre:
                        for h in range(4):
                            p, par = h // 2, h % 2
                            pr = slice(par * 64, (par + 1) * 64)
                            MM(ps[:, bMS + par, p * 128:(p + 1) * 128], kmT[pr, p, tok], qmT[pr, p, tok], True, True, [TkmT[t // 4], TqmT[t // 4]], [PB[bMS + par]])
                        Wt, TWt = Wt_r[t % 2]
                        TT(Wt[:].rearrange("s (p r) q -> s r p q", r=2), ps[:, bMS:bMS + 2, 0:256].rearrange("s b (p q) -> s b p q", p=2),
                           caus[:].unsqueeze(1).unsqueeze(1).to_broadcast([128, 2, 2, 128]), ALU.mult, [PB[bMS], PB[bMS + 1], Tcaus], [TWt])
                    V2, TV2 = V2_r[t % 2]
                    TT(V2[:, :, 0:129], vm1[:, t, :, 0:129], ecol[:, t, 0:4].unsqueeze(2).to_broadcast([128, 4, 129]), ALU.mult, [Tvm1[t], Tecol], [TV2])
                    if not pre:
                        for p in range(2):
                            ACT(STg_b[:, p, 0:129], ST[:, p, :], AF.Copy, [TST, Tgbb], [TSTgb], scale=gbb[:, t, p:p + 1])
                        for h in range(4):
                            p, par = h // 2, h % 2
                            pr = slice(par * 64, (par + 1) * 64)
                            bb = bHO + h // 2
                            cc = (h % 2) * 256
                            MM(ps[:, bb, cc:cc + 129], Wt[:, h, :], V2[:, h, 0:129], True, False, [TWt, TV2], [PB[bb]])
                            MM(ps[:, bb, cc:cc + 129], qmT[pr, p, tok], STg_b[pr, p, 0:129], False, True, [TqmT[t // 4], TSTgb], [PB[bb]])
                    for p in range(2):
                        for par in range(2):
                            h = 2 * p + par
                            pr = slice(par * 64, (par + 1) * 64)
                            c0 = p * 128 + par * 64
                            MM(ps[pr, bSU + p, 0:129], km_tok[:, t, c0:c0 + 64], V2[:, h, 0:129], True, True, [Tkmtok[t], TV2], [PB[bSU + p]])
                    for p in range(2):
                        STT(ST[:, p, :], ST[:, p, :], gbb[:, t, p:p + 1], ps[:, bSU + p, 0:129], ALU.mult, ALU.add, [TST, Tgbb, PB[bSU + p]], [TST])
                    if pre:
                        yield "C"
                    if not pre:
                        HOv = ps[:, bHO:bHO + 2, :].rearrange("p b (c x) -> p (b c) x", c=2)
                        d4, Td4 = d4ring.next()
                        CP(d4[:, 0:4], HOv[:, :, 128], [PB[bHO], PB[bHO + 1]], [Td4])
                        STT(d4[:, 0:4], d4[:, 0:4], -1.0, d4[:, 0:4], ALU.mult, ALU.max, [Td4], [Td4])
                        TT(d4[:, 0:4], d4[:, 0:4], ecol[:, t, 4:8], ALU.max, [Td4, Tecol], [Td4])
                        S.op("dve", lambda e, d4=d4: e.reciprocal(d4[:, 0:4], d4[:, 0:4]), [Td4], [Td4])
                        for h in range(4):
                            ACT(sj[:], HOv[:, h, 0:128], AF.Square, [PB[bHO], PB[bHO + 1], Td4], [Tsj, Td4], scale=d4[:, h:h + 1], accum_out=d4[:, 4 + h:5 + h])
                        ACT(d4[:, 8:12], d4[:, 4:8], AF.Ln, [Td4, Teps], [Td4], scale=1.0 / 128, bias=eps_c[:, 0:1])
                        ACT(d4[:, 8:12], d4[:, 8:12], AF.Exp, [Td4], [Td4], scale=-0.5)
                        TT(d4[:, 8:12], d4[:, 8:12], d4[:, 0:4], ALU.mult, [Td4], [Td4])
                        ym, Tym = ym_r[t % 2]
                        for h in range(4):
                            STT(ym[:, h * 128:(h + 1) * 128], HOv[:, h, 0:128], d4[:, 8 + h:9 + h], og[:, t, h * 128:(h + 1) * 128], ALU.mult, ALU.mult,
                                [PB[bHO], PB[bHO + 1], Td4, Tog[t]], [Tym])
                        pvb = ps_bf(bTR)
                        for h in range(4):
                            TR(pvb[:, h * 128:(h + 1) * 128], ym[:, h * 128:(h + 1) * 128], ident_bf[:], [Tym, Tidb], [PB[bTR]])
                        ACT(ymT[:, :, tok], pvb[:, 0:512].rearrange("p (h q) -> p h q", h=4), AF.Copy, [PB[bTR]], [TymT[t]])
                dbg(f"yattT{sbi}", yattT[:], TyaT[SBT - 1], [128, 4, TSB], BF16)
                dbg(f"ymT{sbi}", ymT[:], TymT[SBT - 1], [128, 4, TSB], BF16)

                if pre:
                    return
                if sbi == 0:
                    sample_phase()
                stage(f"C{sbi}")
                S.barrier()
                samp = (sbi == 0)
                o = 0
                x1 = carve(o, [128, SBT, D], F32); o += SBT * D * 4
                Tx1 = [Tile(f"x1_{t}") for t in range(SBT)]
                x1s = carve(o, [16, D], F32); o += D * 4
                Tx1s = Tile("x1s")
                sga = carve(o, [128, 8, 512], BF16); o += 8192
                sgm = carve(o, [128, 8, 512], BF16); o += 8192
                mixT = carve(o, [128, 8, 512], BF16); o += 8192
                tmpD = [(carve(o + i * 2048, [128, 512], F32), Tile("tmpD")) for i in range(2)]; o += 4096
                sga_s = carve(o, [128, 8, 16], BF16); o += 256
                sgm_s = carve(o, [128, 8, 16], BF16); o += 256
                mix_s = carve(o, [128, 8, 16], BF16); o += 256
                aT_s = carve(o, [128, 2, 16], BF16); o += 64
                e_off = o
                assert o <= ARENA_BYTES
                Tsga = [Tile("sga") for _ in range(8)]
                Tsgm = [Tile("sgm") for _ in range(8)]
                Tmix = [Tile("mix") for _ in range(8)]
                Tsga_s = [Tile("sga_s") for _ in range(8)]
                Tsgm_s = [Tile("sgm_s") for _ in range(8)]
                Tmix_s = [Tile("mix_s") for _ in range(8)]
                TaT_s = Tile("aT_s")

                def subblocks(blk):
                    btok = slice(blk * 512, (blk + 1) * 512)
                    subs = [dict(n=512, hv=hT_[:, :, btok], hrd=ThT_[blk * 4:(blk + 1) * 4],
                                 yav=yattT[:, :, btok], yard=TyaT[blk * 4:(blk + 1) * 4],
                                 ymv=ymT[:, :, btok], ymrd=TymT[blk * 4:(blk + 1) * 4],
                                 sga=sga, sgm=sgm, mix=mixT, Tsga=Tsga, Tsgm=Tsgm, Tmix=Tmix,
                                 tiles=[(x1[:, blk * 4 + tt, :], Tx1[blk * 4 + tt], 128, slice(tt * 128, (tt + 1) * 128)) for tt in range(4)])]
                    if samp and blk == 0:
                        subs.append(dict(n=16, hv=hTs[:], hrd=[ThTs], yav=yattTs[:], yard=[TyaTs], ymv=ymTs[:], ymrd=[TymTs],
                                         sga=sga_s, sgm=sgm_s, mix=mix_s, Tsga=Tsga_s, Tsgm=Tsgm_s, Tmix=Tmix_s,
                                         tiles=[(x1s[:], Tx1s, 16, slice(0, 16))]))
                    return subs

                if samp:
                    DMA("sp", x1s[:], xsm_d, [], [Tx1s])
                for blk in range(NBLK):
                    cd = c_D[blk]
                    subs = subblocks(blk)
                    for tt in range(4):
                        t = blk * 4 + tt
                        r0 = 128 + (gt0 + t) * 128
                        DMA("sp", x1[:, t, :], xs_d[r0:r0 + 128, :], [], [Tx1[t]])
                    for nm in ("ga", "gm"):
                        for half in range(2):
                            wv, Tw = W.get(cd[f"{nm}{half}"])
                            for jj in range(4):
                                j = half * 4 + jj
                                for sub in subs:
                                    n = sub["n"]
                                    dst, Td = (sub["sga"], sub["Tsga"]) if nm == "ga" else (sub["sgm"], sub["Tsgm"])
                                    bb = nextbank()
                                    for k in range(8):
                                        MM(ps[:, bb, 0:n], wv[:, k, jj * 128:(jj + 1) * 128], sub["hv"][:, k, :], k == 0, k == 7, [Tw] + sub["hrd"], [PB[bb]])
                                    ACT(dst[:, j, 0:n], ps[:, bb, 0:n], AF.Sigmoid, [PB[bb]], [Td[j]])
                    wva, Twa = W.get(cd["ao"])
                    wvm, Twm = W.get(cd["mo"], live_from=cd["ao"])
                    for j in range(8):
                        for sub in subs:
                            n = sub["n"]
                            bb = nextbank()
                            for k in range(4):
                                MM(ps[:, bb, 0:n], wva[:, k, j * 128:(j + 1) * 128], sub["yav"][:, k, :], k == 0, k == 3, [Twa] + sub["yard"], [PB[bb]])
                            tmp, Ttmp = tmpD[j % 2]
                            TT(tmp[:, 0:n], ps[:, bb, 0:n], sub["sga"][:, j, 0:n], ALU.mult, [PB[bb], sub["Tsga"][j]], [Ttmp])
                            b2 = nextbank()
                            for k in range(4):
                                MM(ps[:, b2, 0:n], wvm[:, k, j * 128:(j + 1) * 128], sub["ymv"][:, k, :], k == 0, k == 3, [Twm] + sub["ymrd"], [PB[b2]])
                            TT(sub["mix"][:, j, 0:n], ps[:, b2, 0:n], sub["sgm"][:, j, 0:n], ALU.mult, [PB[b2], sub["Tsgm"][j]], [sub["Tmix"][j]])
                            TT(sub["mix"][:, j, 0:n], sub["mix"][:, j, 0:n], tmp[:, 0:n], ALU.add, [sub["Tmix"][j], Ttmp], [sub["Tmix"][j]])
                    for half in range(2):
                        wv, Tw = W.get(cd[f"wo{half}"])
                        for sub in subs:
                            for (xa, Txa, tn, tsl) in sub["tiles"]:
                                bb = nextbank()
                                for j in range(8):
                                    MM(ps[0:tn, bb, :], sub["mix"][:, j, tsl], wv[:, j, :], j == 0, j == 7, [Tw, sub["Tmix"][j]], [PB[bb]])
                                TT(xa[:, half * 512:(half + 1) * 512], xa[:, half * 512:(half + 1) * 512], ps[0:tn, bb, :], ALU.add, [PB[bb], Txa], [Txa])
                dbg(f"x1_{sbi}", x1[:], Tx1[SBT - 1], [128, SBT, D])
                dbg("x1s", x1s[:], Tx1s, [16, D])

                stage(f"D{sbi}")
                o = e_off
                actT = [(carve(o + i * 2048, [128, 2, 512], BF16), Tile("actT")) for i in range(2)]; o += 4096
                sil = [(carve(o + i * 2048, [128, 512], F32), Tile("sil")) for i in range(2)]; o += 4096
                yo = [(carve(o + i * 4096, [128, D], F32), Tile("yo")) for i in range(2)]; o += 8192
                Tx1b = [Tile(f"x1b_{t}") for t in range(SBT)]
                assert o <= ARENA_BYTES, o
                for t in range(SBT):
                    norm_T(x1[:, t, :], [Tx1[t]], g_ffn, Tgffn, hT_[:, :, t * 128:(t + 1) * 128], ThT_[t], src_is_dram=False)
                if samp:
                    norm_T(x1s[:], [Tx1s], g_ffn, Tgffn, hTs[:], ThTs, tn=16, src_is_dram=False)
                for gi, (f0, nf) in enumerate(FG):
                    cg, cu, cdn = c_E[gi]
                    wg, Twg = WE.get(cg)
                    wu, Twu = WE.get(cu, live_from=cg)
                    wd, Twd = WE.get(cdn, live_from=cg)
                    for blk in range(NBLK):
                        for si, sub in enumerate(subblocks(blk)):
                            n = sub["n"]
                            if si == 0:
                                aT, TaT = actT[(gi * NBLK + blk) % 2]
                            else:
                                aT, TaT = aT_s, TaT_s
                            for c in range(nf):
                                bb = nextbank()
                                for k in range(8):
                                    MM(ps[:, bb, 0:n], wg[:, k, c * 128:(c + 1) * 128], sub["hv"][:, k, :], k == 0, k == 7, [Twg] + sub["hrd"], [PB[bb]])
                                b2 = nextbank()
                                for k in range(8):
                                    MM(ps[:, b2, 0:n], wu[:, k, c * 128:(c + 1) * 128], sub["hv"][:, k, :], k == 0, k == 7, [Twu] + sub["hrd"], [PB[b2]])
                                sl, Tsl = sil[c % 2]
                                ACT(sl[:, 0:n], ps[:, bb, 0:n], AF.Silu, [PB[bb]], [Tsl])
                                TT(aT[:, c, 0:n], sl[:, 0:n], ps[:, b2, 0:n], ALU.mult, [Tsl, PB[b2]], [TaT])
                            for ti_, (xa, Txa, tn, tsl) in enumerate(sub["tiles"]):
                                for half in range(2):
                                    bb = nextbank()
                                    for c in range(nf):
                                        MM(ps[0:tn, bb, :], aT[:, c, tsl], wd[:, c, half * 512:(half + 1) * 512], c == 0, c == nf - 1, [TaT, Twd], [PB[bb]])
                                    if True:
                                        TT(xa[:, half * 512:(half + 1) * 512], xa[:, half * 512:(half + 1) * 512], ps[0:tn, bb, :], ALU.add, [PB[bb], Txa], [Txa])
                stage(f"E{sbi}")
                fin = [(x1[:, t, :], [Tx1[t], Tx1b[t]], 128, y_d[(gt0 + t) * 128:(gt0 + t + 1) * 128, :]) for t in range(SBT)]
                if samp:
                    fin.append((x1s[:], [Tx1s], 16, ys_d))
                for fi, (xa, Txa, tn, dst) in enumerate(fin):
                    yb, Tyb = yo[fi % 2]
                    ss, Tss = newscal()
                    ACT(yb[0:tn, :], xa, AF.Square, Txa, [Tyb, Tss], accum_out=ss[0:tn, :])
                    rr, Trr = newscal()
                    ACT(rr[0:tn, :], ss[0:tn, :], AF.Ln, [Tss, Teps], [Trr], scale=1.0 / D, bias=eps_c[0:tn, 0:1])
                    ACT(rr[0:tn, :], rr[0:tn, :], AF.Exp, [Trr], [Trr], scale=-0.5)
                    STT(yb[0:tn, :], xa, rr[0:tn, 0:1], g_fin[0:tn, :], ALU.mult, ALU.mult, Txa + [Trr, Tgfin], [Tyb])
                    DMA("sp", dst, yb[0:tn, :], [Tyb], [Tile("y_d")])
                S.barrier()

            pact = sb("pact", [128, 4]); Tpact = Tile("pact")
            pre_ring[0] = VRing("wP_", 3, 4096)
            DMA("sp", pact[:], pact_d, [], [Tpact])
            in_pre[0] = True

            def boundary(j):
                TT(rstate[:, 1:2], rstate[:, 1:2], rstate[:, 0:1], ALU.subtract, [Trst], [Trst])
                TS(rstate[:, 1:2], rstate[:, 1:2], pact[0:4, j:j + 1], None, ALU.mult, ALU.bypass, [Trst, Tpact], [Trst])
                S.op("dve", lambda e: e.memset(rstate[:, 0:1], 0.0), [], [Trst])
                TS(ST[:], ST[:], pact[:, j:j + 1], None, ALU.mult, ALU.bypass, [TST, Tpact], [TST])

            gens = []
            for j in range(3):
                for sbi in range(NSB):
                    k = j * NSB + sbi
                    gens.append(superblock(sbi, True, ST, TST, rstate, Trst,
                                           lambda gt, j=j: xprev_d[(j * NT + gt) * 128:(j * NT + gt + 1) * 128, :], par_=k % 2))
            def run_until(g, tag):
                for x in g:
                    if x == tag:
                        return True
                return False

            run_until(gens[0], "front")
            for k in range(len(gens)):
                gk = gens[k]
                gn = gens[k + 1] if k + 1 < len(gens) else None
                run_until(gk, "rows")
                for t in range(SBT):
                    if gn is not None:
                        run_until(gn, "A")
                    run_until(gk, "C")
                for _ in gk:
                    pass
                if gn is not None:
                    run_until(gn, "front")
                if k % NSB == NSB - 1:
                    boundary(k // NSB)
            in_pre[0] = False
            S.barrier()
            dbg("STpre", ST[:], TST, [128, 2, 129])
            dbg("rst", rstate[:], Trst, [4, 4])
            stage("pre")
            for sbi in range(NSB):
                for _ in superblock(sbi, False, ST, TST, rstate, Trst, lambda gt: xs_d[128 + gt * 128:128 + (gt + 1) * 128, :]):
                    pass

            Cout = sb("Cout", [128, 4, 64]); TCout = Tile("Cout")
            for h in range(4):
                p, par = h // 2, h % 2
                pr = slice(par * 64, (par + 1) * 64)
                TR(ps[:, par, p * 64:(p + 1) * 64], ST[pr, p, 0:128], ident_f[pr, pr], [TST, Tidf], [PB[par]])
            for par in range(2):
                CP(Cout[:, par::2, :], ps[:, par, 0:128].rearrange("v (p d) -> v p d", p=2), [PB[par]], [TCout])
            DMA("sp", pC_d.rearrange("h v d -> v h d"), Cout[:], [TCout], [Tile("pC_d")])
            for h in range(4):
                p, par = h // 2, h % 2
                pr = slice(par * 64, (par + 1) * 64)
                DMA("sp", AP(pn_d.tensor, h * 64, [[1, 64], [1, 1]]), ST[pr, p, 128:129], [TST], [Tile("pn_d")])
            mo = sb("mo", [4, 1]); Tmo = Tile("mo")
            TT(mo[:], rstate[:, 1:2], rstate[:, 0:1], ALU.subtract, [Trst], [Tmo])
            DMA("sp", pm_out_d, mo[:], [Tmo], [Tile("pm_d")])

        except _Stop:
            pass
        S.finish()
        S.emit()
    return nc, dbg_outs


_CACHE = {}


def _consts():
    ident = np.eye(128, dtype=np.float32)
    dist_rev = 127 - np.arange(128)
    bk = t5_bucket_np(dist_rev)
    ohT_rev = (np.arange(32)[:, None] == bk[None, :]).astype(np.float32)
    causT = (np.arange(128)[:, None] <= np.arange(128)[None, :]).astype(np.float32)
    sel = np.zeros((4, 128), np.float32)
    for h in range(4):
        sel[h, (h % 2) * 64:(h % 2) * 64 + 64] = 1.0
    pm = np.zeros((4, 2), np.float32)
    for h in range(4):
        pm[h, h // 2] = 1.0
    return dict(ident_bf=ident.astype(ml_dtypes.bfloat16), ident_f=ident, ohT_rev=ohT_rev, causT=causT, sel=sel, pm=pm)


def kernel(x_prompt, x_sample, cache_k_win, cache_v_win, state_mlstm_C, state_mlstm_n, state_mlstm_m,
           rel_bias, w_in, b_if, sinks, g_attn_norm, g_head, w_att_out, w_mlstm_out, w_out,
           g_ffn_norm, w_gate, w_up, w_down, g_final, _debug=(), _stop=None, _trace=False):
    f32 = np.float32
    x_prompt = np.asarray(x_prompt, f32)
    x_sample = np.asarray(x_sample, f32)
    cache_k_win = np.asarray(cache_k_win, f32)
    cache_v_win = np.asarray(cache_v_win, f32)
    state_mlstm_C = np.asarray(state_mlstm_C, f32)
    state_mlstm_n = np.asarray(state_mlstm_n, f32)
    state_mlstm_m = np.asarray(state_mlstm_m, f32)
    key = (tuple(_debug), _stop)
    if key not in _CACHE:
        _CACHE[key] = build_program(debug=_debug, stop=_stop)
    nc, dbg_outs = _CACHE[key]
    cst = _consts()
    shared = dict(
        w_in=np.ascontiguousarray(np.asarray(w_in, f32)[0]),
        b_if=np.ascontiguousarray(np.asarray(b_if, f32)[0]),
        sinks=np.ascontiguousarray(np.asarray(sinks, f32)),
        rel_bias=np.ascontiguousarray(np.asarray(rel_bias, f32)),
        g_attn=np.ascontiguousarray(np.asarray(g_attn_norm, f32)),
        g_head=np.ascontiguousarray(np.asarray(g_head, f32)),
        g_ffn=np.ascontiguousarray(np.asarray(g_ffn_norm, f32)),
        g_final=np.ascontiguousarray(np.asarray(g_final, f32)[None]),
        w_att_out=np.ascontiguousarray(np.asarray(w_att_out, f32)[0]),
        w_mlstm_out=np.ascontiguousarray(np.asarray(w_mlstm_out, f32)[0]),
        w_out=np.ascontiguousarray(np.asarray(w_out, f32)[0]),
        w_gate=np.ascontiguousarray(np.asarray(w_gate, f32)[0]),
        w_up=np.ascontiguousarray(np.asarray(w_up, f32)[0]),
        w_down=np.ascontiguousarray(np.asarray(w_down, f32)[0]),
        **cst,
    )
    in_maps = []
    for c in range(NCORE):
        b, s = c // 4, c % 4
        xs = np.zeros((128 + SEG, D), f32)
        xs[128:] = x_prompt[b, s * SEG:(s + 1) * SEG]
        if s > 0:
            xs[:128] = x_prompt[b, s * SEG - 128:s * SEG]
        flag = np.full((128, 1), NEGB if s == 0 else 0.0, f32)
        m = dict(shared)
        m["xs"] = xs
        m["flag"] = flag
        xprev = np.zeros((3 * SEG, D), f32)
        pact = np.zeros((128, 4), f32)
        for j in range(3):
            sj = s - 3 + j
            if sj >= 0:
                xprev[j * SEG:(j + 1) * SEG] = x_prompt[b, sj * SEG:(sj + 1) * SEG]
                pact[:, j] = 1.0
        m["xprev"] = xprev
        m["pact"] = pact
        sl = slice(c * 16, (c + 1) * 16)
        m["xsm"] = np.ascontiguousarray(x_sample[sl, 0, :])
        m["ck"] = np.ascontiguousarray(cache_k_win[0, sl].reshape(16, 128, 128))
        m["cv"] = np.ascontiguousarray(cache_v_win[0, sl].reshape(16, 128, 128))
        m["sC"] = np.ascontiguousarray(state_mlstm_C[0, sl])
        m["sn"] = np.ascontiguousarray(state_mlstm_n[0, sl])
        m["sm"] = np.ascontiguousarray(state_mlstm_m[0, sl])
        in_maps.append(m)
    res = run_bass_kernel_spmd(nc, in_maps, core_ids=list(range(NCORE)), **({'trace': True} if _trace else {}))
    if _trace:
        print('EXEC_TIME_NS', res.exec_time_ns)
    R = res.results
    y_prompt = np.stack([np.concatenate([R[b * 4 + s]["y"] for s in range(4)], axis=0) for b in range(2)])
    p_k = np.stack([R[b * 4 + 3]["pk"].reshape(128, 2, 64) for b in range(2)])[None]
    p_v = np.stack([R[b * 4 + 3]["pv"].reshape(128, 2, 64) for b in range(2)])[None]
    p_C = np.stack([R[b * 4 + 3]["pC"] for b in range(2)])[None]
    p_n = np.stack([R[b * 4 + 3]["pn"] for b in range(2)])[None]
    p_m = np.stack([R[b * 4 + 3]["pm_out"].reshape(4) for b in range(2)])[None]
    y_sample = np.concatenate([R[c]["ys"] for c in range(NCORE)], axis=0)[:, None, :]
    s_k = np.concatenate([R[c]["sko"].reshape(16, 128, 2, 64) for c in range(NCORE)], axis=0)[None]
    s_v = np.concatenate([R[c]["svo"].reshape(16, 128, 2, 64) for c in range(NCORE)], axis=0)[None]
    s_C = np.concatenate([R[c]["sCo"] for c in range(NCORE)], axis=0)[None]
    s_n = np.concatenate([R[c]["sno"][:, :, 0, :] for c in range(NCORE)], axis=0)[None]
    s_m = np.concatenate([R[c]["smo"][:, :, 0] for c in range(NCORE)], axis=0)[None]
    outs = (y_prompt, y_sample, p_k, p_v, p_C, p_n, p_m, s_k, s_v, s_C, s_n, s_m)
    if _debug:
        return outs, [{k: r["dbg_" + k] for k in dbg_outs} for r in R]
    return outs
```
